# Optimizing a Trainium2 kernel written in Bass

```python
import jax, jax.numpy as jnp
from jax import lax
import numpy as np

D_MODEL = 1024
BATCH = 8
SEQ = 8192
DEPTH = 1

N_MEM = 256
D_FF = 2816
CONV_CH = 512
CONV_K = 31
M_HEADS = 4
M_HEAD_DIM = 128
M_WIDTH = M_HEADS * M_HEAD_DIM
QK_CONV_K = 4
CHUNK = 64
MIX_WIDTH = CONV_CH + M_WIDTH
IN_COLS = 2 * CONV_CH + 4 * M_WIDTH + 2 * M_HEADS
X_HEADS = 4
X_HEAD_DIM = D_MODEL // X_HEADS
EPS = 1e-6

kernel_name = 'hybrid_conv_mlstm_parallel_heads_macaron_sandwich_memxattn'


def rmsnorm(x, g):
    xf = x.astype(jnp.float32)
    y = xf * lax.rsqrt(jnp.mean(xf * xf, axis=-1, keepdims=True) + EPS)
    return (y * g.astype(jnp.float32)).astype(x.dtype)


def layernorm(x, g, b):
    xf = x.astype(jnp.float32)
    mu = jnp.mean(xf, axis=-1, keepdims=True)
    var = jnp.mean(jnp.square(xf - mu), axis=-1, keepdims=True)
    y = (xf - mu) * lax.rsqrt(var + 1e-5)
    return (y * g.astype(jnp.float32) + b.astype(jnp.float32)).astype(x.dtype)


def swiglu(x, w_gate, w_up, w_down):
    return (jax.nn.silu(x @ w_gate) * (x @ w_up)) @ w_down


def causal_dwconv(x, w, b):
    k = w.shape[0]
    y = lax.conv_general_dilated(x, w[:, None, :], window_strides=(1,), padding=[(k - 1, 0)],
                                 dimension_numbers=('NWC', 'WIO', 'NWC'),
                                 feature_group_count=x.shape[-1])
    return y + b


def mlstm_chunkwise(q, k, v, log_i, log_f):
    bsz, seq, nh, dk = q.shape
    dv = v.shape[-1]
    nc = seq // CHUNK

    def to_chunks(a):
        return jnp.transpose(a.reshape(bsz, nc, CHUNK, nh, -1), (1, 0, 3, 2, 4))

    qc, kc, vc = to_chunks(q), to_chunks(k), to_chunks(v)
    lic = to_chunks(log_i[..., None])[..., 0]
    lfc = to_chunks(log_f[..., None])[..., 0]
    causal = jnp.tril(jnp.ones((CHUNK, CHUNK), dtype=bool))

    def step(carry, xs):
        c_state, n_state, m_state = carry
        q_, k_, v_, li, lf = xs
        b = jnp.cumsum(lf, axis=-1)
        dmat = b[..., :, None] - b[..., None, :] + li[..., None, :]
        dmat = jnp.where(causal, dmat, -jnp.inf)
        inter = b + m_state[..., None]
        m_t = jnp.maximum(inter, jnp.max(dmat, axis=-1))
        w = jnp.exp(dmat - m_t[..., None]) * jnp.einsum('bhtd,bhsd->bhts', q_, k_)
        a = jnp.exp(inter - m_t)
        num = jnp.einsum('bhts,bhse->bhte', w, v_) + a[..., None] * jnp.einsum('bhtd,bhde->bhte', q_, c_state)
        den = jnp.sum(w, axis=-1) + a * jnp.einsum('bhtd,bhd->bht', q_, n_state)
        h = num / jnp.maximum(jnp.abs(den), jnp.exp(-m_t))[..., None]
        b_end = b[..., -1]
        g = b_end[..., None] - b + li
        m_new = jnp.maximum(b_end + m_state, jnp.max(g, axis=-1))
        decay = jnp.exp(b_end + m_state - m_new)
        wg = jnp.exp(g - m_new[..., None])
        c_state = decay[..., None, None] * c_state + jnp.einsum('bhs,bhsd,bhse->bhde', wg, k_, v_)
        n_state = decay[..., None] * n_state + jnp.einsum('bhs,bhsd->bhd', wg, k_)
        return (c_state, n_state, m_new), h

    init = (jnp.zeros((bsz, nh, dk, dv), jnp.float32),
            jnp.zeros((bsz, nh, dk), jnp.float32),
            jnp.zeros((bsz, nh), jnp.float32))
    _, hs = lax.scan(step, init, (qc, kc, vc, lic, lfc))
    return jnp.transpose(hs, (1, 0, 3, 2, 4)).reshape(bsz, seq, nh, dv)


def parallel_mixer(hn, w_in, conv_w, conv_b, conv_ln_g, conv_ln_b, qk_conv_w, qk_conv_b,
                   b_igate, b_fgate, mlstm_norm_g, w_out):
    bsz, seq, _ = hn.shape
    z = hn @ w_in
    offs = np.cumsum([CONV_CH, CONV_CH, M_WIDTH, M_WIDTH, M_WIDTH, M_WIDTH, M_HEADS]).tolist()
    c_val, c_gate, q, k, v, o, ig, fg = jnp.split(z, offs, axis=-1)

    u = c_val * jax.nn.sigmoid(c_gate)
    u = causal_dwconv(u, conv_w, conv_b)
    u = jax.nn.silu(layernorm(u, conv_ln_g, conv_ln_b))

    qk = jax.nn.silu(causal_dwconv(jnp.concatenate([q, k], axis=-1), qk_conv_w, qk_conv_b))
    q, k = jnp.split(qk, 2, axis=-1)
    f32 = jnp.float32
    q = q.reshape(bsz, seq, M_HEADS, M_HEAD_DIM).astype(f32)
    k = k.reshape(bsz, seq, M_HEADS, M_HEAD_DIM).astype(f32) * (M_HEAD_DIM ** -0.5)
    v = v.reshape(bsz, seq, M_HEADS, M_HEAD_DIM).astype(f32)
    log_i = (ig + b_igate).astype(f32)
    log_f = jax.nn.log_sigmoid((fg + b_fgate).astype(f32))
    h = mlstm_chunkwise(q, k, v, log_i, log_f)
    h = h * lax.rsqrt(jnp.mean(h * h, axis=-1, keepdims=True) + EPS)
    h = h.reshape(bsz, seq, M_WIDTH) * mlstm_norm_g.astype(f32)
    h = jax.nn.sigmoid(o) * h.astype(o.dtype)

    return jnp.concatenate([u, h], axis=-1) @ w_out


def memory_cross_attention(hn, memn, wq, wk, wv, wo):
    bsz, seq, _ = hn.shape
    n_mem = memn.shape[1]
    q = (hn @ wq).reshape(bsz, seq, X_HEADS, X_HEAD_DIM)
    k = (memn @ wk).reshape(bsz, n_mem, X_HEADS, X_HEAD_DIM)
    v = (memn @ wv).reshape(bsz, n_mem, X_HEADS, X_HEAD_DIM)
    s = jnp.einsum('bshd,bmhd->bhsm', q, k).astype(jnp.float32) * (X_HEAD_DIM ** -0.5)
    p = jax.nn.softmax(s, axis=-1).astype(v.dtype)
    out = jnp.einsum('bhsm,bmhd->bshd', p, v).reshape(bsz, seq, X_HEADS * X_HEAD_DIM)
    return out @ wo


def setup_inputs(seed: int = 0) -> dict:
    key = jax.random.key(seed)
    ks = iter(jax.random.split(key, 48))

    def nrm(shape, scale):
        return jax.random.normal(next(ks), shape, jnp.float32) * scale

    def gain(shape):
        return 1.0 + 0.05 * jax.random.normal(next(ks), shape, jnp.float32)

    L = DEPTH
    d = D_MODEL
    return {
        'x': nrm((BATCH, SEQ, d), 1.0),
        'mem': nrm((BATCH, N_MEM, d), 1.0),
        'ffn1_pre_g': gain((L, d)),
        'ffn1_w_gate': nrm((L, d, D_FF), d ** -0.5),
        'ffn1_w_up': nrm((L, d, D_FF), d ** -0.5),
        'ffn1_w_down': nrm((L, D_FF, d), D_FF ** -0.5),
        'ffn1_post_g': gain((L, d)),
        'mix_pre_g': gain((L, d)),
        'w_in': nrm((L, d, IN_COLS), d ** -0.5),
        'conv_w': nrm((L, CONV_K, CONV_CH), CONV_K ** -0.5),
        'conv_b': nrm((L, CONV_CH), 0.02),
        'conv_ln_g': gain((L, CONV_CH)),
        'conv_ln_b': nrm((L, CONV_CH), 0.02),
        'qk_conv_w': nrm((L, QK_CONV_K, 2 * M_WIDTH), QK_CONV_K ** -0.5),
        'qk_conv_b': nrm((L, 2 * M_WIDTH), 0.02),
        'b_igate': nrm((L, M_HEADS), 0.1),
        'b_fgate': jnp.linspace(3.0, 6.0, M_HEADS, dtype=jnp.float32)[None, :] + nrm((L, M_HEADS), 0.1),
        'mlstm_norm_g': gain((L, M_WIDTH)),
        'w_out': nrm((L, MIX_WIDTH, d), MIX_WIDTH ** -0.5),
        'mix_post_g': gain((L, d)),
        'xattn_pre_g': gain((L, d)),
        'mem_norm_g': gain((L, d)),
        'xattn_wq': nrm((L, d, d), d ** -0.5),
        'xattn_wk': nrm((L, d, d), d ** -0.5),
        'xattn_wv': nrm((L, d, d), d ** -0.5),
        'xattn_wo': nrm((L, d, d), d ** -0.5),
        'xattn_post_g': gain((L, d)),
        'ffn2_pre_g': gain((L, d)),
        'ffn2_w_gate': nrm((L, d, D_FF), d ** -0.5),
        'ffn2_w_up': nrm((L, d, D_FF), d ** -0.5),
        'ffn2_w_down': nrm((L, D_FF, d), D_FF ** -0.5),
        'ffn2_post_g': gain((L, d)),
    }


def reference(x, mem, ffn1_pre_g, ffn1_w_gate, ffn1_w_up, ffn1_w_down, ffn1_post_g,
              mix_pre_g, w_in, conv_w, conv_b, conv_ln_g, conv_ln_b, qk_conv_w, qk_conv_b,
              b_igate, b_fgate, mlstm_norm_g, w_out, mix_post_g,
              xattn_pre_g, mem_norm_g, xattn_wq, xattn_wk, xattn_wv, xattn_wo, xattn_post_g,
              ffn2_pre_g, ffn2_w_gate, ffn2_w_up, ffn2_w_down, ffn2_post_g):
    h = x
    for l in range(DEPTH):
        f = swiglu(rmsnorm(h, ffn1_pre_g[l]), ffn1_w_gate[l], ffn1_w_up[l], ffn1_w_down[l])
        h = h + 0.5 * rmsnorm(f, ffn1_post_g[l])
        m = parallel_mixer(rmsnorm(h, mix_pre_g[l]), w_in[l], conv_w[l], conv_b[l], conv_ln_g[l],
                           conv_ln_b[l], qk_conv_w[l], qk_conv_b[l], b_igate[l], b_fgate[l],
                           mlstm_norm_g[l], w_out[l])
        h = h + rmsnorm(m, mix_post_g[l])
        c = memory_cross_attention(rmsnorm(h, xattn_pre_g[l]), rmsnorm(mem, mem_norm_g[l]),
                                   xattn_wq[l], xattn_wk[l], xattn_wv[l], xattn_wo[l])
        h = h + rmsnorm(c, xattn_post_g[l])
        f = swiglu(rmsnorm(h, ffn2_pre_g[l]), ffn2_w_gate[l], ffn2_w_up[l], ffn2_w_down[l])
        h = h + 0.5 * rmsnorm(f, ffn2_post_g[l])
    return h
```

```python
import math
import numpy as np
import ml_dtypes
import concourse.bass as bass
import concourse.mybir as mybir
from concourse.bass_utils import run_bass_kernel_spmd

F32 = mybir.dt.float32
BF16 = mybir.dt.bfloat16
AF = mybir.ActivationFunctionType
ALU = mybir.AluOpType
AX = mybir.AxisListType

D = 1024
DFF = 2816
NMEM = 256
CCH = 512
CK = 31
MH = 4
MW = 512
QK = 4
INC = 3080
XH = 4
XD = 256
EPS = 1e-6
TT = 512
NSUB = TT // 128
NCORES = 8
SEQ = 8192

PV_PRE = 0
PV_CW = 40
PV_CB = PV_CW + 4 * CK
PV_LG = PV_CB + 4
PV_LB = PV_LG + 4
PV_QW = PV_LB + 4
PV_QB = PV_QW + 32
PV_BI = PV_QB + 8
PV_BF = PV_BI + 1
PV_N = PV_BF + 1

RG_POST = 0
RG_M = 4 * D
RG_N = RG_M + MW


class Buf:
    __slots__ = ("name", "w", "r", "al", "cw")

    def __init__(self, name):
        self.name = name
        self.w = None
        self.r = {}
        self.al = ()
        self.cw = None


def alias_groups(ga, gb):
    for a in ga:
        a.al = tuple(a.al) + tuple(gb)
    for b in gb:
        b.al = tuple(b.al) + tuple(ga)


class KB:
    def __init__(self, nc):
        self.nc = nc
        self.engs = {"pe": nc.tensor, "act": nc.scalar, "dve": nc.vector,
                     "pool": nc.gpsimd, "sp": nc.sync}
        self.sems = {}
        self.cnt = {}
        self.known = {e: {} for e in self.engs}
        for e in ("pe", "act", "dve", "pool"):
            self.newsem(e)
        self.nins = 0
        self.nwait = 0

    def newsem(self, name):
        self.sems[name] = self.nc.alloc_semaphore("s_" + name)
        self.cnt[name] = 0
        return name

    def op(self, eng, fn, reads=(), writes=(), dma_sem=None):
        deps = {}

        def add(tag):
            if tag is None:
                return
            k, v = tag
            if deps.get(k, 0) < v:
                deps[k] = v

        for b in reads:
            add(b.w)
        for b0 in writes:
            for b in (b0,) + tuple(b0.al):
                if not (eng == "pe" and b.w is not None and b.w[0] == eng and dma_sem is None):
                    add(b.w)
                for k, v in b.r.items():
                    if k == eng and dma_sem is None:
                        continue
                    add((k, v))
        kn = self.known[eng]
        waits = []
        for k, v in deps.items():
            if eng == "pe" and k == "pe" and dma_sem is None:
                continue
            if kn.get(k, 0) < v:
                waits.append((k, v))
                kn[k] = v
        E = self.engs[eng]
        for k, v in waits[1:]:
            E.wait_ge(self.sems[k], v)
            self.nwait += 1
        ins = fn(E)
        if waits:
            k, v = waits[0]
            ins._wait_ge(self.sems[k], v)
        if dma_sem is None:
            key = eng
            self.cnt[key] += 1
            ins.then_inc(self.sems[key], 1)
        else:
            key = dma_sem
            self.cnt[key] += 16
            ins.then_inc(self.sems[key], 16)
        tag = (key, self.cnt[key])
        for b in reads:
            if b.r.get(key, 0) < tag[1]:
                b.r[key] = tag[1]
        for b0 in writes:
            b0.w = tag
            b0.r = {}
            for b in b0.al:
                b.w = tag
                b.r = {}
        self.nins += 1
        return ins

    def wait_all(self, eng, bufs):
        deps = {}
        for b in bufs:
            for tag in [b.w] + list(b.r.items()):
                if tag is None:
                    continue
                k, v = tag
                if deps.get(k, 0) < v:
                    deps[k] = v
        kn = self.known[eng]
        for k, v in deps.items():
            if kn.get(k, 0) < v:
                self.engs[eng].wait_ge(self.sems[k], v)
                kn[k] = v


def build_program(seq=SEQ, stages=("ffn1", "mix", "xattn", "ffn2")):
    assert seq % TT == 0
    ntile = seq // TT
    nc = bass.Bass("TRN2", target_bir_lowering=False)
    kb = KB(nc)
    op = kb.op

    def dram_in(name, shape, dt=F32):
        return nc.dram_tensor(name, list(shape), dt, kind="ExternalInput").ap()

    x_d = dram_in("x", [seq, D])
    mem_d = dram_in("mem", [NMEM, D])
    pv_d = dram_in("pvec", [128, PV_N])
    rg_d = dram_in("rgain", [128, RG_N])
    cst_d = dram_in("consts", [128, 1032])
    wd = {}
    for nm, shp in (("ffn1_w_gate", [D, DFF]), ("ffn1_w_up", [D, DFF]), ("ffn1_w_down", [DFF, D]),
                    ("w_in", [D, INC]), ("w_out", [D, D]),
                    ("xattn_wq", [D, D]), ("xattn_wk", [D, D]), ("xattn_wv", [D, D]), ("xattn_wo", [D, D]),
                    ("ffn2_w_gate", [D, DFF]), ("ffn2_w_up", [D, DFF]), ("ffn2_w_down", [DFF, D])):
        wd[nm] = dram_in(nm, shp)
    out_d = nc.dram_tensor("out", [seq, D], F32, kind="ExternalOutput").ap()

    def scratch(name, shape):
        return nc.dram_tensor(name, list(shape), BF16, kind="Internal").ap()

    sc = {}
    for f in ("ffn1", "ffn2"):
        sc[f + "_g"] = scratch(f + "_sg", [11, 128, 8, 256])
        sc[f + "_u"] = scratch(f + "_su", [11, 128, 8, 256])
        sc[f + "_d"] = scratch(f + "_sd", [22, 128, D])
    sc["in_fm"] = scratch("s_in_fm", [8, 128, 8, 256])
    sc["in_v"] = scratch("s_in_v", [8, 128, 512])
    sc["in_o"] = scratch("s_in_o", [8, 128, 512])
    sc["w_out"] = scratch("s_w_out", [8, 128, D])
    sc["wq"] = scratch("s_wq", [4, 128, 8, 256])
    sc["wk"] = scratch("s_wk", [4, 128, 8, 256])
    sc["wv"] = scratch("s_wv", [8, 128, D])
    sc["wo"] = scratch("s_wo", [8, 128, D])
    sc["qdiag"] = scratch("s_qdiag", [2, 128, 16, 128])
    sc["cdiag"] = scratch("s_cdiag", [8, 128, 16, 128])

    def sb(name, shape, dt):
        return nc.alloc_sbuf_tensor(name, list(shape), dt)

    pv = sb("pv", [128, PV_N], F32)
    rg = sb("rg", [128, RG_N], F32)
    cst = sb("cst", [128, 1032], F32)
    cstb = sb("cstb", [128, 512], BF16)
    B_const = Buf("const")
    ident_b = cstb[:, 0:128]
    mask_f = cst[:, 128:256]
    ident_f = cst[:, 0:128]
    ones_b = cstb[:, 256:384]
    sel_f = cst[:, 384:512]
    mhalf = cst[:, 512:513]
    cwh = sb("cwh", [128, 4 * CK], F32)

    NSLOT = 6
    ring = []
    for i in range(NSLOT):
        t = sb(f"wr{i}", [128, 2048], BF16)
        ring.append((t, Buf(f"wr{i}"), kb.newsem(f"wr{i}")))
    ring_i = [0]

    converted = {}
    stg_i = [0]
    pending_store = []
    jit_state = {}

    def flush_store():
        while pending_store:
            pending_store.pop(0)()

    def fetch(kind, name, idx, w_ap=None, col0=0, W=256, n=1, gcol=None, store=True):
        t, b, sm_ = ring[ring_i[0] % NSLOT]
        ring_i[0] += 1
        key = (name, idx)
        if kind == "cu":
            piece = sc[name][idx].rearrange("p k c -> p (k c)")
            ncols = 2048
        elif kind == "nat":
            piece = sc[name][idx:idx + n].rearrange("j p c -> p j c")
            ncols = n * W
        else:
            piece = sc[name][idx].rearrange("p m c -> p (m c)")
            ncols = 2048
        if key in converted:
            flush_store()
            op("sp", lambda E: E.dma_start(out=t[:, 0:ncols], in_=piece), reads=[converted[key]], writes=[b], dma_sem=sm_)
            return t, b
        bsc = Buf("sc_%s_%d" % (name, idx))
        converted[key] = bsc
        if kind == "diag":
            for mm in range(16):
                m = idx * 16 + mm
                wsrc, wn = (cwh, 4 * CK) if name == "cdiag" else (pv[:, PV_QW:PV_QW + 32], 32)
                if m < wn:
                    op("dve", lambda E: E.tensor_scalar(out=t[:, mm * 128:(mm + 1) * 128], in0=ident_f,
                                                        scalar1=wsrc[:, m:m + 1], scalar2=None, op0=ALU.mult),
                       reads=[B_const], writes=[b])
                else:
                    op("dve", lambda E: E.memset(t[:, mm * 128:(mm + 1) * 128], 0.0), writes=[b])
        else:
            i = stg_i[0] % 2
            stg_i[0] += 1
            stg_t, stg_b2, stg_s = jit_state["stg"][i]
            if kind == "cu":
                kk, cc = 8, 256
                src = w_ap[:, col0 + idx * 256:col0 + (idx + 1) * 256].rearrange("(k p) c -> p k c", p=128)
                g0 = 0
            else:
                kk, cc = n, W
                src = w_ap[idx * 128:(idx + n) * 128, col0:col0 + W].rearrange("(j p) c -> p j c", p=128)
                g0 = idx
            sview = stg_t[:, 0:kk * cc].rearrange("p (k c) -> p k c", c=cc)
            tview = t[:, 0:kk * cc].rearrange("p (k c) -> p k c", c=cc)
            op("sp", lambda E: E.dma_start(out=sview, in_=src), writes=[stg_b2], dma_sem=stg_s)
            ce = ("dve", "pool")[stg_i[0] % 2]
            if gcol is not None:
                gv = pv[:, gcol + g0:gcol + g0 + kk].unsqueeze(2).to_broadcast([128, kk, cc])
                op(ce, lambda E: E.tensor_tensor(out=tview, in0=sview, in1=gv, op=ALU.mult),
                   reads=[stg_b2, B_const], writes=[b])
            else:
                ce = ("act", "dve", "pool")[stg_i[0] % 3]
                if ce == "act":
                    op("act", lambda E: E.activation(out=tview, in_=sview, func=AF.Copy), reads=[stg_b2], writes=[b])
                else:
                    op(ce, lambda E: E.tensor_copy(out=tview, in_=sview), reads=[stg_b2], writes=[b])
        flush_store()
        if store:
            pending_store.append(lambda: op("sp", lambda E: E.dma_start(out=piece, in_=t[:, 0:ncols]),
                                            reads=[b], writes=[bsc], dma_sem=sm_))
        return t, b

    banks = []
    for i in range(8):
        t = nc.alloc_psum_tensor(f"ps{i}", [128, 512], F32)
        banks.append((t, Buf(f"ps{i}")))
    bank_i = [0]

    def bank():
        r = banks[bank_i[0] % 8]
        bank_i[0] += 1
        return r

    hbuf = []
    for i in range(2):
        t = sb(f"h{i}", [128, NSUB, D], F32)
        hbuf.append((t, [Buf(f"h{i}_{s}") for s in range(NSUB)], kb.newsem(f"h{i}")))
    xnT = sb("xnT", [128, 8, TT], BF16)
    B_xnT = [Buf(f"xnT{s}") for s in range(NSUB)]
    _stg = []
    for i in range(2):
        bst = Buf(f"stg{i}")
        alias_groups([bst], hbuf[1][1][2 * i:2 * i + 2])
        _stg.append((hbuf[1][0][:, 2 * i:2 * i + 2, :].rearrange("p a b -> p (a b)"), bst, kb.newsem(f"stg{i}")))
    jit_state["stg"] = _stg
    def carve(arena, off, shape, dt):
        n = 1
        for d_ in shape[1:]:
            n *= d_
        nb = n * (2 if dt == BF16 else 4)
        assert off % 4 == 0 and nb % 4 == 0
        a = arena[:, off // 4:(off + nb) // 4]
        if dt == BF16:
            a = a.bitcast(BF16)
        if len(shape) == 3:
            a = a.rearrange("p (a b) -> p a b", b=shape[2])
        elif len(shape) == 4:
            a = a.rearrange("p (a b c) -> p a b c", b=shape[2], c=shape[3])
        return a

    arA = sb("arenaA", [128, 29184 // 4], F32)
    arB = sb("arenaB", [128, 32768 // 4], F32)
    arC = sb("arenaC", [128, 31360 // 4], F32)
    hid = carve(arA, 0, [128, 22, TT], BF16)
    B_hid = [Buf(f"hid{j}") for j in range(22)]
    sg = carve(arA, 22528, [128, 2, TT], F32)
    B_sg = [Buf("sg0"), Buf("sg1")]
    zq = carve(arA, 0, [128, 8, 516], BF16)
    B_zq = [Buf(f"zq{g}") for g in range(8)]
    ubf = carve(arA, 16480, [128, 4, 512], F32)
    ub = carve(arA, 24672, [128, 4, 544], BF16)
    B_ubf = [Buf(f"ubf{g}") for g in range(4)]
    B_ub = [Buf(f"ub{g}") for g in range(4)]
    alias_groups(B_hid + B_sg, B_zq + B_ub + B_ubf)
    xs = sb("xs", [128, 2, D], BF16)
    B_xs = [Buf("xs0"), Buf("xs1")]
    junk = sb("junk", [128, 2, D], F32)
    B_jh = [[Buf("junk00"), Buf("junk01")], [Buf("junk10"), Buf("junk11")]]
    B_junk = [B_jh[0], B_jh[1]]
    yconv = carve(arB, 0, [128, 4, TT], F32)
    ybf = carve(arB, 8192, [128, 4, TT], BF16)
    ysq = carve(arB, 12288, [128, 4, TT], BF16)
    mixT = carve(arB, 16384, [128, 8, TT], BF16)
    qkT = carve(arB, 24576, [128, 8, TT], BF16)
    B_yc = [Buf(f"yc{g}") for g in range(4)]
    B_ybf = [Buf(f"ybf{g}") for g in range(4)]
    B_mixT = [[Buf(f"mixT{j}_{q}") for q in range(NSUB)] for j in range(8)]
    B_qk = [Buf(f"qk{g}") for g in range(8)]
    vaug = carve(arC, 0, [128, NSUB, 4, 129], BF16)
    og = carve(arC, 4128, [128, NSUB, 512], BF16)
    ktm = carve(arC, 8224, [128, NSUB, 4, 128], BF16)
    hn = carve(arC, 12320, [128, 1, 512], F32)
    hg = carve(arC, 14368, [128, 2, 512], BF16)
    WT = carve(arC, 16416, [128, 2, 512], BF16)
    gE = carve(arC, 18976, [128, 512], F32)
    gNB = carve(arC, 21024, [128, 512], F32)
    gA = carve(arC, 23072, [128, 512], F32)
    gM = carve(arC, 25120, [128, 512], F32)
    lnm = carve(arC, 27168, [128, 512], F32)
    lnr = carve(arC, 29216, [128, 512], F32)
    B_v = [Buf(f"v{q}") for q in range(NSUB)]
    B_og = [Buf(f"og{q}") for q in range(NSUB)]
    B_ktm = [Buf(f"ktm{q}") for q in range(NSUB)]
    B_hn = [Buf("hn0"), Buf("hn1")]
    B_hg = [Buf("hg0"), Buf("hg1")]
    B_WT = [Buf("WT0"), Buf("WT1")]
    B_gate = Buf("gate")
    B_ln = Buf("ln")
    B_lnr = Buf("lnr")
    qxT = carve(arC, 0, [128, 8, TT], BF16)
    attnT = carve(arC, 8192, [128, 8, TT], BF16)
    Pm = [carve(arC, 16384, [128, 4, 256], BF16), carve(arC, 26624, [128, 4, 256], BF16)]
    Pn = [carve(arC, 18432, [128, 4, 256], BF16), carve(arC, 28672, [128, 4, 256], BF16)]
    PT = [carve(arC, 20480, [128, 8, 128], BF16), carve(arC, 22528, [128, 8, 128], BF16)]
    memnT = carve(arC, 22528, [128, 8, 256], BF16)
    B_qx = [Buf(f"qx{j}") for j in range(8)]
    B_at = [[Buf(f"at{j}_{q}") for q in range(NSUB)] for j in range(8)]
    B_P = [Buf("P0"), Buf("P1")]
    B_Pn = [Buf("Pn0"), Buf("Pn1")]
    B_PT = [Buf("PT0"), Buf("PT1")]
    B_memn = [Buf("memn0"), Buf("memn1")]
    B_xst4 = [Buf(f"xst{q}") for q in range(NSUB)]
    C_mix = B_v + B_og + B_ktm + B_hn + B_hg + B_WT + [B_gate, B_ln, B_lnr]
    C_x = B_qx + [b for r_ in B_at for b in r_] + B_P + B_Pn + B_PT + B_memn
    alias_groups(C_mix, C_x)
    alias_groups([B_PT[1]], B_memn)
    Cst = sb("Cst", [128, 4, 129], F32)
    Cbf = sb("Cbf", [128, 4, 129], BF16)
    B_C = [Buf(f"C{h_}") for h_ in range(4)]
    B_Cbf = [Buf(f"Cbf{h_}") for h_ in range(4)]
    KT = sb("KT", [128, 8, 256], BF16)
    Vx = sb("Vx", [128, 2, D], BF16)
    B_KT = Buf("KT")
    B_Vx = Buf("Vx")
    zq_halo = sb("zq_halo", [128, 8, 4], BF16)
    u_halo = sb("u_halo", [128, 4, 30], BF16)
    B_zqh = [Buf(f"zqh{g}") for g in range(8)]
    B_uh = [Buf(f"uh{g}") for g in range(4)]
    wif_f = sb("wif_f", [128, 8, 8], F32)
    wif = sb("wif", [128, 8, 8], BF16)
    wfT = sb("wfT", [128, 32], F32)
    B_wfT = Buf("wfT")
    decB = sb("decB", [128, 16], F32)
    B_dec = Buf("decB")
    sm = sb("small", [128, 64], F32)
    B_sm = Buf("small")
    jdum = sb("jdum", [128, D], BF16)
    B_jdum = Buf("jdum")
    stat4 = sb("stat4", [128, 16], F32)
    B_stat4 = [Buf("stat4_0"), Buf("stat4_1")]
    junkb = sb("junkb", [128, 128], BF16)
    B_junkb = Buf("junkb")
    stat = sb("stat", [128, 64], F32)
    B_stat = [Buf(f"st{i}") for i in range(64)]
    stat_i = [0]

    def st():
        i = stat_i[0] % 64
        stat_i[0] += 1
        return stat[:, i:i + 1], B_stat[i]

    setup_sem = kb.newsem("setup")
    op("sp", lambda E: E.dma_start(out=pv[:, :], in_=pv_d), writes=[B_const], dma_sem=setup_sem)
    op("sp", lambda E: E.dma_start(out=rg[:, :], in_=rg_d), writes=[B_const], dma_sem=setup_sem)
    op("sp", lambda E: E.dma_start(out=cst[:, :], in_=cst_d), writes=[B_const], dma_sem=setup_sem)
    op("dve", lambda E: E.tensor_copy(out=cstb[:, :], in_=cst[:, 0:512]), reads=[B_const], writes=[B_const])
    op("dve", lambda E: E.tensor_scalar_mul(out=cwh[:, :], in0=pv[:, PV_CW:PV_CW + 4 * CK], scalar1=0.5),
       reads=[B_const], writes=[B_const])

    need_ffn1 = "ffn1" in stages
    need_ffn2 = "ffn2" in stages
    need_mix = "mix" in stages
    need_x = "xattn" in stages
    pre_rs = {}

    def prenorm_stats(h_t, h_bufs, subs):
        tmp = {}
        for s in subs:
            k = s % 2
            ss, bss = st()
            op("act", lambda E: E.activation(out=jdum[:, :], in_=h_t[:, s, :], func=AF.Square, accum_out=ss),
               reads=[h_bufs[s]], writes=[B_jdum, bss])
            tmp[s] = (ss, bss)
        for s in subs:
            ss, bss = tmp[s]
            rs, brs = st()
            op("dve", lambda E: E.tensor_scalar(out=rs, in0=ss, scalar1=D * EPS, scalar2=None, op0=ALU.add),
               reads=[bss], writes=[brs])
            pre_rs[s] = (rs, brs)
        for s in subs:
            rs, brs = pre_rs[s]
            op("pool", lambda E: E.tensor_tensor(out=rs, in0=rs, in1=mhalf, op=ALU.pow),
               reads=[brs, B_const], writes=[brs])

    def prenorm(h_t, h_bufs, gcol, nsub=NSUB, dst=None, dst_bufs=None):
        dst = xnT if dst is None else dst
        dst_bufs = B_xnT if dst_bufs is None else dst_bufs
        todo = [s for s in range(nsub) if s not in pre_rs]
        if todo:
            prenorm_stats(h_t, h_bufs, todo)
        for w in range(0, nsub, 2):
            subs = list(range(w, min(w + 2, nsub)))
            pts = {}
            for s in subs:
                k = s % 2
                rs, brs = pre_rs.pop(s)
                op("dve", lambda E: E.tensor_scalar(out=xs[:, k, :], in0=h_t[:, s, :], scalar1=rs, scalar2=math.sqrt(D),
                                                    op0=ALU.mult, op1=ALU.mult), reads=[h_bufs[s], brs], writes=[B_xs[k]])
            for s in subs:
                k = s % 2
                pt, bpt = bank()
                ptb = pt[:, :].bitcast(BF16)
                pts[s] = (ptb, bpt)
                for kc in range(8):
                    op("pe", lambda E: E.transpose(out=ptb[:, kc * 128:(kc + 1) * 128],
                                                   in_=xs[:, k, kc * 128:(kc + 1) * 128], identity=ident_b),
                       reads=[B_xs[k], B_const], writes=[bpt])
            for s in subs:
                ptb, bpt = pts[s]
                dview = dst[:, :, s * 128:(s + 1) * 128]
                pview = ptb[:, :].rearrange("p (k c) -> p k c", c=128)
                if s % 2:
                    op("act", lambda E: E.activation(out=dview, in_=pview, func=AF.Copy), reads=[bpt], writes=[dst_bufs[s]])
                else:
                    op("dve", lambda E: E.tensor_copy(out=dview, in_=pview), reads=[bpt], writes=[dst_bufs[s]])

    def postnorm_group(grp, acc, h_t, h_bufs, gidx, half_scale, next_pre):
        cfac = math.sqrt(D) * (0.5 if half_scale else 1.0)
        g0 = rg[:, RG_POST + gidx * D:RG_POST + gidx * D + 512]
        g1 = rg[:, RG_POST + gidx * D + 512:RG_POST + (gidx + 1) * D]
        tmp = {}
        for s in grp:
            (p0, b0), (p1, b1) = acc[s]
            ss0, bs0 = st()
            ss1, bs1 = st()
            op("act", lambda E: E.activation(out=jdum[:, 0:512], in_=p0[:, :], func=AF.Square, accum_out=ss0),
               reads=[b0], writes=[B_jdum, bs0])
            op("act", lambda E: E.activation(out=jdum[:, 0:512], in_=p1[:, :], func=AF.Square, accum_out=ss1),
               reads=[b1], writes=[B_jdum, bs1])
            tmp[s] = (ss0, bs0, ss1, bs1)
        rss = {}
        for s in grp:
            ss0, bs0, ss1, bs1 = tmp[s]
            rs, brs = st()
            op("dve", lambda E: E.tensor_scalar(out=rs, in0=ss0, scalar1=ss1, scalar2=None, op0=ALU.add),
               reads=[bs0, bs1], writes=[brs])
            op("dve", lambda E: E.tensor_scalar(out=rs, in0=rs, scalar1=D * EPS, scalar2=1.0 / (cfac * cfac),
                                                op0=ALU.add, op1=ALU.mult), reads=[brs], writes=[brs])
            rss[s] = (rs, brs)
        for s in grp:
            rs, brs = rss[s]
            op("pool", lambda E: E.tensor_tensor(out=rs, in0=rs, in1=mhalf, op=ALU.pow),
               reads=[brs, B_const], writes=[brs])
        for s in grp:
            k = s % 2
            rs, brs = rss[s]
            (p0, b0), (p1, b1) = acc[s]
            for (p, b, g, lo) in ((p0, b0, g0, 0), (p1, b1, g1, 512)):
                bj = B_jh[k][lo // 512]
                op("dve", lambda E: E.scalar_tensor_tensor(out=junk[:, k, lo:lo + 512], in0=p[:, :], scalar=rs, in1=g,
                                                           op0=ALU.mult, op1=ALU.mult),
                   reads=[b, brs, B_const], writes=[bj])
                op("dve", lambda E: E.tensor_tensor(out=h_t[:, s, lo:lo + 512], in0=h_t[:, s, lo:lo + 512],
                                                    in1=junk[:, k, lo:lo + 512], op=ALU.add),
                   reads=[bj, h_bufs[s]], writes=[h_bufs[s]])
        if next_pre:
            prenorm_stats(h_t, h_bufs, list(grp))

    def linear_fm(wspec, nunits, src, src_bufs, consume, ntok=TT):
        for u in range(nunits):
            wt, wb = fetch("cu", wspec["name"], u, w_ap=wspec["w"], col0=wspec.get("col0", 0),
                           gcol=wspec.get("gcol"), store=wspec.get("store", True))
            for cc in range(2):
                c = 2 * u + cc
                pt, bpt = bank()
                for kc in range(8):
                    op("pe", lambda E: E.matmul(out=pt[:, 0:ntok], lhsT=wt[:, kc * 256 + cc * 128:kc * 256 + cc * 128 + 128],
                                                rhs=src[:, kc, :], start=(kc == 0), stop=(kc == 7)),
                       reads=[wb] + list(src_bufs), writes=[bpt])
                consume(c, pt, bpt)

    def linear_tm(wspec, nj, W, lhs, lhs_bufs_fn, consume, subs_groups=((0, 1), (2, 3)), group_consume=None):
        per = 2048 // W
        nh = W // 512
        for grp in subs_groups:
            acc = {s: [bank() for _ in range(nh)] for s in grp}
            for j0 in range(0, nj, per):
                n = min(per, nj - j0)
                wt, wb = fetch("nat", wspec["name"], j0, w_ap=wspec["w"], col0=wspec.get("col0", 0), W=W, n=n,
                               gcol=wspec.get("gcol"), store=wspec.get("store", True))
                for jj in range(n):
                    j = j0 + jj
                    for s in grp:
                        for hh in range(nh):
                            pt, bpt = acc[s][hh]
                            op("pe", lambda E: E.matmul(out=pt[:, :], lhsT=lhs[:, j, s * 128:(s + 1) * 128],
                                                        rhs=wt[:, jj * W + hh * 512:jj * W + hh * 512 + 512],
                                                        start=(j == 0), stop=(j == nj - 1)),
                               reads=[wb] + lhs_bufs_fn(j, s), writes=[bpt])
            if group_consume is not None:
                group_consume(grp, acc)
            else:
                for s in grp:
                    consume(s, acc[s])

    def ffn(f, h_t, h_bufs, pre_col, post_idx, next_pre, after_prenorm=None, skip_prenorm=False,
            early_hook=None, mid_hook=None):
        if not skip_prenorm:
            prenorm(h_t, h_bufs, pre_col)
        if after_prenorm is not None:
            after_prenorm()
        for u in range(11):
            if u == 2 and early_hook is not None:
                early_hook()
            wg, bg = fetch("cu", f + "_g", u, w_ap=wd[f + "_w_gate"], gcol=pre_col)
            wu, bu = fetch("cu", f + "_u", u, w_ap=wd[f + "_w_up"], gcol=pre_col)
            for cc in range(2):
                j = 2 * u + cc
                pg, bpg = bank()
                pu, bpu = bank()
                for (wt, wb, pt, bpt) in ((wg, bg, pg, bpg), (wu, bu, pu, bpu)):
                    for kc in range(8):
                        op("pe", lambda E: E.matmul(out=pt[:, :], lhsT=wt[:, kc * 256 + cc * 128:kc * 256 + cc * 128 + 128],
                                                    rhs=xnT[:, kc, :], start=(kc == 0), stop=(kc == 7)),
                           reads=[wb] + B_xnT, writes=[bpt])
                k = j % 2
                op("act", lambda E: E.activation(out=sg[:, k, :], in_=pg[:, :], func=AF.Silu),
                   reads=[bpg], writes=[B_sg[k]])
                op("dve", lambda E: E.tensor_tensor(out=hid[:, j, :], in0=sg[:, k, :], in1=pu[:, :], op=ALU.mult),
                   reads=[B_sg[k], bpu], writes=[B_hid[j]])
        if mid_hook is not None:
            mid_hook()
        linear_tm(dict(name=f + "_d", w=wd[f + "_w_down"]), 22, D, hid, lambda j, s: [B_hid[j]], None,
                  group_consume=lambda grp, acc: postnorm_group(grp, acc, h_t, h_bufs, post_idx, True, next_pre))


    LN_DK = math.log(1.0 / math.sqrt(128.0))
    MU = sm[0:4, 0:5]
    NBc = sm[0:4, 8:9]
    Mc = sm[0:4, 9:10]
    negbf = sm[0:4, 10:11]
    dtmp = sm[0:4, 12:16]
    dec = sm[0:4, 16:20]
    if need_mix:
        op("pool", lambda E: E.memset(sm[:, :], 0.0), writes=[B_sm])
        op("pool", lambda E: E.memset(sm[:, 11:12], LN_DK), writes=[B_sm])
        op("pool", lambda E: E.memset(sm[:, 20:21], 1e-5), writes=[B_sm])
        op("pool", lambda E: E.memset(Cst[:, :, :], 0.0), writes=B_C)
        op("pool", lambda E: E.memset(zq_halo[:, :, :], 0.0), writes=B_zqh)
        op("pool", lambda E: E.memset(u_halo[:, :, :], 0.0), writes=B_uh)
        op("dve", lambda E: E.tensor_scalar(out=negbf, in0=pv[0:4, PV_BF:PV_BF + 1], scalar1=-1.0, scalar2=None,
                                            op0=ALU.mult), reads=[B_const, B_sm], writes=[B_sm])
        with nc.allow_non_contiguous_dma(reason="tiny gate weights"):
            op("sp", lambda E: E.dma_start(out=wif_f[:, :, :],
                                           in_=wd["w_in"][:, 3072:3080].rearrange("(k p) c -> p k c", p=128)),
               writes=[B_const], dma_sem=setup_sem)
        op("dve", lambda E: E.tensor_tensor(out=wif[:, :, :], in0=wif_f[:, :, :],
                                            in1=pv[:, PV_PRE + 8:PV_PRE + 16].unsqueeze(2).to_broadcast([128, 8, 8]),
                                            op=ALU.mult), reads=[B_const], writes=[B_const])

    def mixer(h_t, h_bufs, after_prenorm=None):
        prenorm(h_t, h_bufs, PV_PRE + 8)
        if after_prenorm is not None:
            after_prenorm()
        op("pool", lambda E: E.memset(vaug[:, :, :, 128:129], 1.0), writes=B_v)
        op("pool", lambda E: E.memset(lnm[0:4, :], 0.0), writes=[B_ln])
        def cons_v(s, acc):
            pt, bpt = acc[0]
            op("dve", lambda E: E.tensor_copy(out=vaug[:, s, :, 0:128],
                                              in_=pt[:, :].rearrange("p (a b) -> p a b", b=128)),
               reads=[bpt], writes=[B_v[s]])

        og_todo = []

        def og_finish():
            gm_ = rg[:, RG_M:RG_M + MW]
            for s in og_todo:
                op("pool", lambda E: E.tensor_tensor(out=og[:, s, :], in0=og[:, s, :], in1=gm_, op=ALU.mult),
                   reads=[B_og[s], B_const], writes=[B_og[s]])
                op("pool", lambda E: E.tensor_tensor(out=og[:, s, :], in0=og[:, s, :], in1=gm_, op=ALU.add),
                   reads=[B_og[s], B_const], writes=[B_og[s]])

        def cons_o(s, acc):
            pt, bpt = acc[0]
            op("act", lambda E: E.activation(out=og[:, s, :], in_=pt[:, :], func=AF.Tanh, scale=0.5),
               reads=[bpt], writes=[B_og[s]])
            og_todo.append(s)

        pi, bpi = bank()
        pf, bpf = bank()
        for (pt, bpt, c0) in ((pi, bpi, 0), (pf, bpf, 4)):
            for kc in range(8):
                op("pe", lambda E: E.matmul(out=pt[0:4, :], lhsT=wif[:, kc, c0:c0 + 4], rhs=xnT[:, kc, :],
                                            start=(kc == 0), stop=(kc == 7)),
                   reads=[B_const] + B_xnT, writes=[bpt])
        G = [B_gate, B_sm]
        op("act", lambda E: E.activation(out=gE[0:4, :], in_=pf[0:4, :], func=AF.Exp, scale=-1.0, bias=negbf),
           reads=[bpf, B_sm], writes=[B_gate])
        op("act", lambda E: E.activation(out=gE[0:4, :], in_=gE[0:4, :], func=AF.Ln, bias=1.0),
           reads=[B_gate], writes=[B_gate])
        op("dve", lambda E: E.tensor_tensor_scan(out=gNB[0:4, :], data0=gE[0:4, :], data1=lnm[0:4, :], initial=NBc,
                                                 op0=ALU.add, op1=ALU.add), reads=G + [B_ln], writes=[B_gate])
        op("dve", lambda E: E.scalar_tensor_tensor(out=gA[0:4, :], in0=pi[0:4, :], scalar=pv[0:4, PV_BI:PV_BI + 1],
                                                   in1=gNB[0:4, :], op0=ALU.add, op1=ALU.add),
           reads=[bpi, B_const] + G, writes=[B_gate])
        op("dve", lambda E: E.tensor_tensor_scan(out=gM[0:4, :], data0=gA[0:4, :], data1=gA[0:4, :], initial=Mc,
                                                 op0=ALU.max, op1=ALU.max), reads=G, writes=[B_gate])
        op("dve", lambda E: E.tensor_copy(out=sm[0:4, 0:1], in_=Mc), reads=G, writes=[B_sm])
        op("dve", lambda E: E.tensor_copy(out=sm[0:4, 1:5],
                                          in_=gM[0:4, :].rearrange("p (c t) -> p c t", t=128)[:, :, 127]),
           reads=G, writes=[B_sm])
        op("dve", lambda E: E.tensor_copy(out=Mc, in_=gM[0:4, 511:512]), reads=G, writes=[B_sm])
        op("dve", lambda E: E.tensor_copy(out=NBc, in_=gNB[0:4, 511:512]), reads=G, writes=[B_sm])
        mub = sm[0:4, 1:5].unsqueeze(2).to_broadcast([4, 4, 128])
        for arr in (gA, gNB):
            a3 = arr[0:4, :].rearrange("p (c t) -> p c t", t=128)
            op("dve", lambda E: E.tensor_tensor(out=a3, in0=a3, in1=mub, op=ALU.subtract), reads=G, writes=[B_gate])
        op("dve", lambda E: E.tensor_tensor(out=dtmp, in0=sm[0:4, 0:4], in1=sm[0:4, 1:5], op=ALU.subtract),
           reads=G, writes=[B_sm])
        op("act", lambda E: E.activation(out=gA[0:4, :], in_=gA[0:4, :], func=AF.Exp, bias=sm[0:4, 11:12]),
           reads=G, writes=[B_gate])
        op("act", lambda E: E.activation(out=gNB[0:4, :], in_=gNB[0:4, :], func=AF.Exp), reads=G, writes=[B_gate])
        op("act", lambda E: E.activation(out=dec, in_=dtmp, func=AF.Exp), reads=G, writes=[B_sm])
        linear_tm(dict(name="in_v", w=wd["w_in"], col0=2048, gcol=PV_PRE + 8), 8, 512, xnT, lambda j, s: [B_xnT[s]], cons_v, subs_groups=((0, 1, 2, 3),))
        linear_tm(dict(name="in_o", w=wd["w_in"], col0=2560, gcol=PV_PRE + 8), 8, 512, xnT, lambda j, s: [B_xnT[s]], cons_o, subs_groups=((0, 1, 2, 3),))
        def cons_z(c, pt, bpt):
            if c < 4:
                op("act", lambda E: E.activation(out=ubf[:, c, :], in_=pt[:, :], func=AF.Copy),
                   reads=[bpt], writes=[B_ubf[c]])
                op("pool", lambda E: E.tensor_copy(out=ub[:, c, 0:30], in_=u_halo[:, c, :]),
                   reads=[B_uh[c]], writes=[B_ub[c]])
            elif c < 8:
                g = c - 4
                k = g % 2
                op("act", lambda E: E.activation(out=junk[:, k, 0:512], in_=pt[:, :], func=AF.Tanh, scale=0.5),
                   reads=[bpt], writes=[B_jh[k][0]])
                op("dve", lambda E: E.scalar_tensor_tensor(out=ub[:, g, 30:542], in0=junk[:, k, 0:512], scalar=1.0,
                                                           in1=ubf[:, g, :], op0=ALU.add, op1=ALU.mult),
                   reads=[B_jh[k][0], B_ubf[g]], writes=[B_ub[g]])
                op("pool", lambda E: E.tensor_copy(out=u_halo[:, g, :], in_=ub[:, g, 512:542]),
                   reads=[B_ub[g]], writes=[B_uh[g]])
            else:
                gq = c - 8
                op("act", lambda E: E.activation(out=zq[:, gq, 3:515], in_=pt[:, :], func=AF.Copy),
                   reads=[bpt], writes=[B_zq[gq]])
                op("pool", lambda E: E.tensor_copy(out=zq[:, gq, 0:3], in_=zq_halo[:, gq, 0:3]),
                   reads=[B_zqh[gq]], writes=[B_zq[gq]])
                op("pool", lambda E: E.tensor_copy(out=zq_halo[:, gq, 0:3], in_=zq[:, gq, 512:515]),
                   reads=[B_zq[gq]], writes=[B_zqh[gq]])
                qpend.append(gq)
                if len(qpend) > 2:
                    emit_qconv(qpend.pop(0))

        qpend = []
        qslots = {}

        def emit_qconv(gq):
            if gq // 4 not in qslots:
                qslots[gq // 4] = fetch("diag", "qdiag", gq // 4)
            wt, wb = qslots[gq // 4]
            pt2, bpt2 = bank()
            for j in range(QK):
                m = (gq * QK + j) % 16
                op("pe", lambda E: E.matmul(out=pt2[:, :], lhsT=wt[:, m * 128:(m + 1) * 128], rhs=zq[:, gq, j:j + 512],
                                            start=(j == 0), stop=(j == QK - 1)), reads=[wb, B_zq[gq]], writes=[bpt2])
            op("act", lambda E: E.activation(out=qkT[:, gq, :], in_=pt2[:, :], func=AF.Silu,
                                             bias=pv[:, PV_QB + gq:PV_QB + gq + 1]),
               reads=[bpt2, B_const], writes=[B_qk[gq]])

        linear_fm(dict(name="in_fm", w=wd["w_in"], gcol=PV_PRE + 8), 8, xnT, B_xnT, cons_z)
        while qpend:
            emit_qconv(qpend.pop(0))
        pw, bpw = bank()
        for s in range(NSUB):
            op("pe", lambda E: E.matmul(out=pw[:, s * 8:s * 8 + 4], lhsT=gA[0:4, s * 128:(s + 1) * 128],
                                        rhs=ident_f[0:4, 0:4], start=True, stop=True),
               reads=[B_gate, B_const], writes=[bpw])
            op("pe", lambda E: E.matmul(out=pw[:, s * 8 + 4:s * 8 + 8], lhsT=gNB[0:4, s * 128:(s + 1) * 128],
                                        rhs=ident_f[0:4, 0:4], start=True, stop=True),
               reads=[B_gate, B_const], writes=[bpw])
        for hh in range(4):
            op("pe", lambda E: E.matmul(out=pw[:, 32 + hh * 4:32 + hh * 4 + 4], lhsT=cst[0:4, 520 + hh * 128:520 + (hh + 1) * 128],
                                        rhs=dec, start=True, stop=True), reads=[B_sm, B_const], writes=[bpw])
        op("dve", lambda E: E.tensor_copy(out=wfT[:, :], in_=pw[:, 0:32]), reads=[bpw], writes=[B_wfT])
        op("dve", lambda E: E.tensor_copy(out=decB[:, :], in_=pw[:, 32:48]), reads=[bpw], writes=[B_dec])

        dslots = {}
        for g in range(4):
            pt, bpt = bank()
            for j in range(CK):
                m = g * CK + j
                if m // 16 not in dslots:
                    dslots[m // 16] = fetch("diag", "cdiag", m // 16)
                wt, wb = dslots[m // 16]
                op("pe", lambda E: E.matmul(out=pt[:, :], lhsT=wt[:, (m % 16) * 128:(m % 16 + 1) * 128],
                                            rhs=ub[:, g, j:j + 512], start=(j == 0), stop=(j == CK - 1)),
                   reads=[wb, B_ub[g]], writes=[bpt])
            cb = pv[:, PV_CB + g:PV_CB + g + 1]
            op("act", lambda E: E.activation(out=yconv[:, g, :], in_=pt[:, :], func=AF.Identity, bias=cb),
               reads=[bpt, B_const], writes=[B_yc[g]])
            op("act", lambda E: E.activation(out=ybf[:, g, :], in_=pt[:, :], func=AF.Identity, bias=cb),
               reads=[bpt, B_const], writes=[B_ybf[g]])
            op("act", lambda E: E.activation(out=ysq[:, g, :], in_=pt[:, :], func=AF.Square, bias=cb),
               reads=[bpt, B_const], writes=[B_ybf[g]])

        og_finish()
        ctx = {}
        rot = [0]

        def sbank():
            r = banks[4 + rot[0] % 4]
            rot[0] += 1
            return r

        def m_p1(s):
            ts_ = slice(s * 128, (s + 1) * 128)
            k2 = s % 2
            wg_s = wfT[:, s * 8:s * 8 + 4]
            WTk = WT[:, k2, :].rearrange("p (a b) -> p a b", b=128)
            pk, bpk = sbank()
            pkb = pk[:, :].bitcast(BF16)
            for hh in range(4):
                op("pe", lambda E: E.transpose(out=pkb[:, hh * 128:(hh + 1) * 128], in_=qkT[:, 4 + hh, ts_],
                                               identity=ident_b), reads=[B_qk[4 + hh], B_const], writes=[bpk])
            for hh in range(4):
                op("act", lambda E: E.activation(out=ktm[:, s, hh, :], in_=pkb[:, hh * 128:(hh + 1) * 128], func=AF.Copy,
                                                 scale=wfT[:, s * 8 + hh:s * 8 + hh + 1]),
                   reads=[bpk, B_wfT], writes=[B_ktm[s]])
            pS, bpS = sbank()
            for hh in range(4):
                op("pe", lambda E: E.matmul(out=pS[:, hh * 128:(hh + 1) * 128], lhsT=qkT[:, 4 + hh, ts_],
                                            rhs=qkT[:, hh, ts_], start=True, stop=True),
                   reads=[B_qk[4 + hh], B_qk[hh]], writes=[bpS])
            op("dve", lambda E: E.tensor_tensor(out=WTk, in0=pS[:, :].rearrange("p (a b) -> p a b", b=128),
                                                in1=wg_s.unsqueeze(2).to_broadcast([128, 4, 128]), op=ALU.mult),
               reads=[bpS, B_wfT], writes=[B_WT[k2]])
            op("pool", lambda E: E.tensor_tensor(out=WTk, in0=WTk, in1=mask_f.unsqueeze(1).to_broadcast([128, 4, 128]),
                                                 op=ALU.mult), reads=[B_WT[k2], B_const], writes=[B_WT[k2]])

        def m_p2a(s):
            ts_ = slice(s * 128, (s + 1) * 128)
            k2 = s % 2
            dec_s = decB[:, :].rearrange("p (h c) -> p h c", c=4)[:, :, s]
            WTk = WT[:, k2, :].rearrange("p (a b) -> p a b", b=128)
            op("dve", lambda E: E.tensor_tensor(out=Cst[:, :, :], in0=Cst[:, :, :],
                                                in1=dec_s.unsqueeze(2).to_broadcast([128, 4, 129]), op=ALU.mult),
               reads=B_C + [B_dec], writes=B_C)
            op("act", lambda E: E.activation(out=Cbf[:, :, :], in_=Cst[:, :, :], func=AF.Copy), reads=B_C, writes=B_Cbf)

        def m_p2mm(s):
            ts_ = slice(s * 128, (s + 1) * 128)
            k2 = s % 2
            WTk = WT[:, k2, :].rearrange("p (a b) -> p a b", b=128)
            pNt, bpN = banks[k2]
            pXt, bpX = banks[2 + k2]
            pCt, bpC = sbank()
            for hh in range(4):
                hs = slice(hh * 128, (hh + 1) * 128)
                op("pe", lambda E: E.matmul(out=pNt[:, hs], lhsT=WTk[:, hh, :], rhs=vaug[:, s, hh, 0:128],
                                            start=True, stop=False), reads=[B_WT[k2], B_v[s]], writes=[bpN])
                op("pe", lambda E: E.matmul(out=pNt[:, hs], lhsT=qkT[:, hh, ts_], rhs=Cbf[:, hh, 0:128],
                                            start=False, stop=True), reads=[B_qk[hh], B_Cbf[hh]], writes=[bpN])
                op("pe", lambda E: E.matmul(out=pXt[:, hh:hh + 1], lhsT=WTk[:, hh, :], rhs=vaug[:, s, hh, 128:129],
                                            start=True, stop=False), reads=[B_WT[k2], B_v[s]], writes=[bpX])
                op("pe", lambda E: E.matmul(out=pXt[:, hh:hh + 1], lhsT=qkT[:, hh, ts_], rhs=Cbf[:, hh, 128:129],
                                            start=False, stop=True), reads=[B_qk[hh], B_Cbf[hh]], writes=[bpX])
            for hh in range(4):
                hs = slice(hh * 128, (hh + 1) * 128)
                op("pe", lambda E: E.matmul(out=pCt[:, hs], lhsT=ktm[:, s, hh, :], rhs=vaug[:, s, hh, 0:128],
                                            start=True, stop=True), reads=[B_ktm[s], B_v[s]], writes=[bpC])
                op("pe", lambda E: E.matmul(out=pXt[:, 4 + hh:5 + hh], lhsT=ktm[:, s, hh, :], rhs=vaug[:, s, hh, 128:129],
                                            start=True, stop=True), reads=[B_ktm[s], B_v[s]], writes=[bpX])
            ctx[s] = (pNt, bpN, pXt, bpX, pCt, bpC)

        def m_p2b(s):
            pNt, bpN, pXt, bpX, pCt, bpC = ctx[s]
            op("dve", lambda E: E.tensor_tensor(out=Cst[:, :, 0:128], in0=Cst[:, :, 0:128],
                                                in1=pCt[:, :].rearrange("p (a b) -> p a b", b=128), op=ALU.add),
               reads=B_C + [bpC], writes=B_C)
            op("dve", lambda E: E.tensor_tensor(out=Cst[:, :, 128], in0=Cst[:, :, 128], in1=pXt[:, 4:8], op=ALU.add),
               reads=B_C + [bpX], writes=B_C)

        def m_n1(s):
            k2 = s % 2
            pNt, bpN, pXt, bpX, pCt, bpC = ctx[s]
            fl_s = wfT[:, s * 8 + 4:s * 8 + 8]
            ad = stat4[:, (2 * k2) * 4:(2 * k2) * 4 + 4]
            ssq = stat4[:, (2 * k2 + 1) * 4:(2 * k2 + 1) * 4 + 4]
            bq = B_stat4[k2]
            op("act", lambda E: E.activation(out=ad, in_=pXt[:, 0:4], func=AF.Abs), reads=[bpX], writes=[bq])
            for hh in range(4):
                op("act", lambda E: E.activation(out=junkb[:, :], in_=pNt[:, hh * 128:(hh + 1) * 128], func=AF.Square,
                                                 accum_out=ssq[:, hh:hh + 1]), reads=[bpN], writes=[B_junkb, bq])
            op("dve", lambda E: E.tensor_tensor(out=ad, in0=ad, in1=fl_s, op=ALU.max), reads=[bq, B_wfT], writes=[bq])
            op("dve", lambda E: E.reciprocal(out=ad, in_=ad), reads=[bq], writes=[bq])
            op("dve", lambda E: E.tensor_tensor(out=ssq, in0=ssq, in1=ad, op=ALU.mult), reads=[bq], writes=[bq])
            op("dve", lambda E: E.tensor_tensor(out=ssq, in0=ssq, in1=ad, op=ALU.mult), reads=[bq], writes=[bq])
            op("dve", lambda E: E.tensor_scalar(out=ssq, in0=ssq, scalar1=4.0 / 128, scalar2=4.0 * EPS,
                                                op0=ALU.mult, op1=ALU.add), reads=[bq], writes=[bq])
            op("pool", lambda E: E.tensor_tensor(out=ssq, in0=ssq, in1=mhalf.to_broadcast([128, 4]), op=ALU.pow),
               reads=[bq, B_const], writes=[bq])

        def m_n2a(s):
            k2 = s % 2
            pNt, bpN, pXt, bpX, pCt, bpC = ctx[s]
            ad = stat4[:, (2 * k2) * 4:(2 * k2) * 4 + 4]
            ssq = stat4[:, (2 * k2 + 1) * 4:(2 * k2 + 1) * 4 + 4]
            bq = B_stat4[k2]
            op("dve", lambda E: E.tensor_tensor(out=ssq, in0=ssq, in1=ad, op=ALU.mult), reads=[bq], writes=[bq])
            op("dve", lambda E: E.tensor_tensor(out=hn[:, 0, :].rearrange("p (a b) -> p a b", b=128),
                                                in0=pNt[:, :].rearrange("p (a b) -> p a b", b=128),
                                                in1=ssq.unsqueeze(2).to_broadcast([128, 4, 128]), op=ALU.mult),
               reads=[bpN, bq], writes=[B_hn[0]])
            op("pool", lambda E: E.tensor_tensor(out=hg[:, k2, :], in0=hn[:, 0, :], in1=og[:, s, :], op=ALU.mult),
               reads=[B_hn[0], B_og[s]], writes=[B_hg[k2]])

        def m_n2b(s):
            ts_ = slice(s * 128, (s + 1) * 128)
            k2 = s % 2
            ph, bph = sbank()
            phb = ph[:, :].bitcast(BF16)
            for hh in range(4):
                op("pe", lambda E: E.transpose(out=phb[:, hh * 128:(hh + 1) * 128], in_=hg[:, k2, hh * 128:(hh + 1) * 128],
                                               identity=ident_b), reads=[B_hg[k2], B_const], writes=[bph])
            op("act", lambda E: E.activation(out=mixT[:, 4:8, ts_], in_=phb[:, 0:512].rearrange("p (a b) -> p a b", b=128),
                                             func=AF.Copy), reads=[bph], writes=[B_mixT[4 + hh_][s] for hh_ in range(4)])

        m_p1(0)
        m_p1(1)
        m_p2a(0)
        m_p2mm(0)
        pm, bpm = sbank()
        pe2, bpe2 = sbank()
        for g in range(4):
            op("pe", lambda E: E.matmul(out=pm[:, :], lhsT=ones_b, rhs=ybf[:, g, :], start=(g == 0), stop=(g == 3)),
               reads=[B_ybf[g], B_const], writes=[bpm])
        for g in range(4):
            op("pe", lambda E: E.matmul(out=pe2[:, :], lhsT=ones_b, rhs=ysq[:, g, :], start=(g == 0), stop=(g == 3)),
               reads=[B_ybf[g], B_const], writes=[bpe2])
        op("dve", lambda E: E.tensor_scalar(out=lnm, in0=pm[:, :], scalar1=1.0 / CCH, scalar2=None, op0=ALU.mult),
           reads=[bpm], writes=[B_ln])
        op("act", lambda E: E.activation(out=lnr, in_=pm[:, :], func=AF.Square, scale=1.0 / CCH),
           reads=[bpm], writes=[B_lnr])
        op("dve", lambda E: E.scalar_tensor_tensor(out=lnr, in0=pe2[:, :], scalar=1.0 / CCH, in1=lnr,
                                                   op0=ALU.mult, op1=ALU.subtract), reads=[bpe2, B_lnr], writes=[B_lnr])
        op("act", lambda E: E.activation(out=lnr, in_=lnr, func=AF.Sqrt, bias=sm[:, 20:21]), reads=[B_lnr, B_sm], writes=[B_lnr])
        op("dve", lambda E: E.reciprocal(out=lnr, in_=lnr), reads=[B_lnr], writes=[B_lnr])
        def ln_group(g):
            op("dve", lambda E: E.tensor_tensor(out=yconv[:, g, :], in0=yconv[:, g, :], in1=lnm, op=ALU.subtract),
               reads=[B_yc[g], B_ln], writes=[B_yc[g]])
            op("dve", lambda E: E.tensor_tensor(out=yconv[:, g, :], in0=yconv[:, g, :], in1=lnr, op=ALU.mult),
               reads=[B_yc[g], B_lnr], writes=[B_yc[g]])
            op("act", lambda E: E.activation(out=mixT[:, g, :], in_=yconv[:, g, :], func=AF.Silu,
                                             scale=pv[:, PV_LG + g:PV_LG + g + 1], bias=pv[:, PV_LB + g:PV_LB + g + 1]),
               reads=[B_yc[g], B_const], writes=B_mixT[g])

        for k in range(NSUB):
            m_p2b(k)
            if k + 1 < NSUB:
                m_p2a(k + 1)
            if k >= 1:
                m_n2a(k - 1)
            if k + 2 < NSUB:
                m_p1(k + 2)
            if k + 1 < NSUB:
                m_p2mm(k + 1)
            m_n1(k)
            ln_group(k)
            if k >= 1:
                m_n2b(k - 1)
        m_n2a(NSUB - 1)
        m_n2b(NSUB - 1)
        linear_tm(dict(name="w_out", w=wd["w_out"]), 8, D, mixT, lambda j, s: [B_mixT[j][s]], None,
                  group_consume=lambda grp, acc: postnorm_group(grp, acc, h_t, h_bufs, 1, False, True))

    def xattn_setup():
        mt, mb, msem = hbuf[0]
        op("pool", lambda E: E.dma_start(out=mt[:, 0:2, :], in_=mem_d.rearrange("(s p) d -> p s d", p=128)),
           writes=mb[0:2], dma_sem=msem)
        prenorm(mt, mb, PV_PRE + 32, nsub=2, dst=memnT, dst_bufs=B_memn)

        def cons_k(c, pt, bpt):
            op("dve", lambda E: E.tensor_copy(out=KT[:, c, :], in_=pt[:, 0:256]), reads=[bpt], writes=[B_KT])

        linear_fm(dict(name="wk", w=wd["xattn_wk"], gcol=PV_PRE + 32, store=False), 4, memnT, B_memn, cons_k, ntok=256)

        def cons_vx(s, acc):
            for hf in range(2):
                pt, bpt = acc[hf]
                op("act", lambda E: E.activation(out=Vx[:, s, hf * 512:(hf + 1) * 512], in_=pt[:, :], func=AF.Copy),
                   reads=[bpt], writes=[B_Vx])

        linear_tm(dict(name="wv", w=wd["xattn_wv"], gcol=PV_PRE + 32, store=False), 8, D, memnT, lambda j, s: [B_memn[s]], cons_vx, subs_groups=((0, 1),))

    def xattn(h_t, h_bufs, after_prenorm=None):
        prenorm(h_t, h_bufs, PV_PRE + 16)
        if after_prenorm is not None:
            after_prenorm()

        def cons_q(c, pt, bpt):
            if c % 2:
                op("act", lambda E: E.activation(out=qxT[:, c, :], in_=pt[:, :], func=AF.Copy), reads=[bpt], writes=[B_qx[c]])
            else:
                op("dve", lambda E: E.tensor_copy(out=qxT[:, c, :], in_=pt[:, :]), reads=[bpt], writes=[B_qx[c]])

        linear_fm(dict(name="wq", w=wd["xattn_wq"], gcol=PV_PRE + 16), 4, xnT, B_xnT, cons_q)
        sc_ = 1.0 / math.sqrt(XD)

        P4 = [Pm[0], Pn[0], Pm[1], Pn[1]]
        B_P4 = [B_P[0], B_Pn[0], B_P[1], B_Pn[1]]

        def x_s1(s):
            nmx = sm[:, 24 + 8 * s:28 + 8 * s]
            rsum = sm[:, 28 + 8 * s:32 + 8 * s]
            bx = B_xst4[s]
            ts_ = slice(s * 128, (s + 1) * 128)
            pA = [bank(), bank()]
            for hh in range(4):
                pt, bpt = pA[hh // 2]
                o = (hh % 2) * 256
                for dc in range(2):
                    op("pe", lambda E: E.matmul(out=pt[:, o:o + 256], lhsT=qxT[:, 2 * hh + dc, ts_], rhs=KT[:, 2 * hh + dc, :],
                                                start=(dc == 0), stop=(dc == 1)),
                       reads=[B_qx[2 * hh + dc], B_KT], writes=[bpt])
            for i2 in range(2):
                pt, bpt = pA[i2]
                op("dve", lambda E: E.reduce_max(out=nmx[:, 2 * i2:2 * i2 + 2], in_=pt[:, :].rearrange("p (a b) -> p a b", b=256),
                                                 axis=AX.X), reads=[bpt], writes=[bx])
            op("dve", lambda E: E.tensor_scalar(out=nmx, in0=nmx, scalar1=-sc_, scalar2=None, op0=ALU.mult),
               reads=[bx], writes=[bx])
            for hh in range(4):
                pt, bpt = pA[hh // 2]
                o = (hh % 2) * 256
                op("act", lambda E: E.activation(out=P4[s][:, hh, :], in_=pt[:, o:o + 256], func=AF.Exp, scale=sc_,
                                                 bias=nmx[:, hh:hh + 1], accum_out=rsum[:, hh:hh + 1]),
                   reads=[bpt, bx], writes=[B_P4[s], bx])
            op("dve", lambda E: E.reciprocal(out=rsum, in_=rsum), reads=[bx], writes=[bx])
            op("dve", lambda E: E.tensor_tensor(out=P4[s][:, :, :], in0=P4[s][:, :, :],
                                                in1=rsum.unsqueeze(2).to_broadcast([128, 4, 256]), op=ALU.mult),
               reads=[B_P4[s], bx], writes=[B_P4[s]])

        def x_s2(s):
            k = s % 2
            ts_ = slice(s * 128, (s + 1) * 128)
            pT, bpT = bank()
            pTb = pT[:, :].bitcast(BF16)
            for hh in range(4):
                for mc in range(2):
                    i8 = hh * 2 + mc
                    op("pe", lambda E: E.transpose(out=pTb[:, i8 * 128:(i8 + 1) * 128], in_=P4[s][:, hh, mc * 128:(mc + 1) * 128],
                                                   identity=ident_b), reads=[B_P4[s], B_const], writes=[bpT])
            op("act", lambda E: E.activation(out=PT[k][:, :, :], in_=pTb[:, :].rearrange("p (a b) -> p a b", b=128), func=AF.Copy),
               reads=[bpT], writes=[B_PT[k]])
            pO = [bank(), bank()]
            for ch in range(8):
                hh = ch // 2
                pt, bpt = pO[ch // 4]
                o = (ch % 4) * 128
                for mc in range(2):
                    op("pe", lambda E: E.matmul(out=pt[:, o:o + 128], lhsT=Vx[:, mc, ch * 128:(ch + 1) * 128],
                                                rhs=PT[k][:, hh * 2 + mc, :], start=(mc == 0), stop=(mc == 1)),
                       reads=[B_Vx, B_PT[k]], writes=[bpt])
            for i2 in range(2):
                pt, bpt = pO[i2]
                eng = "act" if i2 else "dve"
                wr = [B_at[4 * i2 + q][s] for q in range(4)]
                if eng == "act":
                    op("act", lambda E: E.activation(out=attnT[:, 4 * i2:4 * i2 + 4, ts_],
                                                     in_=pt[:, :].rearrange("p (a b) -> p a b", b=128), func=AF.Copy),
                       reads=[bpt], writes=wr)
                else:
                    op("dve", lambda E: E.tensor_copy(out=attnT[:, 4 * i2:4 * i2 + 4, ts_],
                                                      in_=pt[:, :].rearrange("p (a b) -> p a b", b=128)),
                       reads=[bpt], writes=wr)

        for s in range(NSUB):
            x_s1(s)
        for s in range(NSUB):
            x_s2(s)
        linear_tm(dict(name="wo", w=wd["xattn_wo"]), 8, D, attnT, lambda j, s: [B_at[j][s]], None,
                  group_consume=lambda grp, acc: postnorm_group(grp, acc, h_t, h_bufs, 2, False, True))

    if need_x:
        xattn_setup()

    def load_x(it):
        h_t, h_bufs, h_sem = hbuf[it % 2]
        op("pool", lambda E: E.dma_start(out=h_t[:, :, :],
                                         in_=x_d[it * TT:(it + 1) * TT, :].rearrange("(s p) d -> p s d", p=128)),
           writes=h_bufs, dma_sem=h_sem)

    def store_out(it):
        h_t, h_bufs, h_sem = hbuf[it % 2]
        op("pool", lambda E: E.dma_start(out=out_d[it * TT:(it + 1) * TT, :].rearrange("(s p) d -> p s d", p=128),
                                         in_=h_t[:, :, :]),
           reads=h_bufs, dma_sem=h_sem)

    stage_list = [st_ for st_ in ("ffn1", "mix", "xattn", "ffn2") if st_ in stages]
    hoist = len(stage_list) == 4
    hoisted = set()
    load_x(0)
    for it in range(ntile):
        h_t, h_bufs, h_sem = hbuf[it % 2]
        pre_rs.clear()

        def deferred(it=it):
            if it >= 1:
                store_out(it - 1)
            if it >= 1 and it + 1 < ntile:
                load_x(it + 1)

        for si, st_ in enumerate(stage_list):
            nxt = si + 1 < len(stage_list)
            hook = deferred if si == 0 else None
            if st_ == "ffn1":
                ffn("ffn1", h_t, h_bufs, PV_PRE + 0, 0, nxt, after_prenorm=hook, skip_prenorm=(it in hoisted))
            elif st_ == "mix":
                mixer(h_t, h_bufs, after_prenorm=hook)
            elif st_ == "xattn":
                xattn(h_t, h_bufs, after_prenorm=hook)
            else:
                eh = mh = None
                if hoist and it >= 1 and it + 1 < ntile and si == len(stage_list) - 1:
                    nt, nb, _ = hbuf[(it + 1) % 2]
                    eh = lambda nt=nt, nb=nb: prenorm_stats(nt, nb, list(range(NSUB)))
                    mh = lambda nt=nt, nb=nb: prenorm(nt, nb, PV_PRE + 0)
                    hoisted.add(it + 1)
                ffn("ffn2", h_t, h_bufs, PV_PRE + 24, 3, nxt, after_prenorm=hook, early_hook=eh, mid_hook=mh)
        flush_store()
        if it == 0 and ntile > 1:
            load_x(1)
    store_out(ntile - 1)
    kb.wait_all("pool", hbuf[0][1] + hbuf[1][1])
    return nc, kb


def _pack_params(inp):
    def col(v):
        v = np.asarray(v, np.float32).reshape(-1)
        return v.reshape(-1, 128).T
    pvec = np.zeros((128, PV_N), np.float32)
    for i, nm in enumerate(("ffn1_pre_g", "mix_pre_g", "xattn_pre_g", "ffn2_pre_g", "mem_norm_g")):
        pvec[:, PV_PRE + 8 * i:PV_PRE + 8 * i + 8] = col(inp[nm][0])
    cw = np.asarray(inp["conv_w"][0], np.float32)
    pvec[:, PV_CW:PV_CW + 4 * CK] = cw.T.reshape(4, 128, CK).transpose(1, 0, 2).reshape(128, 4 * CK)
    pvec[:, PV_CB:PV_CB + 4] = col(inp["conv_b"][0])
    pvec[:, PV_LG:PV_LG + 4] = col(inp["conv_ln_g"][0])
    pvec[:, PV_LB:PV_LB + 4] = col(inp["conv_ln_b"][0])
    qw = np.asarray(inp["qk_conv_w"][0], np.float32)
    pvec[:, PV_QW:PV_QW + 32] = qw.T.reshape(8, 128, QK).transpose(1, 0, 2).reshape(128, 32)
    pvec[:, PV_QB:PV_QB + 8] = col(inp["qk_conv_b"][0])
    pvec[0:4, PV_BI] = np.asarray(inp["b_igate"][0], np.float32)
    pvec[0:4, PV_BF] = np.asarray(inp["b_fgate"][0], np.float32)
    rgain = np.zeros((128, RG_N), np.float32)
    for i, nm in enumerate(("ffn1_post_g", "mix_post_g", "xattn_post_g", "ffn2_post_g")):
        rgain[:, RG_POST + i * D:RG_POST + (i + 1) * D] = np.asarray(inp[nm][0], np.float32)[None, :]
    rgain[:, RG_M:RG_M + MW] = np.asarray(inp["mlstm_norm_g"][0], np.float32)[None, :]
    consts = np.zeros((128, 1032), np.float32)
    consts[:, 512] = -0.5
    for hh in range(4):
        consts[hh, 520 + hh * 128:520 + (hh + 1) * 128] = 1.0
    consts[:, 0:128] = np.eye(128, dtype=np.float32)
    consts[:, 128:256] = np.triu(np.ones((128, 128), np.float32))
    consts[:, 256:384] = 1.0
    for hh in range(4):
        consts[hh, 384 + hh * 32:384 + (hh + 1) * 32] = 1.0
    return pvec, rgain, consts


_WNAMES = ("ffn1_w_gate", "ffn1_w_up", "ffn1_w_down", "w_in", "w_out", "xattn_wq", "xattn_wk",
           "xattn_wv", "xattn_wo", "ffn2_w_gate", "ffn2_w_up", "ffn2_w_down")


def make_in_maps(inp, ncores, seq):
    pvec, rgain, consts = _pack_params(inp)
    shared = {nm: np.ascontiguousarray(np.asarray(inp[nm], np.float32)[0]) for nm in _WNAMES}
    shared.update(pvec=pvec, rgain=rgain, consts=consts)
    maps = []
    for c in range(ncores):
        m = dict(shared)
        m["x"] = np.ascontiguousarray(np.asarray(inp["x"], np.float32)[c, :seq])
        m["mem"] = np.ascontiguousarray(np.asarray(inp["mem"], np.float32)[c])
        maps.append(m)
    return maps


def kernel(**inputs):
    nc, kb = build_program(SEQ)
    maps = make_in_maps(inputs, NCORES, SEQ)
    res = run_bass_kernel_spmd(nc, maps, core_ids=list(range(NCORES)))
    return np.stack([np.asarray(r["out"], np.float32) for r in res.results], axis=0)
```

```python
import math
import numpy as np
import ml_dtypes
import concourse.bass as bass
import concourse.mybir as mybir
from concourse.bass_utils import run_bass_kernel_spmd

F32 = mybir.dt.float32
BF16 = mybir.dt.bfloat16
AF = mybir.ActivationFunctionType
ALU = mybir.AluOpType
AX = mybir.AxisListType

D = 1024
DFF = 2816
NMEM = 256
CCH = 512
CK = 31
MH = 4
MW = 512
QK = 4
INC = 3080
XH = 4
XD = 256
EPS = 1e-6
TT = 512
NSUB = TT // 128
NCORES = 8
SEQ = 8192

PV_PRE = 0
PV_CW = 40
PV_CB = PV_CW + 4 * CK
PV_LG = PV_CB + 4
PV_LB = PV_LG + 4
PV_QW = PV_LB + 4
PV_QB = PV_QW + 32
PV_BI = PV_QB + 8
PV_BF = PV_BI + 1
PV_N = PV_BF + 1

RG_POST = 0
RG_M = 4 * D
RG_N = RG_M + MW


class Buf:
    __slots__ = ("name", "w", "r", "al", "cw")

    def __init__(self, name):
        self.name = name
        self.w = None
        self.r = {}
        self.al = ()
        self.cw = None


def alias_groups(ga, gb):
    for a in ga:
        a.al = tuple(a.al) + tuple(gb)
    for b in gb:
        b.al = tuple(b.al) + tuple(ga)


class KB:
    def __init__(self, nc):
        self.nc = nc
        self.engs = {"pe": nc.tensor, "act": nc.scalar, "dve": nc.vector,
                     "pool": nc.gpsimd, "sp": nc.sync}
        self.sems = {}
        self.cnt = {}
        self.known = {e: {} for e in self.engs}
        for e in ("pe", "act", "dve", "pool"):
            self.newsem(e)
        self.nins = 0
        self.nwait = 0

    def newsem(self, name):
        self.sems[name] = self.nc.alloc_semaphore("s_" + name)
        self.cnt[name] = 0
        return name

    def op(self, eng, fn, reads=(), writes=(), dma_sem=None):
        deps = {}

        def add(tag):
            if tag is None:
                return
            k, v = tag
            if deps.get(k, 0) < v:
                deps[k] = v

        for b in reads:
            add(b.w)
        for b0 in writes:
            for b in (b0,) + tuple(b0.al):
                if not (b.w is not None and b.w[0] == eng and dma_sem is None):
                    add(b.w)
                for k, v in b.r.items():
                    if k == eng and dma_sem is None:
                        continue
                    add((k, v))
        kn = self.known[eng]
        waits = []
        for k, v in deps.items():
            if eng == "pe" and k == "pe" and dma_sem is None:
                continue
            if kn.get(k, 0) < v:
                waits.append((k, v))
                kn[k] = v
        E = self.engs[eng]
        for k, v in waits[1:]:
            E.wait_ge(self.sems[k], v)
            self.nwait += 1
        ins = fn(E)
        if waits:
            k, v = waits[0]
            ins._wait_ge(self.sems[k], v)
        if dma_sem is None:
            key = eng
            self.cnt[key] += 1
            ins.then_inc(self.sems[key], 1)
        else:
            key = dma_sem
            self.cnt[key] += 16
            ins.then_inc(self.sems[key], 16)
        tag = (key, self.cnt[key])
        for b in reads:
            if b.r.get(key, 0) < tag[1]:
                b.r[key] = tag[1]
        for b0 in writes:
            b0.w = tag
            b0.r = {}
            for b in b0.al:
                b.w = tag
                b.r = {}
        self.nins += 1
        return ins

    def wait_all(self, eng, bufs):
        deps = {}
        for b in bufs:
            for tag in [b.w] + list(b.r.items()):
                if tag is None:
                    continue
                k, v = tag
                if deps.get(k, 0) < v:
                    deps[k] = v
        kn = self.known[eng]
        for k, v in deps.items():
            if kn.get(k, 0) < v:
                self.engs[eng].wait_ge(self.sems[k], v)
                kn[k] = v


def build_program(seq=SEQ, stages=("ffn1", "mix", "xattn", "ffn2")):
    assert seq % TT == 0
    ntile = seq // TT
    nc = bass.Bass("TRN2", target_bir_lowering=False)
    kb = KB(nc)
    op = kb.op

    def dram_in(name, shape, dt=F32):
        return nc.dram_tensor(name, list(shape), dt, kind="ExternalInput").ap()

    x_d = dram_in("x", [seq, D])
    mem_d = dram_in("mem", [NMEM, D])
    pv_d = dram_in("pvec", [128, PV_N])
    rg_d = dram_in("rgain", [128, RG_N])
    cst_d = dram_in("consts", [128, 1032])
    wd = {}
    for nm, shp in (("ffn1_w_gate", [D, DFF]), ("ffn1_w_up", [D, DFF]), ("ffn1_w_down", [DFF, D]),
                    ("w_in", [D, INC]), ("w_out", [D, D]),
                    ("xattn_wq", [D, D]), ("xattn_wk", [D, D]), ("xattn_wv", [D, D]), ("xattn_wo", [D, D]),
                    ("ffn2_w_gate", [D, DFF]), ("ffn2_w_up", [D, DFF]), ("ffn2_w_down", [DFF, D])):
        wd[nm] = dram_in(nm, shp)
    out_d = nc.dram_tensor("out", [seq, D], F32, kind="ExternalOutput").ap()

    def scratch(name, shape):
        return nc.dram_tensor(name, list(shape), BF16, kind="Internal").ap()

    sc = {}
    for f in ("ffn1", "ffn2"):
        sc[f + "_g"] = scratch(f + "_sg", [11, 128, 8, 256])
        sc[f + "_u"] = scratch(f + "_su", [11, 128, 8, 256])
        sc[f + "_d"] = scratch(f + "_sd", [22, 128, D])
    sc["in_fm"] = scratch("s_in_fm", [8, 128, 8, 256])
    sc["in_v"] = scratch("s_in_v", [8, 128, 512])
    sc["in_o"] = scratch("s_in_o", [8, 128, 512])
    sc["w_out"] = scratch("s_w_out", [8, 128, D])
    sc["wq"] = scratch("s_wq", [4, 128, 8, 256])
    sc["wk"] = scratch("s_wk", [4, 128, 8, 256])
    sc["wv"] = scratch("s_wv", [8, 128, D])
    sc["wo"] = scratch("s_wo", [8, 128, D])
    sc["qdiag"] = scratch("s_qdiag", [2, 128, 16, 128])
    sc["cdiag"] = scratch("s_cdiag", [8, 128, 16, 128])

    def sb(name, shape, dt):
        return nc.alloc_sbuf_tensor(name, list(shape), dt)

    pv = sb("pv", [128, PV_N], F32)
    rg = sb("rg", [128, RG_N], F32)
    cst = sb("cst", [128, 1032], F32)
    cstb = sb("cstb", [128, 512], BF16)
    B_const = Buf("const")
    ident_b = cstb[:, 0:128]
    mask_f = cst[:, 128:256]
    ident_f = cst[:, 0:128]
    ones_b = cstb[:, 256:384]
    sel_f = cst[:, 384:512]
    mhalf = cst[:, 512:513]
    cwh = sb("cwh", [128, 4 * CK], F32)

    NSLOT = 6
    ring = []
    for i in range(NSLOT):
        t = sb(f"wr{i}", [128, 2048], BF16)
        ring.append((t, Buf(f"wr{i}"), kb.newsem(f"wr{i}")))
    ring_i = [0]

    converted = {}
    stg_i = [0]
    pending_store = []
    jit_state = {}

    def flush_store():
        while pending_store:
            pending_store.pop(0)()

    def fetch(kind, name, idx, w_ap=None, col0=0, W=256, n=1, gcol=None, store=True):
        t, b, sm_ = ring[ring_i[0] % NSLOT]
        ring_i[0] += 1
        key = (name, idx)
        if kind == "cu":
            piece = sc[name][idx].rearrange("p k c -> p (k c)")
            ncols = 2048
        elif kind == "nat":
            piece = sc[name][idx:idx + n].rearrange("j p c -> p j c")
            ncols = n * W
        else:
            piece = sc[name][idx].rearrange("p m c -> p (m c)")
            ncols = 2048
        if key in converted:
            flush_store()
            op("sp", lambda E: E.dma_start(out=t[:, 0:ncols], in_=piece), reads=[converted[key]], writes=[b], dma_sem=sm_)
            return t, b
        bsc = Buf("sc_%s_%d" % (name, idx))
        converted[key] = bsc
        if kind == "diag":
            for mm in range(16):
                m = idx * 16 + mm
                wsrc, wn = (cwh, 4 * CK) if name == "cdiag" else (pv[:, PV_QW:PV_QW + 32], 32)
                if m < wn:
                    op("dve", lambda E: E.tensor_scalar(out=t[:, mm * 128:(mm + 1) * 128], in0=ident_f,
                                                        scalar1=wsrc[:, m:m + 1], scalar2=None, op0=ALU.mult),
                       reads=[B_const], writes=[b])
                else:
                    op("dve", lambda E: E.memset(t[:, mm * 128:(mm + 1) * 128], 0.0), writes=[b])
        else:
            i = stg_i[0] % 2
            stg_i[0] += 1
            stg_t, stg_b2, stg_s = jit_state["stg"][i]
            if kind == "cu":
                kk, cc = 8, 256
                src = w_ap[:, col0 + idx * 256:col0 + (idx + 1) * 256].rearrange("(k p) c -> p k c", p=128)
                g0 = 0
            else:
                kk, cc = n, W
                src = w_ap[idx * 128:(idx + n) * 128, col0:col0 + W].rearrange("(j p) c -> p j c", p=128)
                g0 = idx
            sview = stg_t[:, 0:kk * cc].rearrange("p (k c) -> p k c", c=cc)
            tview = t[:, 0:kk * cc].rearrange("p (k c) -> p k c", c=cc)
            op("sp", lambda E: E.dma_start(out=sview, in_=src), writes=[stg_b2], dma_sem=stg_s)
            ce = ("dve", "pool")[stg_i[0] % 2]
            if gcol is not None:
                gv = pv[:, gcol + g0:gcol + g0 + kk].unsqueeze(2).to_broadcast([128, kk, cc])
                op(ce, lambda E: E.tensor_tensor(out=tview, in0=sview, in1=gv, op=ALU.mult),
                   reads=[stg_b2, B_const], writes=[b])
            else:
                ce = ("act", "dve", "pool")[stg_i[0] % 3]
                if ce == "act":
                    op("act", lambda E: E.activation(out=tview, in_=sview, func=AF.Copy), reads=[stg_b2], writes=[b])
                else:
                    op(ce, lambda E: E.tensor_copy(out=tview, in_=sview), reads=[stg_b2], writes=[b])
        flush_store()
        if store:
            pending_store.append(lambda: op("sp", lambda E: E.dma_start(out=piece, in_=t[:, 0:ncols]),
                                            reads=[b], writes=[bsc], dma_sem=sm_))
        return t, b

    banks = []
    for i in range(8):
        t = nc.alloc_psum_tensor(f"ps{i}", [128, 512], F32)
        banks.append((t, Buf(f"ps{i}")))
    bank_i = [0]

    def bank():
        r = banks[bank_i[0] % 8]
        bank_i[0] += 1
        return r

    hbuf = []
    for i in range(2):
        t = sb(f"h{i}", [128, NSUB, D], F32)
        hbuf.append((t, [Buf(f"h{i}_{s}") for s in range(NSUB)], kb.newsem(f"h{i}")))
    xnT = sb("xnT", [128, 8, TT], BF16)
    B_xnT = [Buf(f"xnT{s}") for s in range(NSUB)]
    _stg = []
    for i in range(2):
        bst = Buf(f"stg{i}")
        alias_groups([bst], hbuf[1][1][2 * i:2 * i + 2])
        _stg.append((hbuf[1][0][:, 2 * i:2 * i + 2, :].rearrange("p a b -> p (a b)"), bst, kb.newsem(f"stg{i}")))
    jit_state["stg"] = _stg
    def carve(arena, off, shape, dt):
        n = 1
        for d_ in shape[1:]:
            n *= d_
        nb = n * (2 if dt == BF16 else 4)
        assert off % 4 == 0 and nb % 4 == 0
        a = arena[:, off // 4:(off + nb) // 4]
        if dt == BF16:
            a = a.bitcast(BF16)
        if len(shape) == 3:
            a = a.rearrange("p (a b) -> p a b", b=shape[2])
        elif len(shape) == 4:
            a = a.rearrange("p (a b c) -> p a b c", b=shape[2], c=shape[3])
        return a

    arA = sb("arenaA", [128, 29184 // 4], F32)
    arB = sb("arenaB", [128, 32768 // 4], F32)
    arC = sb("arenaC", [128, 31360 // 4], F32)
    hid = carve(arA, 0, [128, 22, TT], BF16)
    B_hid = [Buf(f"hid{j}") for j in range(22)]
    sg = carve(arA, 22528, [128, 2, TT], F32)
    B_sg = [Buf("sg0"), Buf("sg1")]
    zq = carve(arA, 0, [128, 8, 516], BF16)
    B_zq = [Buf(f"zq{g}") for g in range(8)]
    ubf = carve(arA, 16480, [128, 4, 512], F32)
    ub = carve(arA, 24672, [128, 4, 544], BF16)
    B_ubf = [Buf(f"ubf{g}") for g in range(4)]
    B_ub = [Buf(f"ub{g}") for g in range(4)]
    alias_groups(B_hid + B_sg, B_zq + B_ub + B_ubf)
    xs = sb("xs", [128, 2, D], BF16)
    B_xs = [Buf("xs0"), Buf("xs1")]
    junk = sb("junk", [128, 2, D], F32)
    B_jh = [[Buf("junk00"), Buf("junk01")], [Buf("junk10"), Buf("junk11")]]
    B_junk = [B_jh[0], B_jh[1]]
    yconv = carve(arB, 0, [128, 4, TT], F32)
    ybf = carve(arB, 8192, [128, 4, TT], BF16)
    ysq = carve(arB, 12288, [128, 4, TT], BF16)
    mixT = carve(arB, 16384, [128, 8, TT], BF16)
    qkT = carve(arB, 24576, [128, 8, TT], BF16)
    B_yc = [Buf(f"yc{g}") for g in range(4)]
    B_ybf = [Buf(f"ybf{g}") for g in range(4)]
    B_mixT = [[Buf(f"mixT{j}_{q}") for q in range(NSUB)] for j in range(8)]
    B_qk = [Buf(f"qk{g}") for g in range(8)]
    vaug = carve(arC, 0, [128, NSUB, 4, 129], BF16)
    og = carve(arC, 4128, [128, NSUB, 512], BF16)
    ktm = carve(arC, 8224, [128, NSUB, 4, 128], BF16)
    hn = carve(arC, 12320, [128, 1, 512], F32)
    hg = carve(arC, 14368, [128, 2, 512], BF16)
    WT = carve(arC, 16416, [128, 2, 512], BF16)
    gE = carve(arC, 18976, [128, 512], F32)
    gNB = carve(arC, 21024, [128, 512], F32)
    gA = carve(arC, 23072, [128, 512], F32)
    gM = carve(arC, 25120, [128, 512], F32)
    lnm = carve(arC, 27168, [128, 512], F32)
    lnr = carve(arC, 29216, [128, 512], F32)
    B_v = [Buf(f"v{q}") for q in range(NSUB)]
    B_og = [Buf(f"og{q}") for q in range(NSUB)]
    B_ktm = [Buf(f"ktm{q}") for q in range(NSUB)]
    B_hn = [Buf("hn0"), Buf("hn1")]
    B_hg = [Buf("hg0"), Buf("hg1")]
    B_WT = [Buf("WT0"), Buf("WT1")]
    B_gate = Buf("gate")
    B_ln = Buf("ln")
    B_lnr = Buf("lnr")
    qxT = carve(arC, 0, [128, 8, TT], BF16)
    attnT = carve(arC, 8192, [128, 8, TT], BF16)
    Pm = [carve(arC, 16384, [128, 4, 256], BF16), carve(arC, 26624, [128, 4, 256], BF16)]
    Pn = [carve(arC, 18432, [128, 4, 256], BF16), carve(arC, 28672, [128, 4, 256], BF16)]
    PT = [carve(arC, 20480, [128, 8, 128], BF16), carve(arC, 22528, [128, 8, 128], BF16)]
    memnT = carve(arC, 22528, [128, 8, 256], BF16)
    B_qx = [Buf(f"qx{j}") for j in range(8)]
    B_at = [[Buf(f"at{j}_{q}") for q in range(NSUB)] for j in range(8)]
    B_P = [Buf("P0"), Buf("P1")]
    B_Pn = [Buf("Pn0"), Buf("Pn1")]
    B_PT = [Buf("PT0"), Buf("PT1")]
    B_memn = [Buf("memn0"), Buf("memn1")]
    B_xst4 = [Buf(f"xst{q}") for q in range(NSUB)]
    C_mix = B_v + B_og + B_ktm + B_hn + B_hg + B_WT + [B_gate, B_ln, B_lnr]
    C_x = B_qx + [b for r_ in B_at for b in r_] + B_P + B_Pn + B_PT + B_memn
    alias_groups(C_mix, C_x)
    alias_groups([B_PT[1]], B_memn)
    Cst = sb("Cst", [128, 4, 129], F32)
    Cbf = sb("Cbf", [128, 4, 129], BF16)
    B_C = [Buf(f"C{h_}") for h_ in range(4)]
    B_Cbf = [Buf(f"Cbf{h_}") for h_ in range(4)]
    KT = sb("KT", [128, 8, 256], BF16)
    Vx = sb("Vx", [128, 2, D], BF16)
    B_KT = Buf("KT")
    B_Vx = Buf("Vx")
    zq_halo = sb("zq_halo", [128, 8, 4], BF16)
    u_halo = sb("u_halo", [128, 4, 30], BF16)
    B_zqh = [Buf(f"zqh{g}") for g in range(8)]
    B_uh = [Buf(f"uh{g}") for g in range(4)]
    wif_f = sb("wif_f", [128, 8, 8], F32)
    wif = sb("wif", [128, 8, 8], BF16)
    wfT = sb("wfT", [128, 32], F32)
    B_wfT = Buf("wfT")
    decB = sb("decB", [128, 16], F32)
    B_dec = Buf("decB")
    sm = sb("small", [128, 64], F32)
    B_sm = Buf("small")
    jdum = sb("jdum", [128, D], BF16)
    B_jdum = Buf("jdum")
    stat4 = sb("stat4", [128, 16], F32)
    B_stat4 = [Buf("stat4_0"), Buf("stat4_1")]
    junkb = sb("junkb", [128, 128], BF16)
    B_junkb = Buf("junkb")
    stat = sb("stat", [128, 64], F32)
    B_stat = [Buf(f"st{i}") for i in range(64)]
    stat_i = [0]

    def st():
        i = stat_i[0] % 64
        stat_i[0] += 1
        return stat[:, i:i + 1], B_stat[i]

    setup_sem = kb.newsem("setup")
    op("sp", lambda E: E.dma_start(out=pv[:, :], in_=pv_d), writes=[B_const], dma_sem=setup_sem)
    op("sp", lambda E: E.dma_start(out=rg[:, :], in_=rg_d), writes=[B_const], dma_sem=setup_sem)
    op("sp", lambda E: E.dma_start(out=cst[:, :], in_=cst_d), writes=[B_const], dma_sem=setup_sem)
    op("dve", lambda E: E.tensor_copy(out=cstb[:, :], in_=cst[:, 0:512]), reads=[B_const], writes=[B_const])
    op("dve", lambda E: E.tensor_scalar_mul(out=cwh[:, :], in0=pv[:, PV_CW:PV_CW + 4 * CK], scalar1=0.5),
       reads=[B_const], writes=[B_const])

    need_ffn1 = "ffn1" in stages
    need_ffn2 = "ffn2" in stages
    need_mix = "mix" in stages
    need_x = "xattn" in stages
    pre_rs = {}

    def prenorm_stats(h_t, h_bufs, subs):
        tmp = {}
        for s in subs:
            k = s % 2
            ss, bss = st()
            op("act", lambda E: E.activation(out=jdum[:, :], in_=h_t[:, s, :], func=AF.Square, accum_out=ss),
               reads=[h_bufs[s]], writes=[B_jdum, bss])
            tmp[s] = (ss, bss)
        for s in subs:
            ss, bss = tmp[s]
            rs, brs = st()
            op("dve", lambda E: E.tensor_scalar(out=rs, in0=ss, scalar1=D * EPS, scalar2=None, op0=ALU.add),
               reads=[bss], writes=[brs])
            pre_rs[s] = (rs, brs)
        for s in subs:
            rs, brs = pre_rs[s]
            op("pool", lambda E: E.tensor_tensor(out=rs, in0=rs, in1=mhalf, op=ALU.pow),
               reads=[brs, B_const], writes=[brs])

    def prenorm(h_t, h_bufs, gcol, nsub=NSUB, dst=None, dst_bufs=None):
        dst = xnT if dst is None else dst
        dst_bufs = B_xnT if dst_bufs is None else dst_bufs
        todo = [s for s in range(nsub) if s not in pre_rs]
        if todo:
            prenorm_stats(h_t, h_bufs, todo)
        for w in range(0, nsub, 2):
            subs = list(range(w, min(w + 2, nsub)))
            pts = {}
            for s in subs:
                k = s % 2
                rs, brs = pre_rs.pop(s)
                op("dve", lambda E: E.tensor_scalar(out=xs[:, k, :], in0=h_t[:, s, :], scalar1=rs, scalar2=math.sqrt(D),
                                                    op0=ALU.mult, op1=ALU.mult), reads=[h_bufs[s], brs], writes=[B_xs[k]])
            for s in subs:
                k = s % 2
                pt, bpt = bank()
                ptb = pt[:, :].bitcast(BF16)
                pts[s] = (ptb, bpt)
                for kc in range(8):
                    op("pe", lambda E: E.transpose(out=ptb[:, kc * 128:(kc + 1) * 128],
                                                   in_=xs[:, k, kc * 128:(kc + 1) * 128], identity=ident_b),
                       reads=[B_xs[k], B_const], writes=[bpt])
            for s in subs:
                ptb, bpt = pts[s]
                dview = dst[:, :, s * 128:(s + 1) * 128]
                pview = ptb[:, :].rearrange("p (k c) -> p k c", c=128)
                if s % 2:
                    op("act", lambda E: E.activation(out=dview, in_=pview, func=AF.Copy), reads=[bpt], writes=[dst_bufs[s]])
                else:
                    op("dve", lambda E: E.tensor_copy(out=dview, in_=pview), reads=[bpt], writes=[dst_bufs[s]])

    def postnorm_group(grp, acc, h_t, h_bufs, gidx, half_scale, next_pre):
        cfac = math.sqrt(D) * (0.5 if half_scale else 1.0)
        g0 = rg[:, RG_POST + gidx * D:RG_POST + gidx * D + 512]
        g1 = rg[:, RG_POST + gidx * D + 512:RG_POST + (gidx + 1) * D]
        tmp = {}
        for s in grp:
            (p0, b0), (p1, b1) = acc[s]
            ss0, bs0 = st()
            ss1, bs1 = st()
            op("act", lambda E: E.activation(out=jdum[:, 0:512], in_=p0[:, :], func=AF.Square, accum_out=ss0),
               reads=[b0], writes=[B_jdum, bs0])
            op("act", lambda E: E.activation(out=jdum[:, 0:512], in_=p1[:, :], func=AF.Square, accum_out=ss1),
               reads=[b1], writes=[B_jdum, bs1])
            tmp[s] = (ss0, bs0, ss1, bs1)
        rss = {}
        for s in grp:
            ss0, bs0, ss1, bs1 = tmp[s]
            rs, brs = st()
            op("dve", lambda E: E.tensor_scalar(out=rs, in0=ss0, scalar1=ss1, scalar2=None, op0=ALU.add),
               reads=[bs0, bs1], writes=[brs])
            op("dve", lambda E: E.tensor_scalar(out=rs, in0=rs, scalar1=D * EPS, scalar2=1.0 / (cfac * cfac),
                                                op0=ALU.add, op1=ALU.mult), reads=[brs], writes=[brs])
            rss[s] = (rs, brs)
        for s in grp:
            rs, brs = rss[s]
            op("pool", lambda E: E.tensor_tensor(out=rs, in0=rs, in1=mhalf, op=ALU.pow),
               reads=[brs, B_const], writes=[brs])
        for s in grp:
            k = s % 2
            rs, brs = rss[s]
            (p0, b0), (p1, b1) = acc[s]
            for (p, b, g, lo) in ((p0, b0, g0, 0), (p1, b1, g1, 512)):
                bj = B_jh[k][lo // 512]
                op("dve", lambda E: E.scalar_tensor_tensor(out=junk[:, k, lo:lo + 512], in0=p[:, :], scalar=rs, in1=g,
                                                           op0=ALU.mult, op1=ALU.mult),
                   reads=[b, brs, B_const], writes=[bj])
                op("dve", lambda E: E.tensor_tensor(out=h_t[:, s, lo:lo + 512], in0=h_t[:, s, lo:lo + 512],
                                                    in1=junk[:, k, lo:lo + 512], op=ALU.add),
                   reads=[bj, h_bufs[s]], writes=[h_bufs[s]])
        if next_pre:
            prenorm_stats(h_t, h_bufs, list(grp))

    def linear_fm(wspec, nunits, src, src_bufs, consume, ntok=TT):
        for u in range(nunits):
            wt, wb = fetch("cu", wspec["name"], u, w_ap=wspec["w"], col0=wspec.get("col0", 0),
                           gcol=wspec.get("gcol"), store=wspec.get("store", True))
            for cc in range(2):
                c = 2 * u + cc
                pt, bpt = bank()
                for kc in range(8):
                    op("pe", lambda E: E.matmul(out=pt[:, 0:ntok], lhsT=wt[:, kc * 256 + cc * 128:kc * 256 + cc * 128 + 128],
                                                rhs=src[:, kc, :], start=(kc == 0), stop=(kc == 7)),
                       reads=[wb] + list(src_bufs), writes=[bpt])
                consume(c, pt, bpt)

    def linear_tm(wspec, nj, W, lhs, lhs_bufs_fn, consume, subs_groups=((0, 1), (2, 3)), group_consume=None):
        per = 2048 // W
        nh = W // 512
        for grp in subs_groups:
            acc = {s: [bank() for _ in range(nh)] for s in grp}
            for j0 in range(0, nj, per):
                n = min(per, nj - j0)
                wt, wb = fetch("nat", wspec["name"], j0, w_ap=wspec["w"], col0=wspec.get("col0", 0), W=W, n=n,
                               gcol=wspec.get("gcol"), store=wspec.get("store", True))
                for jj in range(n):
                    j = j0 + jj
                    for s in grp:
                        for hh in range(nh):
                            pt, bpt = acc[s][hh]
                            op("pe", lambda E: E.matmul(out=pt[:, :], lhsT=lhs[:, j, s * 128:(s + 1) * 128],
                                                        rhs=wt[:, jj * W + hh * 512:jj * W + hh * 512 + 512],
                                                        start=(j == 0), stop=(j == nj - 1)),
                               reads=[wb] + lhs_bufs_fn(j, s), writes=[bpt])
            if group_consume is not None:
                group_consume(grp, acc)
            else:
                for s in grp:
                    consume(s, acc[s])

    def ffn(f, h_t, h_bufs, pre_col, post_idx, next_pre, after_prenorm=None, skip_prenorm=False,
            early_hook=None, mid_hook=None):
        if not skip_prenorm:
            prenorm(h_t, h_bufs, pre_col)
        if after_prenorm is not None:
            after_prenorm()
        for u in range(11):
            if u == 2 and early_hook is not None:
                early_hook()
            wg, bg = fetch("cu", f + "_g", u, w_ap=wd[f + "_w_gate"], gcol=pre_col)
            wu, bu = fetch("cu", f + "_u", u, w_ap=wd[f + "_w_up"], gcol=pre_col)
            for cc in range(2):
                j = 2 * u + cc
                pg, bpg = bank()
                pu, bpu = bank()
                for (wt, wb, pt, bpt) in ((wg, bg, pg, bpg), (wu, bu, pu, bpu)):
                    for kc in range(8):
                        op("pe", lambda E: E.matmul(out=pt[:, :], lhsT=wt[:, kc * 256 + cc * 128:kc * 256 + cc * 128 + 128],
                                                    rhs=xnT[:, kc, :], start=(kc == 0), stop=(kc == 7)),
                           reads=[wb] + B_xnT, writes=[bpt])
                k = j % 2
                op("act", lambda E: E.activation(out=sg[:, k, :], in_=pg[:, :], func=AF.Silu),
                   reads=[bpg], writes=[B_sg[k]])
                op("dve", lambda E: E.tensor_tensor(out=hid[:, j, :], in0=sg[:, k, :], in1=pu[:, :], op=ALU.mult),
                   reads=[B_sg[k], bpu], writes=[B_hid[j]])
        if mid_hook is not None:
            mid_hook()
        linear_tm(dict(name=f + "_d", w=wd[f + "_w_down"]), 22, D, hid, lambda j, s: [B_hid[j]], None,
                  group_consume=lambda grp, acc: postnorm_group(grp, acc, h_t, h_bufs, post_idx, True, next_pre))


    LN_DK = math.log(1.0 / math.sqrt(128.0))
    MU = sm[0:4, 0:5]
    NBc = sm[0:4, 8:9]
    Mc = sm[0:4, 9:10]
    negbf = sm[0:4, 10:11]
    dtmp = sm[0:4, 12:16]
    dec = sm[0:4, 16:20]
    if need_mix:
        op("pool", lambda E: E.memset(sm[:, :], 0.0), writes=[B_sm])
        op("pool", lambda E: E.memset(sm[:, 11:12], LN_DK), writes=[B_sm])
        op("pool", lambda E: E.memset(sm[:, 20:21], 1e-5), writes=[B_sm])
        op("pool", lambda E: E.memset(Cst[:, :, :], 0.0), writes=B_C)
        op("pool", lambda E: E.memset(zq_halo[:, :, :], 0.0), writes=B_zqh)
        op("pool", lambda E: E.memset(u_halo[:, :, :], 0.0), writes=B_uh)
        op("dve", lambda E: E.tensor_scalar(out=negbf, in0=pv[0:4, PV_BF:PV_BF + 1], scalar1=-1.0, scalar2=None,
                                            op0=ALU.mult), reads=[B_const, B_sm], writes=[B_sm])
        with nc.allow_non_contiguous_dma(reason="tiny gate weights"):
            op("sp", lambda E: E.dma_start(out=wif_f[:, :, :],
                                           in_=wd["w_in"][:, 3072:3080].rearrange("(k p) c -> p k c", p=128)),
               writes=[B_const], dma_sem=setup_sem)
        op("dve", lambda E: E.tensor_tensor(out=wif[:, :, :], in0=wif_f[:, :, :],
                                            in1=pv[:, PV_PRE + 8:PV_PRE + 16].unsqueeze(2).to_broadcast([128, 8, 8]),
                                            op=ALU.mult), reads=[B_const], writes=[B_const])

    def mixer(h_t, h_bufs, after_prenorm=None):
        prenorm(h_t, h_bufs, PV_PRE + 8)
        if after_prenorm is not None:
            after_prenorm()
        op("pool", lambda E: E.memset(vaug[:, :, :, 128:129], 1.0), writes=B_v)
        op("pool", lambda E: E.memset(lnm[0:4, :], 0.0), writes=[B_ln])
        def cons_v(s, acc):
            pt, bpt = acc[0]
            op("dve", lambda E: E.tensor_copy(out=vaug[:, s, :, 0:128],
                                              in_=pt[:, :].rearrange("p (a b) -> p a b", b=128)),
               reads=[bpt], writes=[B_v[s]])

        og_todo = []

        def og_finish():
            gm_ = rg[:, RG_M:RG_M + MW]
            for s in og_todo:
                op("pool", lambda E: E.tensor_tensor(out=og[:, s, :], in0=og[:, s, :], in1=gm_, op=ALU.mult),
                   reads=[B_og[s], B_const], writes=[B_og[s]])
                op("pool", lambda E: E.tensor_tensor(out=og[:, s, :], in0=og[:, s, :], in1=gm_, op=ALU.add),
                   reads=[B_og[s], B_const], writes=[B_og[s]])

        def cons_o(s, acc):
            pt, bpt = acc[0]
            op("act", lambda E: E.activation(out=og[:, s, :], in_=pt[:, :], func=AF.Tanh, scale=0.5),
               reads=[bpt], writes=[B_og[s]])
            og_todo.append(s)

        pi, bpi = bank()
        pf, bpf = bank()
        for (pt, bpt, c0) in ((pi, bpi, 0), (pf, bpf, 4)):
            for kc in range(8):
                op("pe", lambda E: E.matmul(out=pt[0:4, :], lhsT=wif[:, kc, c0:c0 + 4], rhs=xnT[:, kc, :],
                                            start=(kc == 0), stop=(kc == 7)),
                   reads=[B_const] + B_xnT, writes=[bpt])
        G = [B_gate, B_sm]
        op("act", lambda E: E.activation(out=gE[0:4, :], in_=pf[0:4, :], func=AF.Exp, scale=-1.0, bias=negbf),
           reads=[bpf, B_sm], writes=[B_gate])
        op("act", lambda E: E.activation(out=gE[0:4, :], in_=gE[0:4, :], func=AF.Ln, bias=1.0),
           reads=[B_gate], writes=[B_gate])
        op("dve", lambda E: E.tensor_tensor_scan(out=gNB[0:4, :], data0=gE[0:4, :], data1=lnm[0:4, :], initial=NBc,
                                                 op0=ALU.add, op1=ALU.add), reads=G + [B_ln], writes=[B_gate])
        op("dve", lambda E: E.scalar_tensor_tensor(out=gA[0:4, :], in0=pi[0:4, :], scalar=pv[0:4, PV_BI:PV_BI + 1],
                                                   in1=gNB[0:4, :], op0=ALU.add, op1=ALU.add),
           reads=[bpi, B_const] + G, writes=[B_gate])
        op("dve", lambda E: E.tensor_tensor_scan(out=gM[0:4, :], data0=gA[0:4, :], data1=gA[0:4, :], initial=Mc,
                                                 op0=ALU.max, op1=ALU.max), reads=G, writes=[B_gate])
        op("dve", lambda E: E.tensor_copy(out=sm[0:4, 0:1], in_=Mc), reads=G, writes=[B_sm])
        op("dve", lambda E: E.tensor_copy(out=sm[0:4, 1:5],
                                          in_=gM[0:4, :].rearrange("p (c t) -> p c t", t=128)[:, :, 127]),
           reads=G, writes=[B_sm])
        op("dve", lambda E: E.tensor_copy(out=Mc, in_=gM[0:4, 511:512]), reads=G, writes=[B_sm])
        op("dve", lambda E: E.tensor_copy(out=NBc, in_=gNB[0:4, 511:512]), reads=G, writes=[B_sm])
        mub = sm[0:4, 1:5].unsqueeze(2).to_broadcast([4, 4, 128])
        for arr in (gA, gNB):
            a3 = arr[0:4, :].rearrange("p (c t) -> p c t", t=128)
            op("dve", lambda E: E.tensor_tensor(out=a3, in0=a3, in1=mub, op=ALU.subtract), reads=G, writes=[B_gate])
        op("dve", lambda E: E.tensor_tensor(out=dtmp, in0=sm[0:4, 0:4], in1=sm[0:4, 1:5], op=ALU.subtract),
           reads=G, writes=[B_sm])
        op("act", lambda E: E.activation(out=gA[0:4, :], in_=gA[0:4, :], func=AF.Exp, bias=sm[0:4, 11:12]),
           reads=G, writes=[B_gate])
        op("act", lambda E: E.activation(out=gNB[0:4, :], in_=gNB[0:4, :], func=AF.Exp), reads=G, writes=[B_gate])
        op("act", lambda E: E.activation(out=dec, in_=dtmp, func=AF.Exp), reads=G, writes=[B_sm])
        linear_tm(dict(name="in_v", w=wd["w_in"], col0=2048, gcol=PV_PRE + 8), 8, 512, xnT, lambda j, s: [B_xnT[s]], cons_v, subs_groups=((0, 1, 2, 3),))
        linear_tm(dict(name="in_o", w=wd["w_in"], col0=2560, gcol=PV_PRE + 8), 8, 512, xnT, lambda j, s: [B_xnT[s]], cons_o, subs_groups=((0, 1, 2, 3),))
        def cons_z(c, pt, bpt):
            if c < 4:
                op("act", lambda E: E.activation(out=ubf[:, c, :], in_=pt[:, :], func=AF.Copy),
                   reads=[bpt], writes=[B_ubf[c]])
                op("pool", lambda E: E.tensor_copy(out=ub[:, c, 0:30], in_=u_halo[:, c, :]),
                   reads=[B_uh[c]], writes=[B_ub[c]])
            elif c < 8:
                g = c - 4
                k = g % 2
                op("act", lambda E: E.activation(out=junk[:, k, 0:512], in_=pt[:, :], func=AF.Tanh, scale=0.5),
                   reads=[bpt], writes=[B_jh[k][0]])
                op("dve", lambda E: E.scalar_tensor_tensor(out=ub[:, g, 30:542], in0=junk[:, k, 0:512], scalar=1.0,
                                                           in1=ubf[:, g, :], op0=ALU.add, op1=ALU.mult),
                   reads=[B_jh[k][0], B_ubf[g]], writes=[B_ub[g]])
                op("pool", lambda E: E.tensor_copy(out=u_halo[:, g, :], in_=ub[:, g, 512:542]),
                   reads=[B_ub[g]], writes=[B_uh[g]])
            else:
                gq = c - 8
                op("act", lambda E: E.activation(out=zq[:, gq, 3:515], in_=pt[:, :], func=AF.Copy),
                   reads=[bpt], writes=[B_zq[gq]])
                op("pool", lambda E: E.tensor_copy(out=zq[:, gq, 0:3], in_=zq_halo[:, gq, 0:3]),
                   reads=[B_zqh[gq]], writes=[B_zq[gq]])
                op("pool", lambda E: E.tensor_copy(out=zq_halo[:, gq, 0:3], in_=zq[:, gq, 512:515]),
                   reads=[B_zq[gq]], writes=[B_zqh[gq]])
                qpend.append(gq)
                if len(qpend) > 2:
                    emit_qconv(qpend.pop(0))

        qpend = []
        qslots = {}

        def emit_qconv(gq):
            if gq // 4 not in qslots:
                qslots[gq // 4] = fetch("diag", "qdiag", gq // 4)
            wt, wb = qslots[gq // 4]
            pt2, bpt2 = bank()
            for j in range(QK):
                m = (gq * QK + j) % 16
                op("pe", lambda E: E.matmul(out=pt2[:, :], lhsT=wt[:, m * 128:(m + 1) * 128], rhs=zq[:, gq, j:j + 512],
                                            start=(j == 0), stop=(j == QK - 1)), reads=[wb, B_zq[gq]], writes=[bpt2])
            op("act", lambda E: E.activation(out=qkT[:, gq, :], in_=pt2[:, :], func=AF.Silu,
                                             bias=pv[:, PV_QB + gq:PV_QB + gq + 1]),
               reads=[bpt2, B_const], writes=[B_qk[gq]])

        linear_fm(dict(name="in_fm", w=wd["w_in"], gcol=PV_PRE + 8), 8, xnT, B_xnT, cons_z)
        while qpend:
            emit_qconv(qpend.pop(0))
        pw, bpw = bank()
        for s in range(NSUB):
            op("pe", lambda E: E.matmul(out=pw[:, s * 8:s * 8 + 4], lhsT=gA[0:4, s * 128:(s + 1) * 128],
                                        rhs=ident_f[0:4, 0:4], start=True, stop=True),
               reads=[B_gate, B_const], writes=[bpw])
            op("pe", lambda E: E.matmul(out=pw[:, s * 8 + 4:s * 8 + 8], lhsT=gNB[0:4, s * 128:(s + 1) * 128],
                                        rhs=ident_f[0:4, 0:4], start=True, stop=True),
               reads=[B_gate, B_const], writes=[bpw])
        for hh in range(4):
            op("pe", lambda E: E.matmul(out=pw[:, 32 + hh * 4:32 + hh * 4 + 4], lhsT=cst[0:4, 520 + hh * 128:520 + (hh + 1) * 128],
                                        rhs=dec, start=True, stop=True), reads=[B_sm, B_const], writes=[bpw])
        op("dve", lambda E: E.tensor_copy(out=wfT[:, :], in_=pw[:, 0:32]), reads=[bpw], writes=[B_wfT])
        op("dve", lambda E: E.tensor_copy(out=decB[:, :], in_=pw[:, 32:48]), reads=[bpw], writes=[B_dec])

        dslots = {}
        for g in range(4):
            pt, bpt = bank()
            for j in range(CK):
                m = g * CK + j
                if m // 16 not in dslots:
                    dslots[m // 16] = fetch("diag", "cdiag", m // 16)
                wt, wb = dslots[m // 16]
                op("pe", lambda E: E.matmul(out=pt[:, :], lhsT=wt[:, (m % 16) * 128:(m % 16 + 1) * 128],
                                            rhs=ub[:, g, j:j + 512], start=(j == 0), stop=(j == CK - 1)),
                   reads=[wb, B_ub[g]], writes=[bpt])
            cb = pv[:, PV_CB + g:PV_CB + g + 1]
            op("act", lambda E: E.activation(out=yconv[:, g, :], in_=pt[:, :], func=AF.Identity, bias=cb),
               reads=[bpt, B_const], writes=[B_yc[g]])
            op("act", lambda E: E.activation(out=ybf[:, g, :], in_=pt[:, :], func=AF.Identity, bias=cb),
               reads=[bpt, B_const], writes=[B_ybf[g]])
            op("act", lambda E: E.activation(out=ysq[:, g, :], in_=pt[:, :], func=AF.Square, bias=cb),
               reads=[bpt, B_const], writes=[B_ybf[g]])

        og_finish()
        ctx = {}
        rot = [0]

        def sbank():
            r = banks[4 + rot[0] % 4]
            rot[0] += 1
            return r

        def m_p1(s):
            ts_ = slice(s * 128, (s + 1) * 128)
            k2 = s % 2
            wg_s = wfT[:, s * 8:s * 8 + 4]
            WTk = WT[:, k2, :].rearrange("p (a b) -> p a b", b=128)
            pk, bpk = sbank()
            pkb = pk[:, :].bitcast(BF16)
            for hh in range(4):
                op("pe", lambda E: E.transpose(out=pkb[:, hh * 128:(hh + 1) * 128], in_=qkT[:, 4 + hh, ts_],
                                               identity=ident_b), reads=[B_qk[4 + hh], B_const], writes=[bpk])
            for hh in range(4):
                op("act", lambda E: E.activation(out=ktm[:, s, hh, :], in_=pkb[:, hh * 128:(hh + 1) * 128], func=AF.Copy,
                                                 scale=wfT[:, s * 8 + hh:s * 8 + hh + 1]),
                   reads=[bpk, B_wfT], writes=[B_ktm[s]])
            pS, bpS = sbank()
            for hh in range(4):
                op("pe", lambda E: E.matmul(out=pS[:, hh * 128:(hh + 1) * 128], lhsT=qkT[:, 4 + hh, ts_],
                                            rhs=qkT[:, hh, ts_], start=True, stop=True),
                   reads=[B_qk[4 + hh], B_qk[hh]], writes=[bpS])
            op("dve", lambda E: E.tensor_tensor(out=WTk, in0=pS[:, :].rearrange("p (a b) -> p a b", b=128),
                                                in1=wg_s.unsqueeze(2).to_broadcast([128, 4, 128]), op=ALU.mult),
               reads=[bpS, B_wfT], writes=[B_WT[k2]])
            op("pool", lambda E: E.tensor_tensor(out=WTk, in0=WTk, in1=mask_f.unsqueeze(1).to_broadcast([128, 4, 128]),
                                                 op=ALU.mult), reads=[B_WT[k2], B_const], writes=[B_WT[k2]])

        def m_p2a(s):
            ts_ = slice(s * 128, (s + 1) * 128)
            k2 = s % 2
            dec_s = decB[:, :].rearrange("p (h c) -> p h c", c=4)[:, :, s]
            WTk = WT[:, k2, :].rearrange("p (a b) -> p a b", b=128)
            op("dve", lambda E: E.tensor_tensor(out=Cst[:, :, :], in0=Cst[:, :, :],
                                                in1=dec_s.unsqueeze(2).to_broadcast([128, 4, 129]), op=ALU.mult),
               reads=B_C + [B_dec], writes=B_C)
            op("act", lambda E: E.activation(out=Cbf[:, :, :], in_=Cst[:, :, :], func=AF.Copy), reads=B_C, writes=B_Cbf)

        def m_p2mm(s):
            ts_ = slice(s * 128, (s + 1) * 128)
            k2 = s % 2
            WTk = WT[:, k2, :].rearrange("p (a b) -> p a b", b=128)
            pNt, bpN = banks[k2]
            pXt, bpX = banks[2 + k2]
            pCt, bpC = sbank()
            for hh in range(4):
                hs = slice(hh * 128, (hh + 1) * 128)
                op("pe", lambda E: E.matmul(out=pNt[:, hs], lhsT=WTk[:, hh, :], rhs=vaug[:, s, hh, 0:128],
                                            start=True, stop=False), reads=[B_WT[k2], B_v[s]], writes=[bpN])
                op("pe", lambda E: E.matmul(out=pNt[:, hs], lhsT=qkT[:, hh, ts_], rhs=Cbf[:, hh, 0:128],
                                            start=False, stop=True), reads=[B_qk[hh], B_Cbf[hh]], writes=[bpN])
                op("pe", lambda E: E.matmul(out=pXt[:, hh:hh + 1], lhsT=WTk[:, hh, :], rhs=vaug[:, s, hh, 128:129],
                                            start=True, stop=False), reads=[B_WT[k2], B_v[s]], writes=[bpX])
                op("pe", lambda E: E.matmul(out=pXt[:, hh:hh + 1], lhsT=qkT[:, hh, ts_], rhs=Cbf[:, hh, 128:129],
                                            start=False, stop=True), reads=[B_qk[hh], B_Cbf[hh]], writes=[bpX])
            for hh in range(4):
                hs = slice(hh * 128, (hh + 1) * 128)
                op("pe", lambda E: E.matmul(out=pCt[:, hs], lhsT=ktm[:, s, hh, :], rhs=vaug[:, s, hh, 0:128],
                                            start=True, stop=True), reads=[B_ktm[s], B_v[s]], writes=[bpC])
                op("pe", lambda E: E.matmul(out=pXt[:, 4 + hh:5 + hh], lhsT=ktm[:, s, hh, :], rhs=vaug[:, s, hh, 128:129],
                                            start=True, stop=True), reads=[B_ktm[s], B_v[s]], writes=[bpX])
            ctx[s] = (pNt, bpN, pXt, bpX, pCt, bpC)

        def m_p2b(s):
            pNt, bpN, pXt, bpX, pCt, bpC = ctx[s]
            op("dve", lambda E: E.tensor_tensor(out=Cst[:, :, 0:128], in0=Cst[:, :, 0:128],
                                                in1=pCt[:, :].rearrange("p (a b) -> p a b", b=128), op=ALU.add),
               reads=B_C + [bpC], writes=B_C)
            op("dve", lambda E: E.tensor_tensor(out=Cst[:, :, 128], in0=Cst[:, :, 128], in1=pXt[:, 4:8], op=ALU.add),
               reads=B_C + [bpX], writes=B_C)

        def m_n1(s):
            k2 = s % 2
            pNt, bpN, pXt, bpX, pCt, bpC = ctx[s]
            fl_s = wfT[:, s * 8 + 4:s * 8 + 8]
            ad = stat4[:, (2 * k2) * 4:(2 * k2) * 4 + 4]
            ssq = stat4[:, (2 * k2 + 1) * 4:(2 * k2 + 1) * 4 + 4]
            bq = B_stat4[k2]
            op("act", lambda E: E.activation(out=ad, in_=pXt[:, 0:4], func=AF.Abs), reads=[bpX], writes=[bq])
            for hh in range(4):
                op("act", lambda E: E.activation(out=junkb[:, :], in_=pNt[:, hh * 128:(hh + 1) * 128], func=AF.Square,
                                                 accum_out=ssq[:, hh:hh + 1]), reads=[bpN], writes=[B_junkb, bq])
            op("dve", lambda E: E.tensor_tensor(out=ad, in0=ad, in1=fl_s, op=ALU.max), reads=[bq, B_wfT], writes=[bq])
            op("dve", lambda E: E.reciprocal(out=ad, in_=ad), reads=[bq], writes=[bq])
            op("dve", lambda E: E.tensor_tensor(out=ssq, in0=ssq, in1=ad, op=ALU.mult), reads=[bq], writes=[bq])
            op("dve", lambda E: E.tensor_tensor(out=ssq, in0=ssq, in1=ad, op=ALU.mult), reads=[bq], writes=[bq])
            op("dve", lambda E: E.tensor_scalar(out=ssq, in0=ssq, scalar1=4.0 / 128, scalar2=4.0 * EPS,
                                                op0=ALU.mult, op1=ALU.add), reads=[bq], writes=[bq])
            op("pool", lambda E: E.tensor_tensor(out=ssq, in0=ssq, in1=mhalf.to_broadcast([128, 4]), op=ALU.pow),
               reads=[bq, B_const], writes=[bq])

        def m_n2a(s):
            k2 = s % 2
            pNt, bpN, pXt, bpX, pCt, bpC = ctx[s]
            ad = stat4[:, (2 * k2) * 4:(2 * k2) * 4 + 4]
            ssq = stat4[:, (2 * k2 + 1) * 4:(2 * k2 + 1) * 4 + 4]
            bq = B_stat4[k2]
            op("dve", lambda E: E.tensor_tensor(out=ssq, in0=ssq, in1=ad, op=ALU.mult), reads=[bq], writes=[bq])
            op("dve", lambda E: E.tensor_tensor(out=hn[:, 0, :].rearrange("p (a b) -> p a b", b=128),
                                                in0=pNt[:, :].rearrange("p (a b) -> p a b", b=128),
                                                in1=ssq.unsqueeze(2).to_broadcast([128, 4, 128]), op=ALU.mult),
               reads=[bpN, bq], writes=[B_hn[0]])
            op("pool", lambda E: E.tensor_tensor(out=hg[:, k2, :], in0=hn[:, 0, :], in1=og[:, s, :], op=ALU.mult),
               reads=[B_hn[0], B_og[s]], writes=[B_hg[k2]])

        def m_n2b(s):
            ts_ = slice(s * 128, (s + 1) * 128)
            k2 = s % 2
            ph, bph = sbank()
            phb = ph[:, :].bitcast(BF16)
            for hh in range(4):
                op("pe", lambda E: E.transpose(out=phb[:, hh * 128:(hh + 1) * 128], in_=hg[:, k2, hh * 128:(hh + 1) * 128],
                                               identity=ident_b), reads=[B_hg[k2], B_const], writes=[bph])
            op("act", lambda E: E.activation(out=mixT[:, 4:8, ts_], in_=phb[:, 0:512].rearrange("p (a b) -> p a b", b=128),
                                             func=AF.Copy), reads=[bph], writes=[B_mixT[4 + hh_][s] for hh_ in range(4)])

        m_p1(0)
        m_p1(1)
        m_p2a(0)
        m_p2mm(0)
        pm, bpm = sbank()
        pe2, bpe2 = sbank()
        for g in range(4):
            op("pe", lambda E: E.matmul(out=pm[:, :], lhsT=ones_b, rhs=ybf[:, g, :], start=(g == 0), stop=(g == 3)),
               reads=[B_ybf[g], B_const], writes=[bpm])
        for g in range(4):
            op("pe", lambda E: E.matmul(out=pe2[:, :], lhsT=ones_b, rhs=ysq[:, g, :], start=(g == 0), stop=(g == 3)),
               reads=[B_ybf[g], B_const], writes=[bpe2])
        op("dve", lambda E: E.tensor_scalar(out=lnm, in0=pm[:, :], scalar1=1.0 / CCH, scalar2=None, op0=ALU.mult),
           reads=[bpm], writes=[B_ln])
        op("act", lambda E: E.activation(out=lnr, in_=pm[:, :], func=AF.Square, scale=1.0 / CCH),
           reads=[bpm], writes=[B_lnr])
        op("dve", lambda E: E.scalar_tensor_tensor(out=lnr, in0=pe2[:, :], scalar=1.0 / CCH, in1=lnr,
                                                   op0=ALU.mult, op1=ALU.subtract), reads=[bpe2, B_lnr], writes=[B_lnr])
        op("act", lambda E: E.activation(out=lnr, in_=lnr, func=AF.Sqrt, bias=sm[:, 20:21]), reads=[B_lnr, B_sm], writes=[B_lnr])
        op("dve", lambda E: E.reciprocal(out=lnr, in_=lnr), reads=[B_lnr], writes=[B_lnr])
        def ln_group(g):
            op("dve", lambda E: E.tensor_tensor(out=yconv[:, g, :], in0=yconv[:, g, :], in1=lnm, op=ALU.subtract),
               reads=[B_yc[g], B_ln], writes=[B_yc[g]])
            op("dve", lambda E: E.tensor_tensor(out=yconv[:, g, :], in0=yconv[:, g, :], in1=lnr, op=ALU.mult),
               reads=[B_yc[g], B_lnr], writes=[B_yc[g]])
            op("act", lambda E: E.activation(out=mixT[:, g, :], in_=yconv[:, g, :], func=AF.Silu,
                                             scale=pv[:, PV_LG + g:PV_LG + g + 1], bias=pv[:, PV_LB + g:PV_LB + g + 1]),
               reads=[B_yc[g], B_const], writes=B_mixT[g])

        for k in range(NSUB):
            m_p2b(k)
            if k + 1 < NSUB:
                m_p2a(k + 1)
            if k >= 1:
                m_n2a(k - 1)
            if k + 2 < NSUB:
                m_p1(k + 2)
            if k + 1 < NSUB:
                m_p2mm(k + 1)
            m_n1(k)
            ln_group(k)
            if k >= 1:
                m_n2b(k - 1)
        m_n2a(NSUB - 1)
        m_n2b(NSUB - 1)
        linear_tm(dict(name="w_out", w=wd["w_out"]), 8, D, mixT, lambda j, s: [B_mixT[j][s]], None,
                  group_consume=lambda grp, acc: postnorm_group(grp, acc, h_t, h_bufs, 1, False, True))

    def xattn_setup():
        mt, mb, msem = hbuf[0]
        op("pool", lambda E: E.dma_start(out=mt[:, 0:2, :], in_=mem_d.rearrange("(s p) d -> p s d", p=128)),
           writes=mb[0:2], dma_sem=msem)
        prenorm(mt, mb, PV_PRE + 32, nsub=2, dst=memnT, dst_bufs=B_memn)

        def cons_k(c, pt, bpt):
            op("dve", lambda E: E.tensor_copy(out=KT[:, c, :], in_=pt[:, 0:256]), reads=[bpt], writes=[B_KT])

        linear_fm(dict(name="wk", w=wd["xattn_wk"], gcol=PV_PRE + 32, store=False), 4, memnT, B_memn, cons_k, ntok=256)

        def cons_vx(s, acc):
            for hf in range(2):
                pt, bpt = acc[hf]
                op("act", lambda E: E.activation(out=Vx[:, s, hf * 512:(hf + 1) * 512], in_=pt[:, :], func=AF.Copy),
                   reads=[bpt], writes=[B_Vx])

        linear_tm(dict(name="wv", w=wd["xattn_wv"], gcol=PV_PRE + 32, store=False), 8, D, memnT, lambda j, s: [B_memn[s]], cons_vx, subs_groups=((0, 1),))

    def xattn(h_t, h_bufs, after_prenorm=None):
        prenorm(h_t, h_bufs, PV_PRE + 16)
        if after_prenorm is not None:
            after_prenorm()

        def cons_q(c, pt, bpt):
            if c % 2:
                op("act", lambda E: E.activation(out=qxT[:, c, :], in_=pt[:, :], func=AF.Copy), reads=[bpt], writes=[B_qx[c]])
            else:
                op("dve", lambda E: E.tensor_copy(out=qxT[:, c, :], in_=pt[:, :]), reads=[bpt], writes=[B_qx[c]])

        linear_fm(dict(name="wq", w=wd["xattn_wq"], gcol=PV_PRE + 16), 4, xnT, B_xnT, cons_q)
        sc_ = 1.0 / math.sqrt(XD)

        P4 = [Pm[0], Pn[0], Pm[1], Pn[1]]
        B_P4 = [B_P[0], B_Pn[0], B_P[1], B_Pn[1]]

        def x_s1(s):
            nmx = sm[:, 24 + 8 * s:28 + 8 * s]
            rsum = sm[:, 28 + 8 * s:32 + 8 * s]
            bx = B_xst4[s]
            ts_ = slice(s * 128, (s + 1) * 128)
            pA = [bank(), bank()]
            for hh in range(4):
                pt, bpt = pA[hh // 2]
                o = (hh % 2) * 256
                for dc in range(2):
                    op("pe", lambda E: E.matmul(out=pt[:, o:o + 256], lhsT=qxT[:, 2 * hh + dc, ts_], rhs=KT[:, 2 * hh + dc, :],
                                                start=(dc == 0), stop=(dc == 1)),
                       reads=[B_qx[2 * hh + dc], B_KT], writes=[bpt])
            for i2 in range(2):
                pt, bpt = pA[i2]
                op("dve", lambda E: E.reduce_max(out=nmx[:, 2 * i2:2 * i2 + 2], in_=pt[:, :].rearrange("p (a b) -> p a b", b=256),
                                                 axis=AX.X), reads=[bpt], writes=[bx])
            op("dve", lambda E: E.tensor_scalar(out=nmx, in0=nmx, scalar1=-sc_, scalar2=None, op0=ALU.mult),
               reads=[bx], writes=[bx])
            for hh in range(4):
                pt, bpt = pA[hh // 2]
                o = (hh % 2) * 256
                op("act", lambda E: E.activation(out=P4[s][:, hh, :], in_=pt[:, o:o + 256], func=AF.Exp, scale=sc_,
                                                 bias=nmx[:, hh:hh + 1], accum_out=rsum[:, hh:hh + 1]),
                   reads=[bpt, bx], writes=[B_P4[s], bx])
            op("dve", lambda E: E.reciprocal(out=rsum, in_=rsum), reads=[bx], writes=[bx])
            op("dve", lambda E: E.tensor_tensor(out=P4[s][:, :, :], in0=P4[s][:, :, :],
                                                in1=rsum.unsqueeze(2).to_broadcast([128, 4, 256]), op=ALU.mult),
               reads=[B_P4[s], bx], writes=[B_P4[s]])

        def x_s2(s):
            k = s % 2
            ts_ = slice(s * 128, (s + 1) * 128)
            pT, bpT = bank()
            pTb = pT[:, :].bitcast(BF16)
            for hh in range(4):
                for mc in range(2):
                    i8 = hh * 2 + mc
                    op("pe", lambda E: E.transpose(out=pTb[:, i8 * 128:(i8 + 1) * 128], in_=P4[s][:, hh, mc * 128:(mc + 1) * 128],
                                                   identity=ident_b), reads=[B_P4[s], B_const], writes=[bpT])
            op("act", lambda E: E.activation(out=PT[k][:, :, :], in_=pTb[:, :].rearrange("p (a b) -> p a b", b=128), func=AF.Copy),
               reads=[bpT], writes=[B_PT[k]])
            pO = [bank(), bank()]
            for ch in range(8):
                hh = ch // 2
                pt, bpt = pO[ch // 4]
                o = (ch % 4) * 128
                for mc in range(2):
                    op("pe", lambda E: E.matmul(out=pt[:, o:o + 128], lhsT=Vx[:, mc, ch * 128:(ch + 1) * 128],
                                                rhs=PT[k][:, hh * 2 + mc, :], start=(mc == 0), stop=(mc == 1)),
                       reads=[B_Vx, B_PT[k]], writes=[bpt])
            for i2 in range(2):
                pt, bpt = pO[i2]
                eng = "act" if i2 else "dve"
                wr = [B_at[4 * i2 + q][s] for q in range(4)]
                if eng == "act":
                    op("act", lambda E: E.activation(out=attnT[:, 4 * i2:4 * i2 + 4, ts_],
                                                     in_=pt[:, :].rearrange("p (a b) -> p a b", b=128), func=AF.Copy),
                       reads=[bpt], writes=wr)
                else:
                    op("dve", lambda E: E.tensor_copy(out=attnT[:, 4 * i2:4 * i2 + 4, ts_],
                                                      in_=pt[:, :].rearrange("p (a b) -> p a b", b=128)),
                       reads=[bpt], writes=wr)

        for s in range(NSUB):
            x_s1(s)
        for s in range(NSUB):
            x_s2(s)
        linear_tm(dict(name="wo", w=wd["xattn_wo"]), 8, D, attnT, lambda j, s: [B_at[j][s]], None,
                  group_consume=lambda grp, acc: postnorm_group(grp, acc, h_t, h_bufs, 2, False, True))

    if need_x:
        xattn_setup()

    def load_x(it):
        h_t, h_bufs, h_sem = hbuf[it % 2]
        op("pool", lambda E: E.dma_start(out=h_t[:, :, :],
                                         in_=x_d[it * TT:(it + 1) * TT, :].rearrange("(s p) d -> p s d", p=128)),
           writes=h_bufs, dma_sem=h_sem)

    def store_out(it):
        h_t, h_bufs, h_sem = hbuf[it % 2]
        op("pool", lambda E: E.dma_start(out=out_d[it * TT:(it + 1) * TT, :].rearrange("(s p) d -> p s d", p=128),
                                         in_=h_t[:, :, :]),
           reads=h_bufs, dma_sem=h_sem)

    stage_list = [st_ for st_ in ("ffn1", "mix", "xattn", "ffn2") if st_ in stages]
    hoist = len(stage_list) == 4
    hoisted = set()
    load_x(0)
    for it in range(ntile):
        h_t, h_bufs, h_sem = hbuf[it % 2]
        pre_rs.clear()

        def deferred(it=it):
            if it >= 1:
                store_out(it - 1)
            if it >= 1 and it + 1 < ntile:
                load_x(it + 1)

        for si, st_ in enumerate(stage_list):
            nxt = si + 1 < len(stage_list)
            hook = deferred if si == 0 else None
            if st_ == "ffn1":
                ffn("ffn1", h_t, h_bufs, PV_PRE + 0, 0, nxt, after_prenorm=hook, skip_prenorm=(it in hoisted))
            elif st_ == "mix":
                mixer(h_t, h_bufs, after_prenorm=hook)
            elif st_ == "xattn":
                xattn(h_t, h_bufs, after_prenorm=hook)
            else:
                eh = mh = None
                if hoist and it >= 1 and it + 1 < ntile and si == len(stage_list) - 1:
                    nt, nb, _ = hbuf[(it + 1) % 2]
                    eh = lambda nt=nt, nb=nb: prenorm_stats(nt, nb, list(range(NSUB)))
                    mh = lambda nt=nt, nb=nb: prenorm(nt, nb, PV_PRE + 0)
                    hoisted.add(it + 1)
                ffn("ffn2", h_t, h_bufs, PV_PRE + 24, 3, nxt, after_prenorm=hook, early_hook=eh, mid_hook=mh)
        flush_store()
        if it == 0 and ntile > 1:
            load_x(1)
    store_out(ntile - 1)
    kb.wait_all("pool", hbuf[0][1] + hbuf[1][1])
    return nc, kb


def _pack_params(inp):
    def col(v):
        v = np.asarray(v, np.float32).reshape(-1)
        return v.reshape(-1, 128).T
    pvec = np.zeros((128, PV_N), np.float32)
    for i, nm in enumerate(("ffn1_pre_g", "mix_pre_g", "xattn_pre_g", "ffn2_pre_g", "mem_norm_g")):
        pvec[:, PV_PRE + 8 * i:PV_PRE + 8 * i + 8] = col(inp[nm][0])
    cw = np.asarray(inp["conv_w"][0], np.float32)
    pvec[:, PV_CW:PV_CW + 4 * CK] = cw.T.reshape(4, 128, CK).transpose(1, 0, 2).reshape(128, 4 * CK)
    pvec[:, PV_CB:PV_CB + 4] = col(inp["conv_b"][0])
    pvec[:, PV_LG:PV_LG + 4] = col(inp["conv_ln_g"][0])
    pvec[:, PV_LB:PV_LB + 4] = col(inp["conv_ln_b"][0])
    qw = np.asarray(inp["qk_conv_w"][0], np.float32)
    pvec[:, PV_QW:PV_QW + 32] = qw.T.reshape(8, 128, QK).transpose(1, 0, 2).reshape(128, 32)
    pvec[:, PV_QB:PV_QB + 8] = col(inp["qk_conv_b"][0])
    pvec[0:4, PV_BI] = np.asarray(inp["b_igate"][0], np.float32)
    pvec[0:4, PV_BF] = np.asarray(inp["b_fgate"][0], np.float32)
    rgain = np.zeros((128, RG_N), np.float32)
    for i, nm in enumerate(("ffn1_post_g", "mix_post_g", "xattn_post_g", "ffn2_post_g")):
        rgain[:, RG_POST + i * D:RG_POST + (i + 1) * D] = np.asarray(inp[nm][0], np.float32)[None, :]
    rgain[:, RG_M:RG_M + MW] = np.asarray(inp["mlstm_norm_g"][0], np.float32)[None, :]
    consts = np.zeros((128, 1032), np.float32)
    consts[:, 512] = -0.5
    for hh in range(4):
        consts[hh, 520 + hh * 128:520 + (hh + 1) * 128] = 1.0
    consts[:, 0:128] = np.eye(128, dtype=np.float32)
    consts[:, 128:256] = np.triu(np.ones((128, 128), np.float32))
    consts[:, 256:384] = 1.0
    for hh in range(4):
        consts[hh, 384 + hh * 32:384 + (hh + 1) * 32] = 1.0
    return pvec, rgain, consts


_WNAMES = ("ffn1_w_gate", "ffn1_w_up", "ffn1_w_down", "w_in", "w_out", "xattn_wq", "xattn_wk",
           "xattn_wv", "xattn_wo", "ffn2_w_gate", "ffn2_w_up", "ffn2_w_down")


def make_in_maps(inp, ncores, seq):
    pvec, rgain, consts = _pack_params(inp)
    shared = {nm: np.ascontiguousarray(np.asarray(inp[nm], np.float32)[0]) for nm in _WNAMES}
    shared.update(pvec=pvec, rgain=rgain, consts=consts)
    maps = []
    for c in range(ncores):
        m = dict(shared)
        m["x"] = np.ascontiguousarray(np.asarray(inp["x"], np.float32)[c, :seq])
        m["mem"] = np.ascontiguousarray(np.asarray(inp["mem"], np.float32)[c])
        maps.append(m)
    return maps


def kernel(**inputs):
    nc, kb = build_program(SEQ)
    maps = make_in_maps(inputs, NCORES, SEQ)
    res = run_bass_kernel_spmd(nc, maps, core_ids=list(range(NCORES)))
    return np.stack([np.asarray(r["out"], np.float32) for r in res.results], axis=0)
```

```python
import math
import numpy as np
import ml_dtypes
import concourse.bass as bass
import concourse.mybir as mybir
from concourse.bass_utils import run_bass_kernel_spmd

F32 = mybir.dt.float32
BF16 = mybir.dt.bfloat16
AF = mybir.ActivationFunctionType
ALU = mybir.AluOpType
AX = mybir.AxisListType

D = 1024
DFF = 2816
NMEM = 256
CCH = 512
CK = 31
MH = 4
MW = 512
QK = 4
INC = 3080
XH = 4
XD = 256
EPS = 1e-6
TT = 512
NSUB = TT // 128
NCORES = 8
SEQ = 8192

PV_PRE = 0
PV_CW = 40
PV_CB = PV_CW + 4 * CK
PV_LG = PV_CB + 4
PV_LB = PV_LG + 4
PV_QW = PV_LB + 4
PV_QB = PV_QW + 32
PV_BI = PV_QB + 8
PV_BF = PV_BI + 1
PV_N = PV_BF + 1

RG_POST = 0
RG_M = 4 * D
RG_N = RG_M + MW


class Buf:
    __slots__ = ("name", "w", "r", "al", "cw")

    def __init__(self, name):
        self.name = name
        self.w = None
        self.r = {}
        self.al = ()
        self.cw = None


def alias_groups(ga, gb):
    for a in ga:
        a.al = tuple(a.al) + tuple(gb)
    for b in gb:
        b.al = tuple(b.al) + tuple(ga)


class KB:
    def __init__(self, nc):
        self.nc = nc
        self.engs = {"pe": nc.tensor, "act": nc.scalar, "dve": nc.vector,
                     "pool": nc.gpsimd, "sp": nc.sync}
        self.sems = {}
        self.cnt = {}
        self.known = {e: {} for e in self.engs}
        for e in ("pe", "act", "dve", "pool"):
            self.newsem(e)
        self.nins = 0
        self.nwait = 0

    def newsem(self, name):
        self.sems[name] = self.nc.alloc_semaphore("s_" + name)
        self.cnt[name] = 0
        return name

    def op(self, eng, fn, reads=(), writes=(), dma_sem=None):
        deps = {}

        def add(tag):
            if tag is None:
                return
            k, v = tag
            if deps.get(k, 0) < v:
                deps[k] = v

        for b in reads:
            add(b.w)
        for b0 in writes:
            for b in (b0,) + tuple(b0.al):
                if not (b.w is not None and b.w[0] == eng and dma_sem is None):
                    add(b.w)
                for k, v in b.r.items():
                    if k == eng and dma_sem is None:
                        continue
                    add((k, v))
        kn = self.known[eng]
        waits = []
        for k, v in deps.items():
            if eng == "pe" and k == "pe" and dma_sem is None:
                continue
            if kn.get(k, 0) < v:
                waits.append((k, v))
                kn[k] = v
        E = self.engs[eng]
        for k, v in waits[1:]:
            E.wait_ge(self.sems[k], v)
            self.nwait += 1
        ins = fn(E)
        if waits:
            k, v = waits[0]
            ins._wait_ge(self.sems[k], v)
        if dma_sem is None:
            key = eng
            self.cnt[key] += 1
            ins.then_inc(self.sems[key], 1)
        else:
            key = dma_sem
            self.cnt[key] += 16
            ins.then_inc(self.sems[key], 16)
        tag = (key, self.cnt[key])
        for b in reads:
            if b.r.get(key, 0) < tag[1]:
                b.r[key] = tag[1]
        for b0 in writes:
            b0.w = tag
            b0.r = {}
            for b in b0.al:
                b.w = tag
                b.r = {}
        self.nins += 1
        return ins

    def wait_all(self, eng, bufs):
        deps = {}
        for b in bufs:
            for tag in [b.w] + list(b.r.items()):
                if tag is None:
                    continue
                k, v = tag
                if deps.get(k, 0) < v:
                    deps[k] = v
        kn = self.known[eng]
        for k, v in deps.items():
            if kn.get(k, 0) < v:
                self.engs[eng].wait_ge(self.sems[k], v)
                kn[k] = v


def build_program(seq=SEQ, stages=("ffn1", "mix", "xattn", "ffn2")):
    assert seq % TT == 0
    ntile = seq // TT
    nc = bass.Bass("TRN2", target_bir_lowering=False)
    kb = KB(nc)
    op = kb.op

    def dram_in(name, shape, dt=F32):
        return nc.dram_tensor(name, list(shape), dt, kind="ExternalInput").ap()

    x_d = dram_in("x", [seq, D])
    mem_d = dram_in("mem", [NMEM, D])
    pv_d = dram_in("pvec", [128, PV_N])
    rg_d = dram_in("rgain", [128, RG_N])
    cst_d = dram_in("consts", [128, 1032])
    wd = {}
    for nm, shp in (("ffn1_w_gate", [D, DFF]), ("ffn1_w_up", [D, DFF]), ("ffn1_w_down", [DFF, D]),
                    ("w_in", [D, INC]), ("w_out", [D, D]),
                    ("xattn_wq", [D, D]), ("xattn_wk", [D, D]), ("xattn_wv", [D, D]), ("xattn_wo", [D, D]),
                    ("ffn2_w_gate", [D, DFF]), ("ffn2_w_up", [D, DFF]), ("ffn2_w_down", [DFF, D])):
        wd[nm] = dram_in(nm, shp)
    out_d = nc.dram_tensor("out", [seq, D], F32, kind="ExternalOutput").ap()

    def scratch(name, shape):
        return nc.dram_tensor(name, list(shape), BF16, kind="Internal").ap()

    sc = {}
    for f in ("ffn1", "ffn2"):
        sc[f + "_g"] = scratch(f + "_sg", [11, 128, 8, 256])
        sc[f + "_u"] = scratch(f + "_su", [11, 128, 8, 256])
        sc[f + "_d"] = scratch(f + "_sd", [22, 128, D])
    sc["in_fm"] = scratch("s_in_fm", [8, 128, 8, 256])
    sc["in_v"] = scratch("s_in_v", [8, 128, 512])
    sc["in_o"] = scratch("s_in_o", [8, 128, 512])
    sc["w_out"] = scratch("s_w_out", [8, 128, D])
    sc["wq"] = scratch("s_wq", [4, 128, 8, 256])
    sc["wk"] = scratch("s_wk", [4, 128, 8, 256])
    sc["wv"] = scratch("s_wv", [8, 128, D])
    sc["wo"] = scratch("s_wo", [8, 128, D])
    sc["qdiag"] = scratch("s_qdiag", [2, 128, 16, 128])
    sc["cdiag"] = scratch("s_cdiag", [8, 128, 16, 128])

    def sb(name, shape, dt):
        return nc.alloc_sbuf_tensor(name, list(shape), dt)

    pv = sb("pv", [128, PV_N], F32)
    rg = sb("rg", [128, RG_N], F32)
    cst = sb("cst", [128, 1032], F32)
    cstb = sb("cstb", [128, 512], BF16)
    B_const = Buf("const")
    ident_b = cstb[:, 0:128]
    mask_f = cst[:, 128:256]
    ident_f = cst[:, 0:128]
    ones_b = cstb[:, 256:384]
    sel_f = cst[:, 384:512]
    mhalf = cst[:, 512:513]
    cwh = sb("cwh", [128, 4 * CK], F32)

    NSLOT = 6
    ring = []
    for i in range(NSLOT):
        t = sb(f"wr{i}", [128, 2048], BF16)
        ring.append((t, Buf(f"wr{i}"), kb.newsem(f"wr{i}")))
    ring_i = [0]

    converted = {}
    stg_i = [0]
    pending_store = []
    jit_state = {}

    def flush_store():
        while pending_store:
            pending_store.pop(0)()

    def fetch(kind, name, idx, w_ap=None, col0=0, W=256, n=1, gcol=None, store=True):
        t, b, sm_ = ring[ring_i[0] % NSLOT]
        ring_i[0] += 1
        key = (name, idx)
        if kind == "cu":
            piece = sc[name][idx].rearrange("p k c -> p (k c)")
            ncols = 2048
        elif kind == "nat":
            piece = sc[name][idx:idx + n].rearrange("j p c -> p j c")
            ncols = n * W
        else:
            piece = sc[name][idx].rearrange("p m c -> p (m c)")
            ncols = 2048
        if key in converted:
            flush_store()
            op("sp", lambda E: E.dma_start(out=t[:, 0:ncols], in_=piece), reads=[converted[key]], writes=[b], dma_sem=sm_)
            return t, b
        bsc = Buf("sc_%s_%d" % (name, idx))
        converted[key] = bsc
        if kind == "diag":
            for mm in range(16):
                m = idx * 16 + mm
                wsrc, wn = (cwh, 4 * CK) if name == "cdiag" else (pv[:, PV_QW:PV_QW + 32], 32)
                if m < wn:
                    op("dve", lambda E: E.tensor_scalar(out=t[:, mm * 128:(mm + 1) * 128], in0=ident_f,
                                                        scalar1=wsrc[:, m:m + 1], scalar2=None, op0=ALU.mult),
                       reads=[B_const], writes=[b])
                else:
                    op("dve", lambda E: E.memset(t[:, mm * 128:(mm + 1) * 128], 0.0), writes=[b])
        else:
            i = stg_i[0] % 2
            stg_i[0] += 1
            stg_t, stg_b2, stg_s = jit_state["stg"][i]
            if kind == "cu":
                kk, cc = 8, 256
                src = w_ap[:, col0 + idx * 256:col0 + (idx + 1) * 256].rearrange("(k p) c -> p k c", p=128)
                g0 = 0
            else:
                kk, cc = n, W
                src = w_ap[idx * 128:(idx + n) * 128, col0:col0 + W].rearrange("(j p) c -> p j c", p=128)
                g0 = idx
            sview = stg_t[:, 0:kk * cc].rearrange("p (k c) -> p k c", c=cc)
            tview = t[:, 0:kk * cc].rearrange("p (k c) -> p k c", c=cc)
            op("sp", lambda E: E.dma_start(out=sview, in_=src), writes=[stg_b2], dma_sem=stg_s)
            ce = ("dve", "pool")[stg_i[0] % 2]
            if gcol is not None:
                gv = pv[:, gcol + g0:gcol + g0 + kk].unsqueeze(2).to_broadcast([128, kk, cc])
                op(ce, lambda E: E.tensor_tensor(out=tview, in0=sview, in1=gv, op=ALU.mult),
                   reads=[stg_b2, B_const], writes=[b])
            else:
                ce = ("act", "dve", "pool")[stg_i[0] % 3]
                if ce == "act":
                    op("act", lambda E: E.activation(out=tview, in_=sview, func=AF.Copy), reads=[stg_b2], writes=[b])
                else:
                    op(ce, lambda E: E.tensor_copy(out=tview, in_=sview), reads=[stg_b2], writes=[b])
        flush_store()
        if store:
            pending_store.append(lambda: op("sp", lambda E: E.dma_start(out=piece, in_=t[:, 0:ncols]),
                                            reads=[b], writes=[bsc], dma_sem=sm_))
        return t, b

    banks = []
    for i in range(8):
        t = nc.alloc_psum_tensor(f"ps{i}", [128, 512], F32)
        banks.append((t, Buf(f"ps{i}")))
    bank_i = [0]

    def bank():
        r = banks[bank_i[0] % 8]
        bank_i[0] += 1
        return r

    hbuf = []
    for i in range(2):
        t = sb(f"h{i}", [128, NSUB, D], F32)
        hbuf.append((t, [Buf(f"h{i}_{s}") for s in range(NSUB)], kb.newsem(f"h{i}")))
    xnT = sb("xnT", [128, 8, TT], BF16)
    B_xnT = [Buf(f"xnT{s}") for s in range(NSUB)]
    _stg = []
    for i in range(2):
        bst = Buf(f"stg{i}")
        alias_groups([bst], hbuf[1][1][2 * i:2 * i + 2])
        _stg.append((hbuf[1][0][:, 2 * i:2 * i + 2, :].rearrange("p a b -> p (a b)"), bst, kb.newsem(f"stg{i}")))
    jit_state["stg"] = _stg
    def carve(arena, off, shape, dt):
        n = 1
        for d_ in shape[1:]:
            n *= d_
        nb = n * (2 if dt == BF16 else 4)
        assert off % 4 == 0 and nb % 4 == 0
        a = arena[:, off // 4:(off + nb) // 4]
        if dt == BF16:
            a = a.bitcast(BF16)
        if len(shape) == 3:
            a = a.rearrange("p (a b) -> p a b", b=shape[2])
        elif len(shape) == 4:
            a = a.rearrange("p (a b c) -> p a b c", b=shape[2], c=shape[3])
        return a

    arA = sb("arenaA", [128, 29184 // 4], F32)
    arB = sb("arenaB", [128, 32768 // 4], F32)
    arC = sb("arenaC", [128, 31360 // 4], F32)
    hid = carve(arA, 0, [128, 22, TT], BF16)
    B_hid = [Buf(f"hid{j}") for j in range(22)]
    sg = carve(arA, 22528, [128, 2, TT], F32)
    B_sg = [Buf("sg0"), Buf("sg1")]
    zq = carve(arA, 0, [128, 8, 516], BF16)
    B_zq = [Buf(f"zq{g}") for g in range(8)]
    ubf = carve(arA, 16480, [128, 4, 512], F32)
    ub = carve(arA, 24672, [128, 4, 544], BF16)
    B_ubf = [Buf(f"ubf{g}") for g in range(4)]
    B_ub = [Buf(f"ub{g}") for g in range(4)]
    alias_groups(B_hid + B_sg, B_zq + B_ub + B_ubf)
    xs = sb("xs", [128, 2, D], BF16)
    B_xs = [Buf("xs0"), Buf("xs1")]
    junk = sb("junk", [128, 2, D], F32)
    B_jh = [[Buf("junk00"), Buf("junk01")], [Buf("junk10"), Buf("junk11")]]
    B_junk = [B_jh[0], B_jh[1]]
    yconv = carve(arB, 0, [128, 4, TT], F32)
    ybf = carve(arB, 8192, [128, 4, TT], BF16)
    ysq = carve(arB, 12288, [128, 4, TT], BF16)
    mixT = carve(arB, 16384, [128, 8, TT], BF16)
    qkT = carve(arB, 24576, [128, 8, TT], BF16)
    B_yc = [Buf(f"yc{g}") for g in range(4)]
    B_ybf = [Buf(f"ybf{g}") for g in range(4)]
    B_mixT = [[Buf(f"mixT{j}_{q}") for q in range(NSUB)] for j in range(8)]
    B_qk = [Buf(f"qk{g}") for g in range(8)]
    vaug = carve(arC, 0, [128, NSUB, 4, 129], BF16)
    og = carve(arC, 4128, [128, NSUB, 512], BF16)
    ktm = carve(arC, 8224, [128, NSUB, 4, 128], BF16)
    hn = carve(arC, 12320, [128, 1, 512], F32)
    hg = carve(arC, 14368, [128, 2, 512], BF16)
    WT = carve(arC, 16416, [128, 2, 512], BF16)
    gE = carve(arC, 18976, [128, 512], F32)
    gNB = carve(arC, 21024, [128, 512], F32)
    gA = carve(arC, 23072, [128, 512], F32)
    gM = carve(arC, 25120, [128, 512], F32)
    lnm = carve(arC, 27168, [128, 512], F32)
    lnr = carve(arC, 29216, [128, 512], F32)
    B_v = [Buf(f"v{q}") for q in range(NSUB)]
    B_og = [Buf(f"og{q}") for q in range(NSUB)]
    B_ktm = [Buf(f"ktm{q}") for q in range(NSUB)]
    B_hn = [Buf("hn0"), Buf("hn1")]
    B_hg = [Buf("hg0"), Buf("hg1")]
    B_WT = [Buf("WT0"), Buf("WT1")]
    B_gate = Buf("gate")
    B_ln = Buf("ln")
    B_lnr = Buf("lnr")
    qxT = carve(arC, 0, [128, 8, TT], BF16)
    attnT = carve(arC, 8192, [128, 8, TT], BF16)
    Pm = [carve(arC, 16384, [128, 4, 256], BF16), carve(arC, 26624, [128, 4, 256], BF16)]
    Pn = [carve(arC, 18432, [128, 4, 256], BF16), carve(arC, 28672, [128, 4, 256], BF16)]
    PT = [carve(arC, 20480, [128, 8, 128], BF16), carve(arC, 22528, [128, 8, 128], BF16)]
    memnT = carve(arC, 22528, [128, 8, 256], BF16)
    B_qx = [Buf(f"qx{j}") for j in range(8)]
    B_at = [[Buf(f"at{j}_{q}") for q in range(NSUB)] for j in range(8)]
    B_P = [Buf("P0"), Buf("P1")]
    B_Pn = [Buf("Pn0"), Buf("Pn1")]
    B_PT = [Buf("PT0"), Buf("PT1")]
    B_memn = [Buf("memn0"), Buf("memn1")]
    B_xst4 = [Buf(f"xst{q}") for q in range(NSUB)]
    C_mix = B_v + B_og + B_ktm + B_hn + B_hg + B_WT + [B_gate, B_ln, B_lnr]
    C_x = B_qx + [b for r_ in B_at for b in r_] + B_P + B_Pn + B_PT + B_memn
    alias_groups(C_mix, C_x)
    alias_groups([B_PT[1]], B_memn)
    Cst = sb("Cst", [128, 4, 129], F32)
    Cbf = sb("Cbf", [128, 4, 129], BF16)
    B_C = [Buf(f"C{h_}") for h_ in range(4)]
    B_Cbf = [Buf(f"Cbf{h_}") for h_ in range(4)]
    KT = sb("KT", [128, 8, 256], BF16)
    Vx = sb("Vx", [128, 2, D], BF16)
    B_KT = Buf("KT")
    B_Vx = Buf("Vx")
    zq_halo = sb("zq_halo", [128, 8, 4], BF16)
    u_halo = sb("u_halo", [128, 4, 30], BF16)
    B_zqh = [Buf(f"zqh{g}") for g in range(8)]
    B_uh = [Buf(f"uh{g}") for g in range(4)]
    wif_f = sb("wif_f", [128, 8, 8], F32)
    wif = sb("wif", [128, 8, 8], BF16)
    wfT = sb("wfT", [128, 32], F32)
    B_wfT = Buf("wfT")
    decB = sb("decB", [128, 16], F32)
    B_dec = Buf("decB")
    sm = sb("small", [128, 64], F32)
    B_sm = Buf("small")
    jdum = sb("jdum", [128, D], BF16)
    B_jdum = Buf("jdum")
    stat4 = sb("stat4", [128, 16], F32)
    B_stat4 = [Buf("stat4_0"), Buf("stat4_1")]
    junkb = sb("junkb", [128, 128], BF16)
    B_junkb = Buf("junkb")
    stat = sb("stat", [128, 64], F32)
    B_stat = [Buf(f"st{i}") for i in range(64)]
    stat_i = [0]

    def st():
        i = stat_i[0] % 64
        stat_i[0] += 1
        return stat[:, i:i + 1], B_stat[i]

    setup_sem = kb.newsem("setup")
    op("sp", lambda E: E.dma_start(out=pv[:, :], in_=pv_d), writes=[B_const], dma_sem=setup_sem)
    op("sp", lambda E: E.dma_start(out=rg[:, :], in_=rg_d), writes=[B_const], dma_sem=setup_sem)
    op("sp", lambda E: E.dma_start(out=cst[:, :], in_=cst_d), writes=[B_const], dma_sem=setup_sem)
    op("dve", lambda E: E.tensor_copy(out=cstb[:, :], in_=cst[:, 0:512]), reads=[B_const], writes=[B_const])
    op("dve", lambda E: E.tensor_scalar_mul(out=cwh[:, :], in0=pv[:, PV_CW:PV_CW + 4 * CK], scalar1=0.5),
       reads=[B_const], writes=[B_const])

    need_ffn1 = "ffn1" in stages
    need_ffn2 = "ffn2" in stages
    need_mix = "mix" in stages
    need_x = "xattn" in stages
    pre_rs = {}

    def prenorm_stats(h_t, h_bufs, subs):
        tmp = {}
        for s in subs:
            k = s % 2
            ss, bss = st()
            op("act", lambda E: E.activation(out=jdum[:, :], in_=h_t[:, s, :], func=AF.Square, accum_out=ss),
               reads=[h_bufs[s]], writes=[B_jdum, bss])
            tmp[s] = (ss, bss)
        for s in subs:
            ss, bss = tmp[s]
            rs, brs = st()
            op("dve", lambda E: E.tensor_scalar(out=rs, in0=ss, scalar1=D * EPS, scalar2=None, op0=ALU.add),
               reads=[bss], writes=[brs])
            pre_rs[s] = (rs, brs)
        for s in subs:
            rs, brs = pre_rs[s]
            op("pool", lambda E: E.tensor_tensor(out=rs, in0=rs, in1=mhalf, op=ALU.pow),
               reads=[brs, B_const], writes=[brs])

    pre_scaled = set()

    def prenorm_early(h_t, h_bufs, subs):
        prenorm_stats(h_t, h_bufs, subs)
        for s in subs:
            if s < 2:
                rs, brs = pre_rs[s]
                op("dve", lambda E: E.tensor_scalar(out=xs[:, s % 2, :], in0=h_t[:, s, :], scalar1=rs,
                                                    scalar2=math.sqrt(D), op0=ALU.mult, op1=ALU.mult),
                   reads=[h_bufs[s], brs], writes=[B_xs[s % 2]])
                pre_scaled.add(s)

    def prenorm(h_t, h_bufs, gcol, nsub=NSUB, dst=None, dst_bufs=None):
        dst = xnT if dst is None else dst
        dst_bufs = B_xnT if dst_bufs is None else dst_bufs
        todo = [s for s in range(nsub) if s not in pre_rs]
        if todo:
            prenorm_stats(h_t, h_bufs, todo)
        for w in range(0, nsub, 2):
            subs = list(range(w, min(w + 2, nsub)))
            pts = {}
            for s in subs:
                k = s % 2
                rs, brs = pre_rs.pop(s)
                if s in pre_scaled:
                    pre_scaled.discard(s)
                    continue
                op("dve", lambda E: E.tensor_scalar(out=xs[:, k, :], in0=h_t[:, s, :], scalar1=rs, scalar2=math.sqrt(D),
                                                    op0=ALU.mult, op1=ALU.mult), reads=[h_bufs[s], brs], writes=[B_xs[k]])
            for s in subs:
                k = s % 2
                pt, bpt = bank()
                ptb = pt[:, :].bitcast(BF16)
                pts[s] = (ptb, bpt)
                for kc in range(8):
                    op("pe", lambda E: E.transpose(out=ptb[:, kc * 128:(kc + 1) * 128],
                                                   in_=xs[:, k, kc * 128:(kc + 1) * 128], identity=ident_b),
                       reads=[B_xs[k], B_const], writes=[bpt])
            for s in subs:
                ptb, bpt = pts[s]
                dview = dst[:, :, s * 128:(s + 1) * 128]
                pview = ptb[:, :].rearrange("p (k c) -> p k c", c=128)
                if s % 2:
                    op("act", lambda E: E.activation(out=dview, in_=pview, func=AF.Copy), reads=[bpt], writes=[dst_bufs[s]])
                else:
                    op("dve", lambda E: E.tensor_copy(out=dview, in_=pview), reads=[bpt], writes=[dst_bufs[s]])

    def postnorm_group(grp, acc, h_t, h_bufs, gidx, half_scale, next_pre):
        cfac = math.sqrt(D) * (0.5 if half_scale else 1.0)
        g0 = rg[:, RG_POST + gidx * D:RG_POST + gidx * D + 512]
        g1 = rg[:, RG_POST + gidx * D + 512:RG_POST + (gidx + 1) * D]
        tmp = {}
        for s in grp:
            (p0, b0), (p1, b1) = acc[s]
            ss0, bs0 = st()
            ss1, bs1 = st()
            op("act", lambda E: E.activation(out=jdum[:, 0:512], in_=p0[:, :], func=AF.Square, accum_out=ss0),
               reads=[b0], writes=[B_jdum, bs0])
            op("act", lambda E: E.activation(out=jdum[:, 0:512], in_=p1[:, :], func=AF.Square, accum_out=ss1),
               reads=[b1], writes=[B_jdum, bs1])
            tmp[s] = (ss0, bs0, ss1, bs1)
        rss = {}
        for s in grp:
            ss0, bs0, ss1, bs1 = tmp[s]
            rs, brs = st()
            op("dve", lambda E: E.tensor_scalar(out=rs, in0=ss0, scalar1=ss1, scalar2=None, op0=ALU.add),
               reads=[bs0, bs1], writes=[brs])
            op("dve", lambda E: E.tensor_scalar(out=rs, in0=rs, scalar1=D * EPS, scalar2=1.0 / (cfac * cfac),
                                                op0=ALU.add, op1=ALU.mult), reads=[brs], writes=[brs])
            rss[s] = (rs, brs)
        for s in grp:
            rs, brs = rss[s]
            op("pool", lambda E: E.tensor_tensor(out=rs, in0=rs, in1=mhalf, op=ALU.pow),
               reads=[brs, B_const], writes=[brs])
        for s in grp:
            k = s % 2
            rs, brs = rss[s]
            (p0, b0), (p1, b1) = acc[s]
            for (p, b, g, lo) in ((p0, b0, g0, 0), (p1, b1, g1, 512)):
                bj = B_jh[k][lo // 512]
                op("dve", lambda E: E.scalar_tensor_tensor(out=junk[:, k, lo:lo + 512], in0=p[:, :], scalar=rs, in1=g,
                                                           op0=ALU.mult, op1=ALU.mult),
                   reads=[b, brs, B_const], writes=[bj])
                op("dve", lambda E: E.tensor_tensor(out=h_t[:, s, lo:lo + 512], in0=h_t[:, s, lo:lo + 512],
                                                    in1=junk[:, k, lo:lo + 512], op=ALU.add),
                   reads=[bj, h_bufs[s]], writes=[h_bufs[s]])
        if next_pre:
            prenorm_early(h_t, h_bufs, list(grp))

    def linear_fm(wspec, nunits, src, src_bufs, consume, ntok=TT):
        for u in range(nunits):
            wt, wb = fetch("cu", wspec["name"], u, w_ap=wspec["w"], col0=wspec.get("col0", 0),
                           gcol=wspec.get("gcol"), store=wspec.get("store", True))
            for cc in range(2):
                c = 2 * u + cc
                pt, bpt = bank()
                for kc in range(8):
                    op("pe", lambda E: E.matmul(out=pt[:, 0:ntok], lhsT=wt[:, kc * 256 + cc * 128:kc * 256 + cc * 128 + 128],
                                                rhs=src[:, kc, :], start=(kc == 0), stop=(kc == 7)),
                       reads=[wb] + list(src_bufs), writes=[bpt])
                consume(c, pt, bpt)

    def linear_tm(wspec, nj, W, lhs, lhs_bufs_fn, consume, subs_groups=((0, 1), (2, 3)), group_consume=None):
        per = 2048 // W
        nh = W // 512
        for grp in subs_groups:
            acc = {s: [bank() for _ in range(nh)] for s in grp}
            for j0 in range(0, nj, per):
                n = min(per, nj - j0)
                wt, wb = fetch("nat", wspec["name"], j0, w_ap=wspec["w"], col0=wspec.get("col0", 0), W=W, n=n,
                               gcol=wspec.get("gcol"), store=wspec.get("store", True))
                for jj in range(n):
                    j = j0 + jj
                    for s in grp:
                        for hh in range(nh):
                            pt, bpt = acc[s][hh]
                            op("pe", lambda E: E.matmul(out=pt[:, :], lhsT=lhs[:, j, s * 128:(s + 1) * 128],
                                                        rhs=wt[:, jj * W + hh * 512:jj * W + hh * 512 + 512],
                                                        start=(j == 0), stop=(j == nj - 1)),
                               reads=[wb] + lhs_bufs_fn(j, s), writes=[bpt])
            if group_consume is not None:
                group_consume(grp, acc)
            else:
                for s in grp:
                    consume(s, acc[s])

    def ffn(f, h_t, h_bufs, pre_col, post_idx, next_pre, after_prenorm=None, skip_prenorm=False,
            early_hook=None, mid_hook=None):
        if not skip_prenorm:
            prenorm(h_t, h_bufs, pre_col)
        if after_prenorm is not None:
            after_prenorm()
        for u in range(11):
            if u == 2 and early_hook is not None:
                early_hook()
            wg, bg = fetch("cu", f + "_g", u, w_ap=wd[f + "_w_gate"], gcol=pre_col)
            wu, bu = fetch("cu", f + "_u", u, w_ap=wd[f + "_w_up"], gcol=pre_col)
            for cc in range(2):
                j = 2 * u + cc
                pg, bpg = bank()
                pu, bpu = bank()
                for (wt, wb, pt, bpt) in ((wg, bg, pg, bpg), (wu, bu, pu, bpu)):
                    for kc in range(8):
                        op("pe", lambda E: E.matmul(out=pt[:, :], lhsT=wt[:, kc * 256 + cc * 128:kc * 256 + cc * 128 + 128],
                                                    rhs=xnT[:, kc, :], start=(kc == 0), stop=(kc == 7)),
                           reads=[wb] + B_xnT, writes=[bpt])
                k = j % 2
                op("act", lambda E: E.activation(out=sg[:, k, :], in_=pg[:, :], func=AF.Silu),
                   reads=[bpg], writes=[B_sg[k]])
                op("dve", lambda E: E.tensor_tensor(out=hid[:, j, :], in0=sg[:, k, :], in1=pu[:, :], op=ALU.mult),
                   reads=[B_sg[k], bpu], writes=[B_hid[j]])
        if mid_hook is not None:
            mid_hook()
        linear_tm(dict(name=f + "_d", w=wd[f + "_w_down"]), 22, D, hid, lambda j, s: [B_hid[j]], None,
                  group_consume=lambda grp, acc: postnorm_group(grp, acc, h_t, h_bufs, post_idx, True, next_pre))


    LN_DK = math.log(1.0 / math.sqrt(128.0))
    MU = sm[0:4, 0:5]
    NBc = sm[0:4, 8:9]
    Mc = sm[0:4, 9:10]
    negbf = sm[0:4, 10:11]
    dtmp = sm[0:4, 12:16]
    dec = sm[0:4, 16:20]
    if need_mix:
        op("pool", lambda E: E.memset(sm[:, :], 0.0), writes=[B_sm])
        op("pool", lambda E: E.memset(sm[:, 11:12], LN_DK), writes=[B_sm])
        op("pool", lambda E: E.memset(sm[:, 20:21], 1e-5), writes=[B_sm])
        op("pool", lambda E: E.memset(Cst[:, :, :], 0.0), writes=B_C)
        op("pool", lambda E: E.memset(zq_halo[:, :, :], 0.0), writes=B_zqh)
        op("pool", lambda E: E.memset(u_halo[:, :, :], 0.0), writes=B_uh)
        op("dve", lambda E: E.tensor_scalar(out=negbf, in0=pv[0:4, PV_BF:PV_BF + 1], scalar1=-1.0, scalar2=None,
                                            op0=ALU.mult), reads=[B_const, B_sm], writes=[B_sm])
        with nc.allow_non_contiguous_dma(reason="tiny gate weights"):
            op("sp", lambda E: E.dma_start(out=wif_f[:, :, :],
                                           in_=wd["w_in"][:, 3072:3080].rearrange("(k p) c -> p k c", p=128)),
               writes=[B_const], dma_sem=setup_sem)
        op("dve", lambda E: E.tensor_tensor(out=wif[:, :, :], in0=wif_f[:, :, :],
                                            in1=pv[:, PV_PRE + 8:PV_PRE + 16].unsqueeze(2).to_broadcast([128, 8, 8]),
                                            op=ALU.mult), reads=[B_const], writes=[B_const])

    def mixer(h_t, h_bufs, after_prenorm=None):
        prenorm(h_t, h_bufs, PV_PRE + 8)
        if after_prenorm is not None:
            after_prenorm()
        op("pool", lambda E: E.memset(vaug[:, :, :, 128:129], 1.0), writes=B_v)
        op("pool", lambda E: E.memset(lnm[0:4, :], 0.0), writes=[B_ln])
        def cons_v(s, acc):
            pt, bpt = acc[0]
            op("dve", lambda E: E.tensor_copy(out=vaug[:, s, :, 0:128],
                                              in_=pt[:, :].rearrange("p (a b) -> p a b", b=128)),
               reads=[bpt], writes=[B_v[s]])

        og_todo = []

        def og_finish():
            gm_ = rg[:, RG_M:RG_M + MW]
            for s in og_todo:
                op("pool", lambda E: E.tensor_tensor(out=og[:, s, :], in0=og[:, s, :], in1=gm_, op=ALU.mult),
                   reads=[B_og[s], B_const], writes=[B_og[s]])
                op("pool", lambda E: E.tensor_tensor(out=og[:, s, :], in0=og[:, s, :], in1=gm_, op=ALU.add),
                   reads=[B_og[s], B_const], writes=[B_og[s]])

        def cons_o(s, acc):
            pt, bpt = acc[0]
            op("act", lambda E: E.activation(out=og[:, s, :], in_=pt[:, :], func=AF.Tanh, scale=0.5),
               reads=[bpt], writes=[B_og[s]])
            og_todo.append(s)

        pi, bpi = bank()
        pf, bpf = bank()
        for (pt, bpt, c0) in ((pi, bpi, 0), (pf, bpf, 4)):
            for kc in range(8):
                op("pe", lambda E: E.matmul(out=pt[0:4, :], lhsT=wif[:, kc, c0:c0 + 4], rhs=xnT[:, kc, :],
                                            start=(kc == 0), stop=(kc == 7)),
                   reads=[B_const] + B_xnT, writes=[bpt])
        G = [B_gate, B_sm]
        op("act", lambda E: E.activation(out=gE[0:4, :], in_=pf[0:4, :], func=AF.Exp, scale=-1.0, bias=negbf),
           reads=[bpf, B_sm], writes=[B_gate])
        op("act", lambda E: E.activation(out=gE[0:4, :], in_=gE[0:4, :], func=AF.Ln, bias=1.0),
           reads=[B_gate], writes=[B_gate])
        op("dve", lambda E: E.tensor_tensor_scan(out=gNB[0:4, :], data0=gE[0:4, :], data1=lnm[0:4, :], initial=NBc,
                                                 op0=ALU.add, op1=ALU.add), reads=G + [B_ln], writes=[B_gate])
        op("dve", lambda E: E.scalar_tensor_tensor(out=gA[0:4, :], in0=pi[0:4, :], scalar=pv[0:4, PV_BI:PV_BI + 1],
                                                   in1=gNB[0:4, :], op0=ALU.add, op1=ALU.add),
           reads=[bpi, B_const] + G, writes=[B_gate])
        op("dve", lambda E: E.tensor_tensor_scan(out=gM[0:4, :], data0=gA[0:4, :], data1=gA[0:4, :], initial=Mc,
                                                 op0=ALU.max, op1=ALU.max), reads=G, writes=[B_gate])
        op("dve", lambda E: E.tensor_copy(out=sm[0:4, 0:1], in_=Mc), reads=G, writes=[B_sm])
        op("dve", lambda E: E.tensor_copy(out=sm[0:4, 1:5],
                                          in_=gM[0:4, :].rearrange("p (c t) -> p c t", t=128)[:, :, 127]),
           reads=G, writes=[B_sm])
        op("dve", lambda E: E.tensor_copy(out=Mc, in_=gM[0:4, 511:512]), reads=G, writes=[B_sm])
        op("dve", lambda E: E.tensor_copy(out=NBc, in_=gNB[0:4, 511:512]), reads=G, writes=[B_sm])
        mub = sm[0:4, 1:5].unsqueeze(2).to_broadcast([4, 4, 128])
        for arr in (gA, gNB):
            a3 = arr[0:4, :].rearrange("p (c t) -> p c t", t=128)
            op("dve", lambda E: E.tensor_tensor(out=a3, in0=a3, in1=mub, op=ALU.subtract), reads=G, writes=[B_gate])
        op("dve", lambda E: E.tensor_tensor(out=dtmp, in0=sm[0:4, 0:4], in1=sm[0:4, 1:5], op=ALU.subtract),
           reads=G, writes=[B_sm])
        op("act", lambda E: E.activation(out=gA[0:4, :], in_=gA[0:4, :], func=AF.Exp, bias=sm[0:4, 11:12]),
           reads=G, writes=[B_gate])
        op("act", lambda E: E.activation(out=gNB[0:4, :], in_=gNB[0:4, :], func=AF.Exp), reads=G, writes=[B_gate])
        op("act", lambda E: E.activation(out=dec, in_=dtmp, func=AF.Exp), reads=G, writes=[B_sm])
        linear_tm(dict(name="in_v", w=wd["w_in"], col0=2048, gcol=PV_PRE + 8), 8, 512, xnT, lambda j, s: [B_xnT[s]], cons_v, subs_groups=((0, 1, 2, 3),))
        linear_tm(dict(name="in_o", w=wd["w_in"], col0=2560, gcol=PV_PRE + 8), 8, 512, xnT, lambda j, s: [B_xnT[s]], cons_o, subs_groups=((0, 1, 2, 3),))
        def cons_z(c, pt, bpt):
            if c < 4:
                op("act", lambda E: E.activation(out=ubf[:, c, :], in_=pt[:, :], func=AF.Copy),
                   reads=[bpt], writes=[B_ubf[c]])
                op("pool", lambda E: E.tensor_copy(out=ub[:, c, 0:30], in_=u_halo[:, c, :]),
                   reads=[B_uh[c]], writes=[B_ub[c]])
            elif c < 8:
                g = c - 4
                k = g % 2
                op("act", lambda E: E.activation(out=junk[:, k, 0:512], in_=pt[:, :], func=AF.Tanh, scale=0.5),
                   reads=[bpt], writes=[B_jh[k][0]])
                op("dve", lambda E: E.scalar_tensor_tensor(out=ub[:, g, 30:542], in0=junk[:, k, 0:512], scalar=1.0,
                                                           in1=ubf[:, g, :], op0=ALU.add, op1=ALU.mult),
                   reads=[B_jh[k][0], B_ubf[g]], writes=[B_ub[g]])
                op("pool", lambda E: E.tensor_copy(out=u_halo[:, g, :], in_=ub[:, g, 512:542]),
                   reads=[B_ub[g]], writes=[B_uh[g]])
            else:
                gq = c - 8
                op("act", lambda E: E.activation(out=zq[:, gq, 3:515], in_=pt[:, :], func=AF.Copy),
                   reads=[bpt], writes=[B_zq[gq]])
                op("pool", lambda E: E.tensor_copy(out=zq[:, gq, 0:3], in_=zq_halo[:, gq, 0:3]),
                   reads=[B_zqh[gq]], writes=[B_zq[gq]])
                op("pool", lambda E: E.tensor_copy(out=zq_halo[:, gq, 0:3], in_=zq[:, gq, 512:515]),
                   reads=[B_zq[gq]], writes=[B_zqh[gq]])
                qpend.append(gq)
                if len(qpend) > 2:
                    emit_qconv(qpend.pop(0))

        qpend = []
        qslots = {}

        def emit_qconv(gq):
            if gq // 4 not in qslots:
                qslots[gq // 4] = fetch("diag", "qdiag", gq // 4)
            wt, wb = qslots[gq // 4]
            pt2, bpt2 = bank()
            for j in range(QK):
                m = (gq * QK + j) % 16
                op("pe", lambda E: E.matmul(out=pt2[:, :], lhsT=wt[:, m * 128:(m + 1) * 128], rhs=zq[:, gq, j:j + 512],
                                            start=(j == 0), stop=(j == QK - 1)), reads=[wb, B_zq[gq]], writes=[bpt2])
            op("act", lambda E: E.activation(out=qkT[:, gq, :], in_=pt2[:, :], func=AF.Silu,
                                             bias=pv[:, PV_QB + gq:PV_QB + gq + 1]),
               reads=[bpt2, B_const], writes=[B_qk[gq]])

        linear_fm(dict(name="in_fm", w=wd["w_in"], gcol=PV_PRE + 8), 8, xnT, B_xnT, cons_z)
        while qpend:
            emit_qconv(qpend.pop(0))
        pw, bpw = bank()
        for s in range(NSUB):
            op("pe", lambda E: E.matmul(out=pw[:, s * 8:s * 8 + 4], lhsT=gA[0:4, s * 128:(s + 1) * 128],
                                        rhs=ident_f[0:4, 0:4], start=True, stop=True),
               reads=[B_gate, B_const], writes=[bpw])
            op("pe", lambda E: E.matmul(out=pw[:, s * 8 + 4:s * 8 + 8], lhsT=gNB[0:4, s * 128:(s + 1) * 128],
                                        rhs=ident_f[0:4, 0:4], start=True, stop=True),
               reads=[B_gate, B_const], writes=[bpw])
        for hh in range(4):
            op("pe", lambda E: E.matmul(out=pw[:, 32 + hh * 4:32 + hh * 4 + 4], lhsT=cst[0:4, 520 + hh * 128:520 + (hh + 1) * 128],
                                        rhs=dec, start=True, stop=True), reads=[B_sm, B_const], writes=[bpw])
        op("dve", lambda E: E.tensor_copy(out=wfT[:, :], in_=pw[:, 0:32]), reads=[bpw], writes=[B_wfT])
        op("dve", lambda E: E.tensor_copy(out=decB[:, :], in_=pw[:, 32:48]), reads=[bpw], writes=[B_dec])

        dslots = {}
        for g in range(4):
            pt, bpt = bank()
            for j in range(CK):
                m = g * CK + j
                if m // 16 not in dslots:
                    dslots[m // 16] = fetch("diag", "cdiag", m // 16)
                wt, wb = dslots[m // 16]
                op("pe", lambda E: E.matmul(out=pt[:, :], lhsT=wt[:, (m % 16) * 128:(m % 16 + 1) * 128],
                                            rhs=ub[:, g, j:j + 512], start=(j == 0), stop=(j == CK - 1)),
                   reads=[wb, B_ub[g]], writes=[bpt])
            cb = pv[:, PV_CB + g:PV_CB + g + 1]
            op("act", lambda E: E.activation(out=yconv[:, g, :], in_=pt[:, :], func=AF.Identity, bias=cb),
               reads=[bpt, B_const], writes=[B_yc[g]])
            op("act", lambda E: E.activation(out=ybf[:, g, :], in_=pt[:, :], func=AF.Identity, bias=cb),
               reads=[bpt, B_const], writes=[B_ybf[g]])
            op("act", lambda E: E.activation(out=ysq[:, g, :], in_=pt[:, :], func=AF.Square, bias=cb),
               reads=[bpt, B_const], writes=[B_ybf[g]])

        og_finish()
        ctx = {}
        rot = [0]

        def sbank():
            r = banks[4 + rot[0] % 4]
            rot[0] += 1
            return r

        def m_p1(s):
            ts_ = slice(s * 128, (s + 1) * 128)
            k2 = s % 2
            wg_s = wfT[:, s * 8:s * 8 + 4]
            WTk = WT[:, k2, :].rearrange("p (a b) -> p a b", b=128)
            pk, bpk = sbank()
            pkb = pk[:, :].bitcast(BF16)
            for hh in range(4):
                op("pe", lambda E: E.transpose(out=pkb[:, hh * 128:(hh + 1) * 128], in_=qkT[:, 4 + hh, ts_],
                                               identity=ident_b), reads=[B_qk[4 + hh], B_const], writes=[bpk])
            for hh in range(4):
                op("act", lambda E: E.activation(out=ktm[:, s, hh, :], in_=pkb[:, hh * 128:(hh + 1) * 128], func=AF.Copy,
                                                 scale=wfT[:, s * 8 + hh:s * 8 + hh + 1]),
                   reads=[bpk, B_wfT], writes=[B_ktm[s]])
            pS, bpS = sbank()
            for hh in range(4):
                op("pe", lambda E: E.matmul(out=pS[:, hh * 128:(hh + 1) * 128], lhsT=qkT[:, 4 + hh, ts_],
                                            rhs=qkT[:, hh, ts_], start=True, stop=True),
                   reads=[B_qk[4 + hh], B_qk[hh]], writes=[bpS])
            op("dve", lambda E: E.tensor_tensor(out=WTk, in0=pS[:, :].rearrange("p (a b) -> p a b", b=128),
                                                in1=wg_s.unsqueeze(2).to_broadcast([128, 4, 128]), op=ALU.mult),
               reads=[bpS, B_wfT], writes=[B_WT[k2]])
            op("pool", lambda E: E.tensor_tensor(out=WTk, in0=WTk, in1=mask_f.unsqueeze(1).to_broadcast([128, 4, 128]),
                                                 op=ALU.mult), reads=[B_WT[k2], B_const], writes=[B_WT[k2]])

        def m_p2a(s):
            ts_ = slice(s * 128, (s + 1) * 128)
            k2 = s % 2
            dec_s = decB[:, :].rearrange("p (h c) -> p h c", c=4)[:, :, s]
            WTk = WT[:, k2, :].rearrange("p (a b) -> p a b", b=128)
            op("dve", lambda E: E.tensor_tensor(out=Cst[:, :, :], in0=Cst[:, :, :],
                                                in1=dec_s.unsqueeze(2).to_broadcast([128, 4, 129]), op=ALU.mult),
               reads=B_C + [B_dec], writes=B_C)
            op("act", lambda E: E.activation(out=Cbf[:, :, :], in_=Cst[:, :, :], func=AF.Copy), reads=B_C, writes=B_Cbf)

        def m_p2mm(s):
            ts_ = slice(s * 128, (s + 1) * 128)
            k2 = s % 2
            WTk = WT[:, k2, :].rearrange("p (a b) -> p a b", b=128)
            pNt, bpN = banks[k2]
            pXt, bpX = banks[2 + k2]
            pCt, bpC = sbank()
            for hh in range(4):
                hs = slice(hh * 128, (hh + 1) * 128)
                op("pe", lambda E: E.matmul(out=pNt[:, hs], lhsT=WTk[:, hh, :], rhs=vaug[:, s, hh, 0:128],
                                            start=True, stop=False), reads=[B_WT[k2], B_v[s]], writes=[bpN])
                op("pe", lambda E: E.matmul(out=pNt[:, hs], lhsT=qkT[:, hh, ts_], rhs=Cbf[:, hh, 0:128],
                                            start=False, stop=True), reads=[B_qk[hh], B_Cbf[hh]], writes=[bpN])
                op("pe", lambda E: E.matmul(out=pXt[:, hh:hh + 1], lhsT=WTk[:, hh, :], rhs=vaug[:, s, hh, 128:129],
                                            start=True, stop=False), reads=[B_WT[k2], B_v[s]], writes=[bpX])
                op("pe", lambda E: E.matmul(out=pXt[:, hh:hh + 1], lhsT=qkT[:, hh, ts_], rhs=Cbf[:, hh, 128:129],
                                            start=False, stop=True), reads=[B_qk[hh], B_Cbf[hh]], writes=[bpX])
            for hh in range(4):
                hs = slice(hh * 128, (hh + 1) * 128)
                op("pe", lambda E: E.matmul(out=pCt[:, hs], lhsT=ktm[:, s, hh, :], rhs=vaug[:, s, hh, 0:128],
                                            start=True, stop=True), reads=[B_ktm[s], B_v[s]], writes=[bpC])
                op("pe", lambda E: E.matmul(out=pXt[:, 4 + hh:5 + hh], lhsT=ktm[:, s, hh, :], rhs=vaug[:, s, hh, 128:129],
                                            start=True, stop=True), reads=[B_ktm[s], B_v[s]], writes=[bpX])
            ctx[s] = (pNt, bpN, pXt, bpX, pCt, bpC)

        def m_p2b(s):
            pNt, bpN, pXt, bpX, pCt, bpC = ctx[s]
            op("dve", lambda E: E.tensor_tensor(out=Cst[:, :, 0:128], in0=Cst[:, :, 0:128],
                                                in1=pCt[:, :].rearrange("p (a b) -> p a b", b=128), op=ALU.add),
               reads=B_C + [bpC], writes=B_C)
            op("dve", lambda E: E.tensor_tensor(out=Cst[:, :, 128], in0=Cst[:, :, 128], in1=pXt[:, 4:8], op=ALU.add),
               reads=B_C + [bpX], writes=B_C)

        def m_n1(s):
            k2 = s % 2
            pNt, bpN, pXt, bpX, pCt, bpC = ctx[s]
            fl_s = wfT[:, s * 8 + 4:s * 8 + 8]
            ad = stat4[:, (2 * k2) * 4:(2 * k2) * 4 + 4]
            ssq = stat4[:, (2 * k2 + 1) * 4:(2 * k2 + 1) * 4 + 4]
            bq = B_stat4[k2]
            op("act", lambda E: E.activation(out=ad, in_=pXt[:, 0:4], func=AF.Abs), reads=[bpX], writes=[bq])
            for hh in range(4):
                op("act", lambda E: E.activation(out=junkb[:, :], in_=pNt[:, hh * 128:(hh + 1) * 128], func=AF.Square,
                                                 accum_out=ssq[:, hh:hh + 1]), reads=[bpN], writes=[B_junkb, bq])
            op("dve", lambda E: E.tensor_tensor(out=ad, in0=ad, in1=fl_s, op=ALU.max), reads=[bq, B_wfT], writes=[bq])
            op("dve", lambda E: E.reciprocal(out=ad, in_=ad), reads=[bq], writes=[bq])
            op("dve", lambda E: E.tensor_tensor(out=ssq, in0=ssq, in1=ad, op=ALU.mult), reads=[bq], writes=[bq])
            op("dve", lambda E: E.tensor_tensor(out=ssq, in0=ssq, in1=ad, op=ALU.mult), reads=[bq], writes=[bq])
            op("dve", lambda E: E.tensor_scalar(out=ssq, in0=ssq, scalar1=4.0 / 128, scalar2=4.0 * EPS,
                                                op0=ALU.mult, op1=ALU.add), reads=[bq], writes=[bq])
            op("pool", lambda E: E.tensor_tensor(out=ssq, in0=ssq, in1=mhalf.to_broadcast([128, 4]), op=ALU.pow),
               reads=[bq, B_const], writes=[bq])

        def m_n2a(s):
            k2 = s % 2
            pNt, bpN, pXt, bpX, pCt, bpC = ctx[s]
            ad = stat4[:, (2 * k2) * 4:(2 * k2) * 4 + 4]
            ssq = stat4[:, (2 * k2 + 1) * 4:(2 * k2 + 1) * 4 + 4]
            bq = B_stat4[k2]
            op("dve", lambda E: E.tensor_tensor(out=ssq, in0=ssq, in1=ad, op=ALU.mult), reads=[bq], writes=[bq])
            op("dve", lambda E: E.tensor_tensor(out=hn[:, 0, :].rearrange("p (a b) -> p a b", b=128),
                                                in0=pNt[:, :].rearrange("p (a b) -> p a b", b=128),
                                                in1=ssq.unsqueeze(2).to_broadcast([128, 4, 128]), op=ALU.mult),
               reads=[bpN, bq], writes=[B_hn[0]])
            op("pool", lambda E: E.tensor_tensor(out=hg[:, k2, :], in0=hn[:, 0, :], in1=og[:, s, :], op=ALU.mult),
               reads=[B_hn[0], B_og[s]], writes=[B_hg[k2]])

        def m_n2b(s):
            ts_ = slice(s * 128, (s + 1) * 128)
            k2 = s % 2
            ph, bph = sbank()
            phb = ph[:, :].bitcast(BF16)
            for hh in range(4):
                op("pe", lambda E: E.transpose(out=phb[:, hh * 128:(hh + 1) * 128], in_=hg[:, k2, hh * 128:(hh + 1) * 128],
                                               identity=ident_b), reads=[B_hg[k2], B_const], writes=[bph])
            op("act", lambda E: E.activation(out=mixT[:, 4:8, ts_], in_=phb[:, 0:512].rearrange("p (a b) -> p a b", b=128),
                                             func=AF.Copy), reads=[bph], writes=[B_mixT[4 + hh_][s] for hh_ in range(4)])

        m_p1(0)
        m_p1(1)
        m_p2a(0)
        m_p2mm(0)
        pm, bpm = sbank()
        pe2, bpe2 = sbank()
        for g in range(4):
            op("pe", lambda E: E.matmul(out=pm[:, :], lhsT=ones_b, rhs=ybf[:, g, :], start=(g == 0), stop=(g == 3)),
               reads=[B_ybf[g], B_const], writes=[bpm])
        for g in range(4):
            op("pe", lambda E: E.matmul(out=pe2[:, :], lhsT=ones_b, rhs=ysq[:, g, :], start=(g == 0), stop=(g == 3)),
               reads=[B_ybf[g], B_const], writes=[bpe2])
        op("dve", lambda E: E.tensor_scalar(out=lnm, in0=pm[:, :], scalar1=1.0 / CCH, scalar2=None, op0=ALU.mult),
           reads=[bpm], writes=[B_ln])
        op("act", lambda E: E.activation(out=lnr, in_=pm[:, :], func=AF.Square, scale=1.0 / CCH),
           reads=[bpm], writes=[B_lnr])
        op("dve", lambda E: E.scalar_tensor_tensor(out=lnr, in0=pe2[:, :], scalar=1.0 / CCH, in1=lnr,
                                                   op0=ALU.mult, op1=ALU.subtract), reads=[bpe2, B_lnr], writes=[B_lnr])
        op("act", lambda E: E.activation(out=lnr, in_=lnr, func=AF.Sqrt, bias=sm[:, 20:21]), reads=[B_lnr, B_sm], writes=[B_lnr])
        op("dve", lambda E: E.reciprocal(out=lnr, in_=lnr), reads=[B_lnr], writes=[B_lnr])
        def ln_group(g):
            op("dve", lambda E: E.tensor_tensor(out=yconv[:, g, :], in0=yconv[:, g, :], in1=lnm, op=ALU.subtract),
               reads=[B_yc[g], B_ln], writes=[B_yc[g]])
            op("dve", lambda E: E.tensor_tensor(out=yconv[:, g, :], in0=yconv[:, g, :], in1=lnr, op=ALU.mult),
               reads=[B_yc[g], B_lnr], writes=[B_yc[g]])
            op("act", lambda E: E.activation(out=mixT[:, g, :], in_=yconv[:, g, :], func=AF.Silu,
                                             scale=pv[:, PV_LG + g:PV_LG + g + 1], bias=pv[:, PV_LB + g:PV_LB + g + 1]),
               reads=[B_yc[g], B_const], writes=B_mixT[g])

        for k in range(NSUB):
            m_p2b(k)
            if k + 1 < NSUB:
                m_p2a(k + 1)
            if k >= 1:
                m_n2a(k - 1)
            if k + 2 < NSUB:
                m_p1(k + 2)
            if k + 1 < NSUB:
                m_p2mm(k + 1)
            m_n1(k)
            ln_group(k)
            if k >= 1:
                m_n2b(k - 1)
        m_n2a(NSUB - 1)
        m_n2b(NSUB - 1)
        linear_tm(dict(name="w_out", w=wd["w_out"]), 8, D, mixT, lambda j, s: [B_mixT[j][s]], None,
                  group_consume=lambda grp, acc: postnorm_group(grp, acc, h_t, h_bufs, 1, False, True))

    def xattn_setup():
        mt, mb, msem = hbuf[0]
        op("pool", lambda E: E.dma_start(out=mt[:, 0:2, :], in_=mem_d.rearrange("(s p) d -> p s d", p=128)),
           writes=mb[0:2], dma_sem=msem)
        prenorm(mt, mb, PV_PRE + 32, nsub=2, dst=memnT, dst_bufs=B_memn)

        def cons_k(c, pt, bpt):
            op("dve", lambda E: E.tensor_copy(out=KT[:, c, :], in_=pt[:, 0:256]), reads=[bpt], writes=[B_KT])

        linear_fm(dict(name="wk", w=wd["xattn_wk"], gcol=PV_PRE + 32, store=False), 4, memnT, B_memn, cons_k, ntok=256)

        def cons_vx(s, acc):
            for hf in range(2):
                pt, bpt = acc[hf]
                op("act", lambda E: E.activation(out=Vx[:, s, hf * 512:(hf + 1) * 512], in_=pt[:, :], func=AF.Copy),
                   reads=[bpt], writes=[B_Vx])

        linear_tm(dict(name="wv", w=wd["xattn_wv"], gcol=PV_PRE + 32, store=False), 8, D, memnT, lambda j, s: [B_memn[s]], cons_vx, subs_groups=((0, 1),))

    def xattn(h_t, h_bufs, after_prenorm=None):
        prenorm(h_t, h_bufs, PV_PRE + 16)
        if after_prenorm is not None:
            after_prenorm()

        def cons_q(c, pt, bpt):
            if c % 2:
                op("act", lambda E: E.activation(out=qxT[:, c, :], in_=pt[:, :], func=AF.Copy), reads=[bpt], writes=[B_qx[c]])
            else:
                op("dve", lambda E: E.tensor_copy(out=qxT[:, c, :], in_=pt[:, :]), reads=[bpt], writes=[B_qx[c]])

        linear_fm(dict(name="wq", w=wd["xattn_wq"], gcol=PV_PRE + 16), 4, xnT, B_xnT, cons_q)
        sc_ = 1.0 / math.sqrt(XD)

        P4 = [Pm[0], Pn[0], Pm[1], Pn[1]]
        B_P4 = [B_P[0], B_Pn[0], B_P[1], B_Pn[1]]

        def x_s1(s):
            nmx = sm[:, 24 + 8 * s:28 + 8 * s]
            rsum = sm[:, 28 + 8 * s:32 + 8 * s]
            bx = B_xst4[s]
            ts_ = slice(s * 128, (s + 1) * 128)
            pA = [bank(), bank()]
            for hh in range(4):
                pt, bpt = pA[hh // 2]
                o = (hh % 2) * 256
                for dc in range(2):
                    op("pe", lambda E: E.matmul(out=pt[:, o:o + 256], lhsT=qxT[:, 2 * hh + dc, ts_], rhs=KT[:, 2 * hh + dc, :],
                                                start=(dc == 0), stop=(dc == 1)),
                       reads=[B_qx[2 * hh + dc], B_KT], writes=[bpt])
            for i2 in range(2):
                pt, bpt = pA[i2]
                op("dve", lambda E: E.reduce_max(out=nmx[:, 2 * i2:2 * i2 + 2], in_=pt[:, :].rearrange("p (a b) -> p a b", b=256),
                                                 axis=AX.X), reads=[bpt], writes=[bx])
            op("dve", lambda E: E.tensor_scalar(out=nmx, in0=nmx, scalar1=-sc_, scalar2=None, op0=ALU.mult),
               reads=[bx], writes=[bx])
            for hh in range(4):
                pt, bpt = pA[hh // 2]
                o = (hh % 2) * 256
                op("act", lambda E: E.activation(out=P4[s][:, hh, :], in_=pt[:, o:o + 256], func=AF.Exp, scale=sc_,
                                                 bias=nmx[:, hh:hh + 1], accum_out=rsum[:, hh:hh + 1]),
                   reads=[bpt, bx], writes=[B_P4[s], bx])
            op("dve", lambda E: E.reciprocal(out=rsum, in_=rsum), reads=[bx], writes=[bx])
            op("dve", lambda E: E.tensor_tensor(out=P4[s][:, :, :], in0=P4[s][:, :, :],
                                                in1=rsum.unsqueeze(2).to_broadcast([128, 4, 256]), op=ALU.mult),
               reads=[B_P4[s], bx], writes=[B_P4[s]])

        def x_s2(s):
            k = s % 2
            ts_ = slice(s * 128, (s + 1) * 128)
            pT, bpT = bank()
            pTb = pT[:, :].bitcast(BF16)
            for hh in range(4):
                for mc in range(2):
                    i8 = hh * 2 + mc
                    op("pe", lambda E: E.transpose(out=pTb[:, i8 * 128:(i8 + 1) * 128], in_=P4[s][:, hh, mc * 128:(mc + 1) * 128],
                                                   identity=ident_b), reads=[B_P4[s], B_const], writes=[bpT])
            op("act", lambda E: E.activation(out=PT[k][:, :, :], in_=pTb[:, :].rearrange("p (a b) -> p a b", b=128), func=AF.Copy),
               reads=[bpT], writes=[B_PT[k]])
            pO = [bank(), bank()]
            for ch in range(8):
                hh = ch // 2
                pt, bpt = pO[ch // 4]
                o = (ch % 4) * 128
                for mc in range(2):
                    op("pe", lambda E: E.matmul(out=pt[:, o:o + 128], lhsT=Vx[:, mc, ch * 128:(ch + 1) * 128],
                                                rhs=PT[k][:, hh * 2 + mc, :], start=(mc == 0), stop=(mc == 1)),
                       reads=[B_Vx, B_PT[k]], writes=[bpt])
            for i2 in range(2):
                pt, bpt = pO[i2]
                eng = "act" if i2 else "dve"
                wr = [B_at[4 * i2 + q][s] for q in range(4)]
                if eng == "act":
                    op("act", lambda E: E.activation(out=attnT[:, 4 * i2:4 * i2 + 4, ts_],
                                                     in_=pt[:, :].rearrange("p (a b) -> p a b", b=128), func=AF.Copy),
                       reads=[bpt], writes=wr)
                else:
                    op("dve", lambda E: E.tensor_copy(out=attnT[:, 4 * i2:4 * i2 + 4, ts_],
                                                      in_=pt[:, :].rearrange("p (a b) -> p a b", b=128)),
                       reads=[bpt], writes=wr)

        for s in range(NSUB):
            x_s1(s)
        for s in range(NSUB):
            x_s2(s)
        linear_tm(dict(name="wo", w=wd["xattn_wo"]), 8, D, attnT, lambda j, s: [B_at[j][s]], None,
                  group_consume=lambda grp, acc: postnorm_group(grp, acc, h_t, h_bufs, 2, False, True))

    if need_x:
        xattn_setup()

    def load_x(it):
        h_t, h_bufs, h_sem = hbuf[it % 2]
        op("pool", lambda E: E.dma_start(out=h_t[:, :, :],
                                         in_=x_d[it * TT:(it + 1) * TT, :].rearrange("(s p) d -> p s d", p=128)),
           writes=h_bufs, dma_sem=h_sem)

    def store_out(it):
        h_t, h_bufs, h_sem = hbuf[it % 2]
        op("pool", lambda E: E.dma_start(out=out_d[it * TT:(it + 1) * TT, :].rearrange("(s p) d -> p s d", p=128),
                                         in_=h_t[:, :, :]),
           reads=h_bufs, dma_sem=h_sem)

    stage_list = [st_ for st_ in ("ffn1", "mix", "xattn", "ffn2") if st_ in stages]
    hoist = len(stage_list) == 4
    hoisted = set()
    load_x(0)
    for it in range(ntile):
        h_t, h_bufs, h_sem = hbuf[it % 2]
        pre_rs.clear()
        pre_scaled.clear()

        def deferred(it=it):
            if it >= 1:
                store_out(it - 1)
            if it >= 1 and it + 1 < ntile:
                load_x(it + 1)

        for si, st_ in enumerate(stage_list):
            nxt = si + 1 < len(stage_list)
            hook = deferred if si == 0 else None
            if st_ == "ffn1":
                ffn("ffn1", h_t, h_bufs, PV_PRE + 0, 0, nxt, after_prenorm=hook, skip_prenorm=(it in hoisted))
            elif st_ == "mix":
                mixer(h_t, h_bufs, after_prenorm=hook)
            elif st_ == "xattn":
                xattn(h_t, h_bufs, after_prenorm=hook)
            else:
                eh = mh = None
                if hoist and it >= 1 and it + 1 < ntile and si == len(stage_list) - 1:
                    nt, nb, _ = hbuf[(it + 1) % 2]
                    eh = lambda nt=nt, nb=nb: prenorm_early(nt, nb, list(range(NSUB)))
                    mh = lambda nt=nt, nb=nb: prenorm(nt, nb, PV_PRE + 0)
                    hoisted.add(it + 1)
                ffn("ffn2", h_t, h_bufs, PV_PRE + 24, 3, nxt, after_prenorm=hook, early_hook=eh, mid_hook=mh)
        flush_store()
        if it == 0 and ntile > 1:
            load_x(1)
    store_out(ntile - 1)
    kb.wait_all("pool", hbuf[0][1] + hbuf[1][1])
    return nc, kb


def _pack_params(inp):
    def col(v):
        v = np.asarray(v, np.float32).reshape(-1)
        return v.reshape(-1, 128).T
    pvec = np.zeros((128, PV_N), np.float32)
    for i, nm in enumerate(("ffn1_pre_g", "mix_pre_g", "xattn_pre_g", "ffn2_pre_g", "mem_norm_g")):
        pvec[:, PV_PRE + 8 * i:PV_PRE + 8 * i + 8] = col(inp[nm][0])
    cw = np.asarray(inp["conv_w"][0], np.float32)
    pvec[:, PV_CW:PV_CW + 4 * CK] = cw.T.reshape(4, 128, CK).transpose(1, 0, 2).reshape(128, 4 * CK)
    pvec[:, PV_CB:PV_CB + 4] = col(inp["conv_b"][0])
    pvec[:, PV_LG:PV_LG + 4] = col(inp["conv_ln_g"][0])
    pvec[:, PV_LB:PV_LB + 4] = col(inp["conv_ln_b"][0])
    qw = np.asarray(inp["qk_conv_w"][0], np.float32)
    pvec[:, PV_QW:PV_QW + 32] = qw.T.reshape(8, 128, QK).transpose(1, 0, 2).reshape(128, 32)
    pvec[:, PV_QB:PV_QB + 8] = col(inp["qk_conv_b"][0])
    pvec[0:4, PV_BI] = np.asarray(inp["b_igate"][0], np.float32)
    pvec[0:4, PV_BF] = np.asarray(inp["b_fgate"][0], np.float32)
    rgain = np.zeros((128, RG_N), np.float32)
    for i, nm in enumerate(("ffn1_post_g", "mix_post_g", "xattn_post_g", "ffn2_post_g")):
        rgain[:, RG_POST + i * D:RG_POST + (i + 1) * D] = np.asarray(inp[nm][0], np.float32)[None, :]
    rgain[:, RG_M:RG_M + MW] = np.asarray(inp["mlstm_norm_g"][0], np.float32)[None, :]
    consts = np.zeros((128, 1032), np.float32)
    consts[:, 512] = -0.5
    for hh in range(4):
        consts[hh, 520 + hh * 128:520 + (hh + 1) * 128] = 1.0
    consts[:, 0:128] = np.eye(128, dtype=np.float32)
    consts[:, 128:256] = np.triu(np.ones((128, 128), np.float32))
    consts[:, 256:384] = 1.0
    for hh in range(4):
        consts[hh, 384 + hh * 32:384 + (hh + 1) * 32] = 1.0
    return pvec, rgain, consts


_WNAMES = ("ffn1_w_gate", "ffn1_w_up", "ffn1_w_down", "w_in", "w_out", "xattn_wq", "xattn_wk",
           "xattn_wv", "xattn_wo", "ffn2_w_gate", "ffn2_w_up", "ffn2_w_down")


def make_in_maps(inp, ncores, seq):
    pvec, rgain, consts = _pack_params(inp)
    shared = {nm: np.ascontiguousarray(np.asarray(inp[nm], np.float32)[0]) for nm in _WNAMES}
    shared.update(pvec=pvec, rgain=rgain, consts=consts)
    maps = []
    for c in range(ncores):
        m = dict(shared)
        m["x"] = np.ascontiguousarray(np.asarray(inp["x"], np.float32)[c, :seq])
        m["mem"] = np.ascontiguousarray(np.asarray(inp["mem"], np.float32)[c])
        maps.append(m)
    return maps


def kernel(**inputs):
    nc, kb = build_program(SEQ)
    maps = make_in_maps(inputs, NCORES, SEQ)
    res = run_bass_kernel_spmd(nc, maps, core_ids=list(range(NCORES)))
    return np.stack([np.asarray(r["out"], np.float32) for r in res.results], axis=0)
```

```python
import math
import numpy as np
import ml_dtypes
import concourse.bass as bass
import concourse.mybir as mybir
from concourse.bass_utils import run_bass_kernel_spmd

F32 = mybir.dt.float32
BF16 = mybir.dt.bfloat16
AF = mybir.ActivationFunctionType
ALU = mybir.AluOpType
AX = mybir.AxisListType

D = 1024
DFF = 2816
NMEM = 256
CCH = 512
CK = 31
MH = 4
MW = 512
QK = 4
INC = 3080
XH = 4
XD = 256
EPS = 1e-6
TT = 512
NSUB = TT // 128
NCORES = 8
SEQ = 8192

PV_PRE = 0
PV_CW = 40
PV_CB = PV_CW + 4 * CK
PV_LG = PV_CB + 4
PV_LB = PV_LG + 4
PV_QW = PV_LB + 4
PV_QB = PV_QW + 32
PV_BI = PV_QB + 8
PV_BF = PV_BI + 1
PV_N = PV_BF + 1

RG_POST = 0
RG_M = 4 * D
RG_N = RG_M + MW


class Buf:
    __slots__ = ("name", "w", "r", "al", "cw")

    def __init__(self, name):
        self.name = name
        self.w = None
        self.r = {}
        self.al = ()
        self.cw = None


def alias_groups(ga, gb):
    for a in ga:
        a.al = tuple(a.al) + tuple(gb)
    for b in gb:
        b.al = tuple(b.al) + tuple(ga)


class KB:
    def __init__(self, nc):
        self.nc = nc
        self.engs = {"pe": nc.tensor, "act": nc.scalar, "dve": nc.vector,
                     "pool": nc.gpsimd, "sp": nc.sync}
        self.sems = {}
        self.cnt = {}
        self.known = {e: {} for e in self.engs}
        for e in ("pe", "act", "dve", "pool"):
            self.newsem(e)
        self.nins = 0
        self.nwait = 0

    def newsem(self, name):
        self.sems[name] = self.nc.alloc_semaphore("s_" + name)
        self.cnt[name] = 0
        return name

    def op(self, eng, fn, reads=(), writes=(), dma_sem=None):
        deps = {}

        def add(tag):
            if tag is None:
                return
            k, v = tag
            if deps.get(k, 0) < v:
                deps[k] = v

        for b in reads:
            add(b.w)
        for b0 in writes:
            for b in (b0,) + tuple(b0.al):
                if not (b.w is not None and b.w[0] == eng and dma_sem is None):
                    add(b.w)
                for k, v in b.r.items():
                    if k == eng and dma_sem is None:
                        continue
                    add((k, v))
        kn = self.known[eng]
        waits = []
        for k, v in deps.items():
            if eng == "pe" and k == "pe" and dma_sem is None:
                continue
            if kn.get(k, 0) < v:
                waits.append((k, v))
                kn[k] = v
        E = self.engs[eng]
        for k, v in waits[1:]:
            E.wait_ge(self.sems[k], v)
            self.nwait += 1
        ins = fn(E)
        if waits:
            k, v = waits[0]
            ins._wait_ge(self.sems[k], v)
        if dma_sem is None:
            key = eng
            self.cnt[key] += 1
            ins.then_inc(self.sems[key], 1)
        else:
            key = dma_sem
            self.cnt[key] += 16
            ins.then_inc(self.sems[key], 16)
        tag = (key, self.cnt[key])
        for b in reads:
            if b.r.get(key, 0) < tag[1]:
                b.r[key] = tag[1]
        for b0 in writes:
            b0.w = tag
            b0.r = {}
            for b in b0.al:
                b.w = tag
                b.r = {}
        self.nins += 1
        return ins

    def wait_all(self, eng, bufs):
        deps = {}
        for b in bufs:
            for tag in [b.w] + list(b.r.items()):
                if tag is None:
                    continue
                k, v = tag
                if deps.get(k, 0) < v:
                    deps[k] = v
        kn = self.known[eng]
        for k, v in deps.items():
            if kn.get(k, 0) < v:
                self.engs[eng].wait_ge(self.sems[k], v)
                kn[k] = v


def build_program(seq=SEQ, stages=("ffn1", "mix", "xattn", "ffn2")):
    assert seq % TT == 0
    ntile = seq // TT
    nc = bass.Bass("TRN2", target_bir_lowering=False)
    kb = KB(nc)
    op = kb.op

    def dram_in(name, shape, dt=F32):
        return nc.dram_tensor(name, list(shape), dt, kind="ExternalInput").ap()

    x_d = dram_in("x", [seq, D])
    mem_d = dram_in("mem", [NMEM, D])
    pv_d = dram_in("pvec", [128, PV_N])
    rg_d = dram_in("rgain", [128, RG_N])
    cst_d = dram_in("consts", [128, 1032])
    wd = {}
    for nm, shp in (("ffn1_w_gate", [D, DFF]), ("ffn1_w_up", [D, DFF]), ("ffn1_w_down", [DFF, D]),
                    ("w_in", [D, INC]), ("w_out", [D, D]),
                    ("xattn_wq", [D, D]), ("xattn_wk", [D, D]), ("xattn_wv", [D, D]), ("xattn_wo", [D, D]),
                    ("ffn2_w_gate", [D, DFF]), ("ffn2_w_up", [D, DFF]), ("ffn2_w_down", [DFF, D])):
        wd[nm] = dram_in(nm, shp)
    out_d = nc.dram_tensor("out", [seq, D], F32, kind="ExternalOutput").ap()

    def scratch(name, shape):
        return nc.dram_tensor(name, list(shape), BF16, kind="Internal").ap()

    sc = {}
    for f in ("ffn1", "ffn2"):
        sc[f + "_g"] = scratch(f + "_sg", [11, 128, 8, 256])
        sc[f + "_u"] = scratch(f + "_su", [11, 128, 8, 256])
        sc[f + "_d"] = scratch(f + "_sd", [22, 128, D])
    sc["in_fm"] = scratch("s_in_fm", [8, 128, 8, 256])
    sc["in_v"] = scratch("s_in_v", [8, 128, 512])
    sc["in_o"] = scratch("s_in_o", [8, 128, 512])
    sc["w_out"] = scratch("s_w_out", [8, 128, D])
    sc["wq"] = scratch("s_wq", [4, 128, 8, 256])
    sc["wk"] = scratch("s_wk", [4, 128, 8, 256])
    sc["wv"] = scratch("s_wv", [8, 128, D])
    sc["wo"] = scratch("s_wo", [8, 128, D])
    sc["qdiag"] = scratch("s_qdiag", [2, 128, 16, 128])
    sc["cdiag"] = scratch("s_cdiag", [8, 128, 16, 128])

    def sb(name, shape, dt):
        return nc.alloc_sbuf_tensor(name, list(shape), dt)

    pv = sb("pv", [128, PV_N], F32)
    rg = sb("rg", [128, RG_N], F32)
    cst = sb("cst", [128, 1032], F32)
    cstb = sb("cstb", [128, 512], BF16)
    B_const = Buf("const")
    ident_b = cstb[:, 0:128]
    mask_f = cst[:, 128:256]
    ident_f = cst[:, 0:128]
    ones_b = cstb[:, 256:384]
    sel_f = cst[:, 384:512]
    mhalf = cst[:, 512:513]
    cwh = sb("cwh", [128, 4 * CK], F32)

    NSLOT = 6
    ring = []
    for i in range(NSLOT):
        t = sb(f"wr{i}", [128, 2048], BF16)
        ring.append((t, Buf(f"wr{i}"), kb.newsem(f"wr{i}")))
    ring_i = [0]

    converted = {}
    stg_i = [0]
    pending_store = []
    jit_state = {}

    def flush_store():
        while pending_store:
            pending_store.pop(0)()

    def fetch(kind, name, idx, w_ap=None, col0=0, W=256, n=1, gcol=None, store=True):
        t, b, sm_ = ring[ring_i[0] % NSLOT]
        ring_i[0] += 1
        key = (name, idx)
        if kind == "cu":
            piece = sc[name][idx].rearrange("p k c -> p (k c)")
            ncols = 2048
        elif kind == "nat":
            piece = sc[name][idx:idx + n].rearrange("j p c -> p j c")
            ncols = n * W
        else:
            piece = sc[name][idx].rearrange("p m c -> p (m c)")
            ncols = 2048
        if key in converted:
            flush_store()
            op("sp", lambda E: E.dma_start(out=t[:, 0:ncols], in_=piece), reads=[converted[key]], writes=[b], dma_sem=sm_)
            return t, b
        bsc = Buf("sc_%s_%d" % (name, idx))
        converted[key] = bsc
        if kind == "diag":
            for mm in range(16):
                m = idx * 16 + mm
                wsrc, wn = (cwh, 4 * CK) if name == "cdiag" else (pv[:, PV_QW:PV_QW + 32], 32)
                if m < wn:
                    op("dve", lambda E: E.tensor_scalar(out=t[:, mm * 128:(mm + 1) * 128], in0=ident_f,
                                                        scalar1=wsrc[:, m:m + 1], scalar2=None, op0=ALU.mult),
                       reads=[B_const], writes=[b])
                else:
                    op("dve", lambda E: E.memset(t[:, mm * 128:(mm + 1) * 128], 0.0), writes=[b])
        else:
            i = stg_i[0] % 2
            stg_i[0] += 1
            stg_t, stg_b2, stg_s = jit_state["stg"][i]
            if kind == "cu":
                kk, cc = 8, 256
                src = w_ap[:, col0 + idx * 256:col0 + (idx + 1) * 256].rearrange("(k p) c -> p k c", p=128)
                g0 = 0
            else:
                kk, cc = n, W
                src = w_ap[idx * 128:(idx + n) * 128, col0:col0 + W].rearrange("(j p) c -> p j c", p=128)
                g0 = idx
            sview = stg_t[:, 0:kk * cc].rearrange("p (k c) -> p k c", c=cc)
            tview = t[:, 0:kk * cc].rearrange("p (k c) -> p k c", c=cc)
            op("sp", lambda E: E.dma_start(out=sview, in_=src), writes=[stg_b2], dma_sem=stg_s)
            ce = ("dve", "pool")[stg_i[0] % 2]
            if gcol is not None:
                gv = pv[:, gcol + g0:gcol + g0 + kk].unsqueeze(2).to_broadcast([128, kk, cc])
                op(ce, lambda E: E.tensor_tensor(out=tview, in0=sview, in1=gv, op=ALU.mult),
                   reads=[stg_b2, B_const], writes=[b])
            else:
                ce = ("act", "dve", "pool")[stg_i[0] % 3]
                if ce == "act":
                    op("act", lambda E: E.activation(out=tview, in_=sview, func=AF.Copy), reads=[stg_b2], writes=[b])
                else:
                    op(ce, lambda E: E.tensor_copy(out=tview, in_=sview), reads=[stg_b2], writes=[b])
        flush_store()
        if store:
            pending_store.append(lambda: op("sp", lambda E: E.dma_start(out=piece, in_=t[:, 0:ncols]),
                                            reads=[b], writes=[bsc], dma_sem=sm_))
        return t, b

    banks = []
    for i in range(8):
        t = nc.alloc_psum_tensor(f"ps{i}", [128, 512], F32)
        banks.append((t, Buf(f"ps{i}")))
    bank_i = [0]

    def bank():
        r = banks[bank_i[0] % 8]
        bank_i[0] += 1
        return r

    hbuf = []
    for i in range(2):
        t = sb(f"h{i}", [128, NSUB, D], F32)
        hbuf.append((t, [Buf(f"h{i}_{s}") for s in range(NSUB)], kb.newsem(f"h{i}")))
    xnT = sb("xnT", [128, 8, TT], BF16)
    B_xnT = [Buf(f"xnT{s}") for s in range(NSUB)]
    _stg = []
    for i in range(2):
        bst = Buf(f"stg{i}")
        alias_groups([bst], hbuf[1][1][2 * i:2 * i + 2])
        _stg.append((hbuf[1][0][:, 2 * i:2 * i + 2, :].rearrange("p a b -> p (a b)"), bst, kb.newsem(f"stg{i}")))
    jit_state["stg"] = _stg
    def carve(arena, off, shape, dt):
        n = 1
        for d_ in shape[1:]:
            n *= d_
        nb = n * (2 if dt == BF16 else 4)
        assert off % 4 == 0 and nb % 4 == 0
        a = arena[:, off // 4:(off + nb) // 4]
        if dt == BF16:
            a = a.bitcast(BF16)
        if len(shape) == 3:
            a = a.rearrange("p (a b) -> p a b", b=shape[2])
        elif len(shape) == 4:
            a = a.rearrange("p (a b c) -> p a b c", b=shape[2], c=shape[3])
        return a

    arA = sb("arenaA", [128, 29184 // 4], F32)
    arB = sb("arenaB", [128, 32768 // 4], F32)
    arC = sb("arenaC", [128, 31360 // 4], F32)
    hid = carve(arA, 0, [128, 22, TT], BF16)
    B_hid = [Buf(f"hid{j}") for j in range(22)]
    sg = carve(arA, 22528, [128, 2, TT], F32)
    B_sg = [Buf("sg0"), Buf("sg1")]
    zq = carve(arA, 0, [128, 8, 516], BF16)
    B_zq = [Buf(f"zq{g}") for g in range(8)]
    ubf = carve(arA, 16480, [128, 4, 512], F32)
    ub = carve(arA, 24672, [128, 4, 544], BF16)
    B_ubf = [Buf(f"ubf{g}") for g in range(4)]
    B_ub = [Buf(f"ub{g}") for g in range(4)]
    alias_groups(B_hid + B_sg, B_zq + B_ub + B_ubf)
    xs = sb("xs", [128, 2, D], BF16)
    B_xs = [Buf("xs0"), Buf("xs1")]
    junk = sb("junk", [128, 2, D], F32)
    B_jh = [[Buf("junk00"), Buf("junk01")], [Buf("junk10"), Buf("junk11")]]
    B_junk = [B_jh[0], B_jh[1]]
    yconv = carve(arB, 0, [128, 4, TT], F32)
    ybf = carve(arB, 8192, [128, 4, TT], BF16)
    ysq = carve(arB, 12288, [128, 4, TT], BF16)
    mixT = carve(arB, 16384, [128, 8, TT], BF16)
    qkT = carve(arB, 24576, [128, 8, TT], BF16)
    B_yc = [Buf(f"yc{g}") for g in range(4)]
    B_ybf = [Buf(f"ybf{g}") for g in range(4)]
    B_mixT = [[Buf(f"mixT{j}_{q}") for q in range(NSUB)] for j in range(8)]
    B_qk = [Buf(f"qk{g}") for g in range(8)]
    vaug = carve(arC, 0, [128, NSUB, 4, 129], BF16)
    og = carve(arC, 4128, [128, NSUB, 512], BF16)
    ktm = carve(arC, 8224, [128, NSUB, 4, 128], BF16)
    hn = carve(arC, 12320, [128, 1, 512], F32)
    hg = carve(arC, 14368, [128, 2, 512], BF16)
    WT = carve(arC, 16416, [128, 2, 512], BF16)
    gE = carve(arC, 18976, [128, 512], F32)
    gNB = carve(arC, 21024, [128, 512], F32)
    gA = carve(arC, 23072, [128, 512], F32)
    gM = carve(arC, 25120, [128, 512], F32)
    lnm = carve(arC, 27168, [128, 512], F32)
    lnr = carve(arC, 29216, [128, 512], F32)
    B_v = [Buf(f"v{q}") for q in range(NSUB)]
    B_og = [Buf(f"og{q}") for q in range(NSUB)]
    B_ktm = [Buf(f"ktm{q}") for q in range(NSUB)]
    B_hn = [Buf("hn0"), Buf("hn1")]
    B_hg = [Buf("hg0"), Buf("hg1")]
    B_WT = [Buf("WT0"), Buf("WT1")]
    B_gate = Buf("gate")
    B_ln = Buf("ln")
    B_lnr = Buf("lnr")
    qxT = carve(arC, 0, [128, 8, TT], BF16)
    attnT = carve(arC, 8192, [128, 8, TT], BF16)
    Pm = [carve(arC, 16384, [128, 4, 256], BF16), carve(arC, 26624, [128, 4, 256], BF16)]
    Pn = [carve(arC, 18432, [128, 4, 256], BF16), carve(arC, 28672, [128, 4, 256], BF16)]
    PT = [carve(arC, 20480, [128, 8, 128], BF16), carve(arC, 22528, [128, 8, 128], BF16)]
    memnT = carve(arC, 22528, [128, 8, 256], BF16)
    B_qx = [Buf(f"qx{j}") for j in range(8)]
    B_at = [[Buf(f"at{j}_{q}") for q in range(NSUB)] for j in range(8)]
    B_P = [Buf("P0"), Buf("P1")]
    B_Pn = [Buf("Pn0"), Buf("Pn1")]
    B_PT = [Buf("PT0"), Buf("PT1")]
    B_memn = [Buf("memn0"), Buf("memn1")]
    B_xst4 = [Buf(f"xst{q}") for q in range(NSUB)]
    C_mix = B_v + B_og + B_ktm + B_hn + B_hg + B_WT + [B_gate, B_ln, B_lnr]
    C_x = B_qx + [b for r_ in B_at for b in r_] + B_P + B_Pn + B_PT + B_memn
    alias_groups(C_mix, C_x)
    alias_groups([B_PT[1]], B_memn)
    Cst = sb("Cst", [128, 4, 129], F32)
    Cbf = sb("Cbf", [128, 4, 129], BF16)
    B_C = [Buf(f"C{h_}") for h_ in range(4)]
    B_Cbf = [Buf(f"Cbf{h_}") for h_ in range(4)]
    KT = sb("KT", [128, 8, 256], BF16)
    Vx = sb("Vx", [128, 2, D], BF16)
    B_KT = Buf("KT")
    B_Vx = Buf("Vx")
    zq_halo = sb("zq_halo", [128, 8, 4], BF16)
    u_halo = sb("u_halo", [128, 4, 30], BF16)
    B_zqh = [Buf(f"zqh{g}") for g in range(8)]
    B_uh = [Buf(f"uh{g}") for g in range(4)]
    wif_f = sb("wif_f", [128, 8, 8], F32)
    wif = sb("wif", [128, 8, 8], BF16)
    wfT = sb("wfT", [128, 32], F32)
    B_wfT = Buf("wfT")
    decB = sb("decB", [128, 16], F32)
    B_dec = Buf("decB")
    sm = sb("small", [128, 64], F32)
    B_sm = Buf("small")
    jdum = sb("jdum", [128, D], BF16)
    B_jdum = Buf("jdum")
    stat4 = sb("stat4", [128, 16], F32)
    B_stat4 = [Buf("stat4_0"), Buf("stat4_1")]
    junkb = sb("junkb", [128, 128], BF16)
    B_junkb = Buf("junkb")
    stat = sb("stat", [128, 64], F32)
    B_stat = [Buf(f"st{i}") for i in range(64)]
    stat_i = [0]

    def st():
        i = stat_i[0] % 64
        stat_i[0] += 1
        return stat[:, i:i + 1], B_stat[i]

    setup_sem = kb.newsem("setup")
    op("sp", lambda E: E.dma_start(out=pv[:, :], in_=pv_d), writes=[B_const], dma_sem=setup_sem)
    op("sp", lambda E: E.dma_start(out=rg[:, :], in_=rg_d), writes=[B_const], dma_sem=setup_sem)
    op("sp", lambda E: E.dma_start(out=cst[:, :], in_=cst_d), writes=[B_const], dma_sem=setup_sem)
    op("dve", lambda E: E.tensor_copy(out=cstb[:, :], in_=cst[:, 0:512]), reads=[B_const], writes=[B_const])
    op("dve", lambda E: E.tensor_scalar_mul(out=cwh[:, :], in0=pv[:, PV_CW:PV_CW + 4 * CK], scalar1=0.5),
       reads=[B_const], writes=[B_const])

    need_ffn1 = "ffn1" in stages
    need_ffn2 = "ffn2" in stages
    need_mix = "mix" in stages
    need_x = "xattn" in stages
    pre_rs = {}

    def prenorm_stats(h_t, h_bufs, subs):
        tmp = {}
        for s in subs:
            k = s % 2
            ss, bss = st()
            op("act", lambda E: E.activation(out=jdum[:, :], in_=h_t[:, s, :], func=AF.Square, accum_out=ss),
               reads=[h_bufs[s]], writes=[B_jdum, bss])
            tmp[s] = (ss, bss)
        for s in subs:
            ss, bss = tmp[s]
            rs, brs = st()
            op("dve", lambda E: E.tensor_scalar(out=rs, in0=ss, scalar1=D * EPS, scalar2=None, op0=ALU.add),
               reads=[bss], writes=[brs])
            pre_rs[s] = (rs, brs)
        for s in subs:
            rs, brs = pre_rs[s]
            op("pool", lambda E: E.tensor_tensor(out=rs, in0=rs, in1=mhalf, op=ALU.pow),
               reads=[brs, B_const], writes=[brs])

    pre_scaled = set()

    def prenorm_early(h_t, h_bufs, subs):
        prenorm_stats(h_t, h_bufs, subs)
        for s in subs:
            if s < 2:
                rs, brs = pre_rs[s]
                op("dve", lambda E: E.tensor_scalar(out=xs[:, s % 2, :], in0=h_t[:, s, :], scalar1=rs,
                                                    scalar2=math.sqrt(D), op0=ALU.mult, op1=ALU.mult),
                   reads=[h_bufs[s], brs], writes=[B_xs[s % 2]])
                pre_scaled.add(s)

    def prenorm(h_t, h_bufs, gcol, nsub=NSUB, dst=None, dst_bufs=None):
        dst = xnT if dst is None else dst
        dst_bufs = B_xnT if dst_bufs is None else dst_bufs
        todo = [s for s in range(nsub) if s not in pre_rs]
        if todo:
            prenorm_stats(h_t, h_bufs, todo)
        for w in range(0, nsub, 2):
            subs = list(range(w, min(w + 2, nsub)))
            pts = {}
            for s in subs:
                k = s % 2
                rs, brs = pre_rs.pop(s)
                if s in pre_scaled:
                    pre_scaled.discard(s)
                    continue
                op("dve", lambda E: E.tensor_scalar(out=xs[:, k, :], in0=h_t[:, s, :], scalar1=rs, scalar2=math.sqrt(D),
                                                    op0=ALU.mult, op1=ALU.mult), reads=[h_bufs[s], brs], writes=[B_xs[k]])
            for s in subs:
                k = s % 2
                pt, bpt = bank()
                ptb = pt[:, :].bitcast(BF16)
                pts[s] = (ptb, bpt)
                for kc in range(8):
                    op("pe", lambda E: E.transpose(out=ptb[:, kc * 128:(kc + 1) * 128],
                                                   in_=xs[:, k, kc * 128:(kc + 1) * 128], identity=ident_b),
                       reads=[B_xs[k], B_const], writes=[bpt])
            for s in subs:
                ptb, bpt = pts[s]
                dview = dst[:, :, s * 128:(s + 1) * 128]
                pview = ptb[:, :].rearrange("p (k c) -> p k c", c=128)
                if s % 2:
                    op("act", lambda E: E.activation(out=dview, in_=pview, func=AF.Copy), reads=[bpt], writes=[dst_bufs[s]])
                else:
                    op("dve", lambda E: E.tensor_copy(out=dview, in_=pview), reads=[bpt], writes=[dst_bufs[s]])

    def postnorm_group(grp, acc, h_t, h_bufs, gidx, half_scale, next_pre):
        cfac = math.sqrt(D) * (0.5 if half_scale else 1.0)
        g0 = rg[:, RG_POST + gidx * D:RG_POST + gidx * D + 512]
        g1 = rg[:, RG_POST + gidx * D + 512:RG_POST + (gidx + 1) * D]
        tmp = {}
        for s in grp:
            (p0, b0), (p1, b1) = acc[s]
            ss0, bs0 = st()
            ss1, bs1 = st()
            op("act", lambda E: E.activation(out=jdum[:, 0:512], in_=p0[:, :], func=AF.Square, accum_out=ss0),
               reads=[b0], writes=[B_jdum, bs0])
            op("act", lambda E: E.activation(out=jdum[:, 0:512], in_=p1[:, :], func=AF.Square, accum_out=ss1),
               reads=[b1], writes=[B_jdum, bs1])
            tmp[s] = (ss0, bs0, ss1, bs1)
        rss = {}
        for s in grp:
            ss0, bs0, ss1, bs1 = tmp[s]
            rs, brs = st()
            op("dve", lambda E: E.tensor_scalar(out=rs, in0=ss0, scalar1=ss1, scalar2=None, op0=ALU.add),
               reads=[bs0, bs1], writes=[brs])
            op("dve", lambda E: E.tensor_scalar(out=rs, in0=rs, scalar1=D * EPS, scalar2=1.0 / (cfac * cfac),
                                                op0=ALU.add, op1=ALU.mult), reads=[brs], writes=[brs])
            rss[s] = (rs, brs)
        for s in grp:
            rs, brs = rss[s]
            op("pool", lambda E: E.tensor_tensor(out=rs, in0=rs, in1=mhalf, op=ALU.pow),
               reads=[brs, B_const], writes=[brs])
        for s in grp:
            k = s % 2
            rs, brs = rss[s]
            (p0, b0), (p1, b1) = acc[s]
            for (p, b, g, lo) in ((p0, b0, g0, 0), (p1, b1, g1, 512)):
                bj = B_jh[k][lo // 512]
                op("dve", lambda E: E.scalar_tensor_tensor(out=junk[:, k, lo:lo + 512], in0=p[:, :], scalar=rs, in1=g,
                                                           op0=ALU.mult, op1=ALU.mult),
                   reads=[b, brs, B_const], writes=[bj])
                op("dve", lambda E: E.tensor_tensor(out=h_t[:, s, lo:lo + 512], in0=h_t[:, s, lo:lo + 512],
                                                    in1=junk[:, k, lo:lo + 512], op=ALU.add),
                   reads=[bj, h_bufs[s]], writes=[h_bufs[s]])
        if next_pre:
            prenorm_early(h_t, h_bufs, list(grp))

    def linear_fm(wspec, nunits, src, src_bufs, consume, ntok=TT):
        for u in range(nunits):
            wt, wb = fetch("cu", wspec["name"], u, w_ap=wspec["w"], col0=wspec.get("col0", 0),
                           gcol=wspec.get("gcol"), store=wspec.get("store", True))
            for cc in range(2):
                c = 2 * u + cc
                pt, bpt = bank()
                for kc in range(8):
                    op("pe", lambda E: E.matmul(out=pt[:, 0:ntok], lhsT=wt[:, kc * 256 + cc * 128:kc * 256 + cc * 128 + 128],
                                                rhs=src[:, kc, :], start=(kc == 0), stop=(kc == 7)),
                       reads=[wb] + list(src_bufs), writes=[bpt])
                consume(c, pt, bpt)

    def linear_tm(wspec, nj, W, lhs, lhs_bufs_fn, consume, subs_groups=((0, 1), (2, 3)), group_consume=None):
        per = 2048 // W
        nh = W // 512
        for grp in subs_groups:
            acc = {s: [bank() for _ in range(nh)] for s in grp}
            for j0 in range(0, nj, per):
                n = min(per, nj - j0)
                wt, wb = fetch("nat", wspec["name"], j0, w_ap=wspec["w"], col0=wspec.get("col0", 0), W=W, n=n,
                               gcol=wspec.get("gcol"), store=wspec.get("store", True))
                for jj in range(n):
                    j = j0 + jj
                    for s in grp:
                        for hh in range(nh):
                            pt, bpt = acc[s][hh]
                            op("pe", lambda E: E.matmul(out=pt[:, :], lhsT=lhs[:, j, s * 128:(s + 1) * 128],
                                                        rhs=wt[:, jj * W + hh * 512:jj * W + hh * 512 + 512],
                                                        start=(j == 0), stop=(j == nj - 1)),
                               reads=[wb] + lhs_bufs_fn(j, s), writes=[bpt])
            if group_consume is not None:
                group_consume(grp, acc)
            else:
                for s in grp:
                    consume(s, acc[s])

    def ffn(f, h_t, h_bufs, pre_col, post_idx, next_pre, after_prenorm=None, skip_prenorm=False,
            early_hook=None, mid_hook=None):
        if not skip_prenorm:
            prenorm(h_t, h_bufs, pre_col)
        if after_prenorm is not None:
            after_prenorm()
        for u in range(11):
            if u == 2 and early_hook is not None:
                early_hook()
            wg, bg = fetch("cu", f + "_g", u, w_ap=wd[f + "_w_gate"], gcol=pre_col)
            wu, bu = fetch("cu", f + "_u", u, w_ap=wd[f + "_w_up"], gcol=pre_col)
            for cc in range(2):
                j = 2 * u + cc
                pg, bpg = bank()
                pu, bpu = bank()
                for (wt, wb, pt, bpt) in ((wg, bg, pg, bpg), (wu, bu, pu, bpu)):
                    for kc in range(8):
                        op("pe", lambda E: E.matmul(out=pt[:, :], lhsT=wt[:, kc * 256 + cc * 128:kc * 256 + cc * 128 + 128],
                                                    rhs=xnT[:, kc, :], start=(kc == 0), stop=(kc == 7)),
                           reads=[wb] + B_xnT, writes=[bpt])
                k = j % 2
                op("act", lambda E: E.activation(out=sg[:, k, :], in_=pg[:, :], func=AF.Silu),
                   reads=[bpg], writes=[B_sg[k]])
                op("dve", lambda E: E.tensor_tensor(out=hid[:, j, :], in0=sg[:, k, :], in1=pu[:, :], op=ALU.mult),
                   reads=[B_sg[k], bpu], writes=[B_hid[j]])
        if mid_hook is not None:
            mid_hook()
        linear_tm(dict(name=f + "_d", w=wd[f + "_w_down"]), 22, D, hid, lambda j, s: [B_hid[j]], None,
                  group_consume=lambda grp, acc: postnorm_group(grp, acc, h_t, h_bufs, post_idx, True, next_pre))


    LN_DK = math.log(1.0 / math.sqrt(128.0))
    MU = sm[0:4, 0:5]
    NBc = sm[0:4, 8:9]
    Mc = sm[0:4, 9:10]
    negbf = sm[0:4, 10:11]
    dtmp = sm[0:4, 12:16]
    dec = sm[0:4, 16:20]
    if need_mix:
        op("pool", lambda E: E.memset(sm[:, :], 0.0), writes=[B_sm])
        op("pool", lambda E: E.memset(sm[:, 11:12], LN_DK), writes=[B_sm])
        op("pool", lambda E: E.memset(sm[:, 20:21], 1e-5), writes=[B_sm])
        op("pool", lambda E: E.memset(Cst[:, :, :], 0.0), writes=B_C)
        op("pool", lambda E: E.memset(zq_halo[:, :, :], 0.0), writes=B_zqh)
        op("pool", lambda E: E.memset(u_halo[:, :, :], 0.0), writes=B_uh)
        op("dve", lambda E: E.tensor_scalar(out=negbf, in0=pv[0:4, PV_BF:PV_BF + 1], scalar1=-1.0, scalar2=None,
                                            op0=ALU.mult), reads=[B_const, B_sm], writes=[B_sm])
        with nc.allow_non_contiguous_dma(reason="tiny gate weights"):
            op("sp", lambda E: E.dma_start(out=wif_f[:, :, :],
                                           in_=wd["w_in"][:, 3072:3080].rearrange("(k p) c -> p k c", p=128)),
               writes=[B_const], dma_sem=setup_sem)
        op("dve", lambda E: E.tensor_tensor(out=wif[:, :, :], in0=wif_f[:, :, :],
                                            in1=pv[:, PV_PRE + 8:PV_PRE + 16].unsqueeze(2).to_broadcast([128, 8, 8]),
                                            op=ALU.mult), reads=[B_const], writes=[B_const])

    def mixer(h_t, h_bufs, after_prenorm=None):
        prenorm(h_t, h_bufs, PV_PRE + 8)
        if after_prenorm is not None:
            after_prenorm()
        op("pool", lambda E: E.memset(vaug[:, :, :, 128:129], 1.0), writes=B_v)
        op("pool", lambda E: E.memset(lnm[0:4, :], 0.0), writes=[B_ln])
        def cons_v(s, acc):
            pt, bpt = acc[0]
            op("dve", lambda E: E.tensor_copy(out=vaug[:, s, :, 0:128],
                                              in_=pt[:, :].rearrange("p (a b) -> p a b", b=128)),
               reads=[bpt], writes=[B_v[s]])

        og_todo = []

        def og_finish():
            gm_ = rg[:, RG_M:RG_M + MW]
            for s in og_todo:
                op("pool", lambda E: E.tensor_tensor(out=og[:, s, :], in0=og[:, s, :], in1=gm_, op=ALU.mult),
                   reads=[B_og[s], B_const], writes=[B_og[s]])
                op("pool", lambda E: E.tensor_tensor(out=og[:, s, :], in0=og[:, s, :], in1=gm_, op=ALU.add),
                   reads=[B_og[s], B_const], writes=[B_og[s]])

        def cons_o(s, acc):
            pt, bpt = acc[0]
            op("act", lambda E: E.activation(out=og[:, s, :], in_=pt[:, :], func=AF.Tanh, scale=0.5),
               reads=[bpt], writes=[B_og[s]])
            og_todo.append(s)

        pi, bpi = bank()
        pf, bpf = bank()
        for (pt, bpt, c0) in ((pi, bpi, 0), (pf, bpf, 4)):
            for kc in range(8):
                op("pe", lambda E: E.matmul(out=pt[0:4, :], lhsT=wif[:, kc, c0:c0 + 4], rhs=xnT[:, kc, :],
                                            start=(kc == 0), stop=(kc == 7)),
                   reads=[B_const] + B_xnT, writes=[bpt])
        G = [B_gate, B_sm]
        op("act", lambda E: E.activation(out=gE[0:4, :], in_=pf[0:4, :], func=AF.Exp, scale=-1.0, bias=negbf),
           reads=[bpf, B_sm], writes=[B_gate])
        op("act", lambda E: E.activation(out=gE[0:4, :], in_=gE[0:4, :], func=AF.Ln, bias=1.0),
           reads=[B_gate], writes=[B_gate])
        op("dve", lambda E: E.tensor_tensor_scan(out=gNB[0:4, :], data0=gE[0:4, :], data1=lnm[0:4, :], initial=NBc,
                                                 op0=ALU.add, op1=ALU.add), reads=G + [B_ln], writes=[B_gate])
        op("dve", lambda E: E.scalar_tensor_tensor(out=gA[0:4, :], in0=pi[0:4, :], scalar=pv[0:4, PV_BI:PV_BI + 1],
                                                   in1=gNB[0:4, :], op0=ALU.add, op1=ALU.add),
           reads=[bpi, B_const] + G, writes=[B_gate])
        op("dve", lambda E: E.tensor_tensor_scan(out=gM[0:4, :], data0=gA[0:4, :], data1=gA[0:4, :], initial=Mc,
                                                 op0=ALU.max, op1=ALU.max), reads=G, writes=[B_gate])
        op("dve", lambda E: E.tensor_copy(out=sm[0:4, 0:1], in_=Mc), reads=G, writes=[B_sm])
        op("dve", lambda E: E.tensor_copy(out=sm[0:4, 1:5],
                                          in_=gM[0:4, :].rearrange("p (c t) -> p c t", t=128)[:, :, 127]),
           reads=G, writes=[B_sm])
        op("dve", lambda E: E.tensor_copy(out=Mc, in_=gM[0:4, 511:512]), reads=G, writes=[B_sm])
        op("dve", lambda E: E.tensor_copy(out=NBc, in_=gNB[0:4, 511:512]), reads=G, writes=[B_sm])
        mub = sm[0:4, 1:5].unsqueeze(2).to_broadcast([4, 4, 128])
        for arr in (gA, gNB):
            a3 = arr[0:4, :].rearrange("p (c t) -> p c t", t=128)
            op("dve", lambda E: E.tensor_tensor(out=a3, in0=a3, in1=mub, op=ALU.subtract), reads=G, writes=[B_gate])
        op("dve", lambda E: E.tensor_tensor(out=dtmp, in0=sm[0:4, 0:4], in1=sm[0:4, 1:5], op=ALU.subtract),
           reads=G, writes=[B_sm])
        op("act", lambda E: E.activation(out=gA[0:4, :], in_=gA[0:4, :], func=AF.Exp, bias=sm[0:4, 11:12]),
           reads=G, writes=[B_gate])
        op("act", lambda E: E.activation(out=gNB[0:4, :], in_=gNB[0:4, :], func=AF.Exp), reads=G, writes=[B_gate])
        op("act", lambda E: E.activation(out=dec, in_=dtmp, func=AF.Exp), reads=G, writes=[B_sm])
        linear_tm(dict(name="in_v", w=wd["w_in"], col0=2048, gcol=PV_PRE + 8), 8, 512, xnT, lambda j, s: [B_xnT[s]], cons_v, subs_groups=((0, 1, 2, 3),))
        linear_tm(dict(name="in_o", w=wd["w_in"], col0=2560, gcol=PV_PRE + 8), 8, 512, xnT, lambda j, s: [B_xnT[s]], cons_o, subs_groups=((0, 1, 2, 3),))
        def cons_z(c, pt, bpt):
            if c < 4:
                op("act", lambda E: E.activation(out=ubf[:, c, :], in_=pt[:, :], func=AF.Copy),
                   reads=[bpt], writes=[B_ubf[c]])
                op("pool", lambda E: E.tensor_copy(out=ub[:, c, 0:30], in_=u_halo[:, c, :]),
                   reads=[B_uh[c]], writes=[B_ub[c]])
            elif c < 8:
                g = c - 4
                k = g % 2
                op("act", lambda E: E.activation(out=junk[:, k, 0:512], in_=pt[:, :], func=AF.Tanh, scale=0.5),
                   reads=[bpt], writes=[B_jh[k][0]])
                op("dve", lambda E: E.scalar_tensor_tensor(out=ub[:, g, 30:542], in0=junk[:, k, 0:512], scalar=1.0,
                                                           in1=ubf[:, g, :], op0=ALU.add, op1=ALU.mult),
                   reads=[B_jh[k][0], B_ubf[g]], writes=[B_ub[g]])
                op("pool", lambda E: E.tensor_copy(out=u_halo[:, g, :], in_=ub[:, g, 512:542]),
                   reads=[B_ub[g]], writes=[B_uh[g]])
            else:
                gq = c - 8
                op("act", lambda E: E.activation(out=zq[:, gq, 3:515], in_=pt[:, :], func=AF.Copy),
                   reads=[bpt], writes=[B_zq[gq]])
                op("pool", lambda E: E.tensor_copy(out=zq[:, gq, 0:3], in_=zq_halo[:, gq, 0:3]),
                   reads=[B_zqh[gq]], writes=[B_zq[gq]])
                op("pool", lambda E: E.tensor_copy(out=zq_halo[:, gq, 0:3], in_=zq[:, gq, 512:515]),
                   reads=[B_zq[gq]], writes=[B_zqh[gq]])
                qpend.append(gq)
                if len(qpend) > 2:
                    emit_qconv(qpend.pop(0))

        qpend = []
        qslots = {}

        def emit_qconv(gq):
            if gq // 4 not in qslots:
                qslots[gq // 4] = fetch("diag", "qdiag", gq // 4)
            wt, wb = qslots[gq // 4]
            pt2, bpt2 = bank()
            for j in range(QK):
                m = (gq * QK + j) % 16
                op("pe", lambda E: E.matmul(out=pt2[:, :], lhsT=wt[:, m * 128:(m + 1) * 128], rhs=zq[:, gq, j:j + 512],
                                            start=(j == 0), stop=(j == QK - 1)), reads=[wb, B_zq[gq]], writes=[bpt2])
            op("act", lambda E: E.activation(out=qkT[:, gq, :], in_=pt2[:, :], func=AF.Silu,
                                             bias=pv[:, PV_QB + gq:PV_QB + gq + 1]),
               reads=[bpt2, B_const], writes=[B_qk[gq]])

        linear_fm(dict(name="in_fm", w=wd["w_in"], gcol=PV_PRE + 8), 8, xnT, B_xnT, cons_z)
        while qpend:
            emit_qconv(qpend.pop(0))
        pw, bpw = bank()
        for s in range(NSUB):
            op("pe", lambda E: E.matmul(out=pw[:, s * 8:s * 8 + 4], lhsT=gA[0:4, s * 128:(s + 1) * 128],
                                        rhs=ident_f[0:4, 0:4], start=True, stop=True),
               reads=[B_gate, B_const], writes=[bpw])
            op("pe", lambda E: E.matmul(out=pw[:, s * 8 + 4:s * 8 + 8], lhsT=gNB[0:4, s * 128:(s + 1) * 128],
                                        rhs=ident_f[0:4, 0:4], start=True, stop=True),
               reads=[B_gate, B_const], writes=[bpw])
        for hh in range(4):
            op("pe", lambda E: E.matmul(out=pw[:, 32 + hh * 4:32 + hh * 4 + 4], lhsT=cst[0:4, 520 + hh * 128:520 + (hh + 1) * 128],
                                        rhs=dec, start=True, stop=True), reads=[B_sm, B_const], writes=[bpw])
        op("dve", lambda E: E.tensor_copy(out=wfT[:, :], in_=pw[:, 0:32]), reads=[bpw], writes=[B_wfT])
        op("dve", lambda E: E.tensor_copy(out=decB[:, :], in_=pw[:, 32:48]), reads=[bpw], writes=[B_dec])

        dslots = {}
        for g in range(4):
            pt, bpt = bank()
            for j in range(CK):
                m = g * CK + j
                if m // 16 not in dslots:
                    dslots[m // 16] = fetch("diag", "cdiag", m // 16)
                wt, wb = dslots[m // 16]
                op("pe", lambda E: E.matmul(out=pt[:, :], lhsT=wt[:, (m % 16) * 128:(m % 16 + 1) * 128],
                                            rhs=ub[:, g, j:j + 512], start=(j == 0), stop=(j == CK - 1)),
                   reads=[wb, B_ub[g]], writes=[bpt])
            cb = pv[:, PV_CB + g:PV_CB + g + 1]
            op("act", lambda E: E.activation(out=yconv[:, g, :], in_=pt[:, :], func=AF.Identity, bias=cb),
               reads=[bpt, B_const], writes=[B_yc[g]])
            op("act", lambda E: E.activation(out=ybf[:, g, :], in_=pt[:, :], func=AF.Identity, bias=cb),
               reads=[bpt, B_const], writes=[B_ybf[g]])
            op("act", lambda E: E.activation(out=ysq[:, g, :], in_=pt[:, :], func=AF.Square, bias=cb),
               reads=[bpt, B_const], writes=[B_ybf[g]])

        og_finish()
        ctx = {}
        rot = [0]

        def sbank():
            r = banks[4 + rot[0] % 4]
            rot[0] += 1
            return r

        def m_p1(s):
            ts_ = slice(s * 128, (s + 1) * 128)
            k2 = s % 2
            wg_s = wfT[:, s * 8:s * 8 + 4]
            WTk = WT[:, k2, :].rearrange("p (a b) -> p a b", b=128)
            pk, bpk = sbank()
            pkb = pk[:, :].bitcast(BF16)
            for hh in range(4):
                op("pe", lambda E: E.transpose(out=pkb[:, hh * 128:(hh + 1) * 128], in_=qkT[:, 4 + hh, ts_],
                                               identity=ident_b), reads=[B_qk[4 + hh], B_const], writes=[bpk])
            for hh in range(4):
                op("act", lambda E: E.activation(out=ktm[:, s, hh, :], in_=pkb[:, hh * 128:(hh + 1) * 128], func=AF.Copy,
                                                 scale=wfT[:, s * 8 + hh:s * 8 + hh + 1]),
                   reads=[bpk, B_wfT], writes=[B_ktm[s]])
            pS, bpS = sbank()
            for hh in range(4):
                op("pe", lambda E: E.matmul(out=pS[:, hh * 128:(hh + 1) * 128], lhsT=qkT[:, 4 + hh, ts_],
                                            rhs=qkT[:, hh, ts_], start=True, stop=True),
                   reads=[B_qk[4 + hh], B_qk[hh]], writes=[bpS])
            op("dve", lambda E: E.tensor_tensor(out=WTk, in0=pS[:, :].rearrange("p (a b) -> p a b", b=128),
                                                in1=wg_s.unsqueeze(2).to_broadcast([128, 4, 128]), op=ALU.mult),
               reads=[bpS, B_wfT], writes=[B_WT[k2]])
            op("pool", lambda E: E.tensor_tensor(out=WTk, in0=WTk, in1=mask_f.unsqueeze(1).to_broadcast([128, 4, 128]),
                                                 op=ALU.mult), reads=[B_WT[k2], B_const], writes=[B_WT[k2]])

        def m_p2a(s):
            ts_ = slice(s * 128, (s + 1) * 128)
            k2 = s % 2
            dec_s = decB[:, :].rearrange("p (h c) -> p h c", c=4)[:, :, s]
            WTk = WT[:, k2, :].rearrange("p (a b) -> p a b", b=128)
            op("dve", lambda E: E.tensor_tensor(out=Cst[:, :, :], in0=Cst[:, :, :],
                                                in1=dec_s.unsqueeze(2).to_broadcast([128, 4, 129]), op=ALU.mult),
               reads=B_C + [B_dec], writes=B_C)
            op("act", lambda E: E.activation(out=Cbf[:, :, :], in_=Cst[:, :, :], func=AF.Copy), reads=B_C, writes=B_Cbf)

        def m_p2mm(s):
            ts_ = slice(s * 128, (s + 1) * 128)
            k2 = s % 2
            WTk = WT[:, k2, :].rearrange("p (a b) -> p a b", b=128)
            pNt, bpN = banks[k2]
            pXt, bpX = banks[2 + k2]
            pCt, bpC = sbank()
            for hh in range(4):
                hs = slice(hh * 128, (hh + 1) * 128)
                op("pe", lambda E: E.matmul(out=pNt[:, hs], lhsT=WTk[:, hh, :], rhs=vaug[:, s, hh, 0:128],
                                            start=True, stop=False), reads=[B_WT[k2], B_v[s]], writes=[bpN])
                op("pe", lambda E: E.matmul(out=pNt[:, hs], lhsT=qkT[:, hh, ts_], rhs=Cbf[:, hh, 0:128],
                                            start=False, stop=True), reads=[B_qk[hh], B_Cbf[hh]], writes=[bpN])
                op("pe", lambda E: E.matmul(out=pXt[:, hh:hh + 1], lhsT=WTk[:, hh, :], rhs=vaug[:, s, hh, 128:129],
                                            start=True, stop=False), reads=[B_WT[k2], B_v[s]], writes=[bpX])
                op("pe", lambda E: E.matmul(out=pXt[:, hh:hh + 1], lhsT=qkT[:, hh, ts_], rhs=Cbf[:, hh, 128:129],
                                            start=False, stop=True), reads=[B_qk[hh], B_Cbf[hh]], writes=[bpX])
            for hh in range(4):
                hs = slice(hh * 128, (hh + 1) * 128)
                op("pe", lambda E: E.matmul(out=pCt[:, hs], lhsT=ktm[:, s, hh, :], rhs=vaug[:, s, hh, 0:128],
                                            start=True, stop=True), reads=[B_ktm[s], B_v[s]], writes=[bpC])
                op("pe", lambda E: E.matmul(out=pXt[:, 4 + hh:5 + hh], lhsT=ktm[:, s, hh, :], rhs=vaug[:, s, hh, 128:129],
                                            start=True, stop=True), reads=[B_ktm[s], B_v[s]], writes=[bpX])
            ctx[s] = (pNt, bpN, pXt, bpX, pCt, bpC)

        def m_p2b(s):
            pNt, bpN, pXt, bpX, pCt, bpC = ctx[s]
            op("dve", lambda E: E.tensor_tensor(out=Cst[:, :, 0:128], in0=Cst[:, :, 0:128],
                                                in1=pCt[:, :].rearrange("p (a b) -> p a b", b=128), op=ALU.add),
               reads=B_C + [bpC], writes=B_C)
            op("dve", lambda E: E.tensor_tensor(out=Cst[:, :, 128], in0=Cst[:, :, 128], in1=pXt[:, 4:8], op=ALU.add),
               reads=B_C + [bpX], writes=B_C)

        def m_n1(s):
            k2 = s % 2
            pNt, bpN, pXt, bpX, pCt, bpC = ctx[s]
            fl_s = wfT[:, s * 8 + 4:s * 8 + 8]
            ad = stat4[:, (2 * k2) * 4:(2 * k2) * 4 + 4]
            ssq = stat4[:, (2 * k2 + 1) * 4:(2 * k2 + 1) * 4 + 4]
            bq = B_stat4[k2]
            op("act", lambda E: E.activation(out=ad, in_=pXt[:, 0:4], func=AF.Abs), reads=[bpX], writes=[bq])
            for hh in range(4):
                op("act", lambda E: E.activation(out=junkb[:, :], in_=pNt[:, hh * 128:(hh + 1) * 128], func=AF.Square,
                                                 accum_out=ssq[:, hh:hh + 1]), reads=[bpN], writes=[B_junkb, bq])
            op("dve", lambda E: E.tensor_tensor(out=ad, in0=ad, in1=fl_s, op=ALU.max), reads=[bq, B_wfT], writes=[bq])
            op("dve", lambda E: E.reciprocal(out=ad, in_=ad), reads=[bq], writes=[bq])
            op("dve", lambda E: E.tensor_tensor(out=ssq, in0=ssq, in1=ad, op=ALU.mult), reads=[bq], writes=[bq])
            op("dve", lambda E: E.tensor_tensor(out=ssq, in0=ssq, in1=ad, op=ALU.mult), reads=[bq], writes=[bq])
            op("dve", lambda E: E.tensor_scalar(out=ssq, in0=ssq, scalar1=4.0 / 128, scalar2=4.0 * EPS,
                                                op0=ALU.mult, op1=ALU.add), reads=[bq], writes=[bq])
            op("pool", lambda E: E.tensor_tensor(out=ssq, in0=ssq, in1=mhalf.to_broadcast([128, 4]), op=ALU.pow),
               reads=[bq, B_const], writes=[bq])

        def m_n2a(s):
            k2 = s % 2
            pNt, bpN, pXt, bpX, pCt, bpC = ctx[s]
            ad = stat4[:, (2 * k2) * 4:(2 * k2) * 4 + 4]
            ssq = stat4[:, (2 * k2 + 1) * 4:(2 * k2 + 1) * 4 + 4]
            bq = B_stat4[k2]
            op("dve", lambda E: E.tensor_tensor(out=ssq, in0=ssq, in1=ad, op=ALU.mult), reads=[bq], writes=[bq])
            op("dve", lambda E: E.tensor_tensor(out=hn[:, 0, :].rearrange("p (a b) -> p a b", b=128),
                                                in0=pNt[:, :].rearrange("p (a b) -> p a b", b=128),
                                                in1=ssq.unsqueeze(2).to_broadcast([128, 4, 128]), op=ALU.mult),
               reads=[bpN, bq], writes=[B_hn[0]])
            op("pool", lambda E: E.tensor_tensor(out=hg[:, k2, :], in0=hn[:, 0, :], in1=og[:, s, :], op=ALU.mult),
               reads=[B_hn[0], B_og[s]], writes=[B_hg[k2]])

        def m_n2b(s):
            ts_ = slice(s * 128, (s + 1) * 128)
            k2 = s % 2
            ph, bph = sbank()
            phb = ph[:, :].bitcast(BF16)
            for hh in range(4):
                op("pe", lambda E: E.transpose(out=phb[:, hh * 128:(hh + 1) * 128], in_=hg[:, k2, hh * 128:(hh + 1) * 128],
                                               identity=ident_b), reads=[B_hg[k2], B_const], writes=[bph])
            op("act", lambda E: E.activation(out=mixT[:, 4:8, ts_], in_=phb[:, 0:512].rearrange("p (a b) -> p a b", b=128),
                                             func=AF.Copy), reads=[bph], writes=[B_mixT[4 + hh_][s] for hh_ in range(4)])

        m_p1(0)
        m_p1(1)
        m_p2a(0)
        m_p2mm(0)
        pm, bpm = sbank()
        pe2, bpe2 = sbank()
        for g in range(4):
            op("pe", lambda E: E.matmul(out=pm[:, :], lhsT=ones_b, rhs=ybf[:, g, :], start=(g == 0), stop=(g == 3)),
               reads=[B_ybf[g], B_const], writes=[bpm])
        for g in range(4):
            op("pe", lambda E: E.matmul(out=pe2[:, :], lhsT=ones_b, rhs=ysq[:, g, :], start=(g == 0), stop=(g == 3)),
               reads=[B_ybf[g], B_const], writes=[bpe2])
        op("dve", lambda E: E.tensor_scalar(out=lnm, in0=pm[:, :], scalar1=1.0 / CCH, scalar2=None, op0=ALU.mult),
           reads=[bpm], writes=[B_ln])
        op("act", lambda E: E.activation(out=lnr, in_=pm[:, :], func=AF.Square, scale=1.0 / CCH),
           reads=[bpm], writes=[B_lnr])
        op("dve", lambda E: E.scalar_tensor_tensor(out=lnr, in0=pe2[:, :], scalar=1.0 / CCH, in1=lnr,
                                                   op0=ALU.mult, op1=ALU.subtract), reads=[bpe2, B_lnr], writes=[B_lnr])
        op("act", lambda E: E.activation(out=lnr, in_=lnr, func=AF.Sqrt, bias=sm[:, 20:21]), reads=[B_lnr, B_sm], writes=[B_lnr])
        op("dve", lambda E: E.reciprocal(out=lnr, in_=lnr), reads=[B_lnr], writes=[B_lnr])
        def ln_group(g):
            op("dve", lambda E: E.tensor_tensor(out=yconv[:, g, :], in0=yconv[:, g, :], in1=lnm, op=ALU.subtract),
               reads=[B_yc[g], B_ln], writes=[B_yc[g]])
            op("dve", lambda E: E.tensor_tensor(out=yconv[:, g, :], in0=yconv[:, g, :], in1=lnr, op=ALU.mult),
               reads=[B_yc[g], B_lnr], writes=[B_yc[g]])
            op("act", lambda E: E.activation(out=mixT[:, g, :], in_=yconv[:, g, :], func=AF.Silu,
                                             scale=pv[:, PV_LG + g:PV_LG + g + 1], bias=pv[:, PV_LB + g:PV_LB + g + 1]),
               reads=[B_yc[g], B_const], writes=B_mixT[g])

        for k in range(NSUB):
            m_p2b(k)
            if k + 1 < NSUB:
                m_p2a(k + 1)
            if k >= 1:
                m_n2a(k - 1)
            if k + 2 < NSUB:
                m_p1(k + 2)
            if k + 1 < NSUB:
                m_p2mm(k + 1)
            m_n1(k)
            ln_group(k)
            if k >= 1:
                m_n2b(k - 1)
        m_n2a(NSUB - 1)
        m_n2b(NSUB - 1)
        linear_tm(dict(name="w_out", w=wd["w_out"]), 8, D, mixT, lambda j, s: [B_mixT[j][s]], None,
                  group_consume=lambda grp, acc: postnorm_group(grp, acc, h_t, h_bufs, 1, False, True))

    def xattn_setup():
        mt, mb, msem = hbuf[0]
        op("pool", lambda E: E.dma_start(out=mt[:, 0:2, :], in_=mem_d.rearrange("(s p) d -> p s d", p=128)),
           writes=mb[0:2], dma_sem=msem)
        prenorm(mt, mb, PV_PRE + 32, nsub=2, dst=memnT, dst_bufs=B_memn)

        def cons_k(c, pt, bpt):
            op("dve", lambda E: E.tensor_copy(out=KT[:, c, :], in_=pt[:, 0:256]), reads=[bpt], writes=[B_KT])

        linear_fm(dict(name="wk", w=wd["xattn_wk"], gcol=PV_PRE + 32, store=False), 4, memnT, B_memn, cons_k, ntok=256)

        def cons_vx(s, acc):
            for hf in range(2):
                pt, bpt = acc[hf]
                op("act", lambda E: E.activation(out=Vx[:, s, hf * 512:(hf + 1) * 512], in_=pt[:, :], func=AF.Copy),
                   reads=[bpt], writes=[B_Vx])

        linear_tm(dict(name="wv", w=wd["xattn_wv"], gcol=PV_PRE + 32, store=False), 8, D, memnT, lambda j, s: [B_memn[s]], cons_vx, subs_groups=((0, 1),))

    def xattn(h_t, h_bufs, after_prenorm=None):
        prenorm(h_t, h_bufs, PV_PRE + 16)
        if after_prenorm is not None:
            after_prenorm()

        def cons_q(c, pt, bpt):
            if c % 2:
                op("act", lambda E: E.activation(out=qxT[:, c, :], in_=pt[:, :], func=AF.Copy), reads=[bpt], writes=[B_qx[c]])
            else:
                op("dve", lambda E: E.tensor_copy(out=qxT[:, c, :], in_=pt[:, :]), reads=[bpt], writes=[B_qx[c]])

        linear_fm(dict(name="wq", w=wd["xattn_wq"], gcol=PV_PRE + 16), 4, xnT, B_xnT, cons_q)
        sc_ = 1.0 / math.sqrt(XD)

        P4 = [Pm[0], Pn[0], Pm[1], Pn[1]]
        B_P4 = [B_P[0], B_Pn[0], B_P[1], B_Pn[1]]

        def x_s1(s):
            nmx = sm[:, 24 + 8 * s:28 + 8 * s]
            rsum = sm[:, 28 + 8 * s:32 + 8 * s]
            bx = B_xst4[s]
            ts_ = slice(s * 128, (s + 1) * 128)
            pA = [bank(), bank()]
            for hh in range(4):
                pt, bpt = pA[hh // 2]
                o = (hh % 2) * 256
                for dc in range(2):
                    op("pe", lambda E: E.matmul(out=pt[:, o:o + 256], lhsT=qxT[:, 2 * hh + dc, ts_], rhs=KT[:, 2 * hh + dc, :],
                                                start=(dc == 0), stop=(dc == 1)),
                       reads=[B_qx[2 * hh + dc], B_KT], writes=[bpt])
            for i2 in range(2):
                pt, bpt = pA[i2]
                op("dve", lambda E: E.reduce_max(out=nmx[:, 2 * i2:2 * i2 + 2], in_=pt[:, :].rearrange("p (a b) -> p a b", b=256),
                                                 axis=AX.X), reads=[bpt], writes=[bx])
            op("dve", lambda E: E.tensor_scalar(out=nmx, in0=nmx, scalar1=-sc_, scalar2=None, op0=ALU.mult),
               reads=[bx], writes=[bx])
            for hh in range(4):
                pt, bpt = pA[hh // 2]
                o = (hh % 2) * 256
                op("act", lambda E: E.activation(out=P4[s][:, hh, :], in_=pt[:, o:o + 256], func=AF.Exp, scale=sc_,
                                                 bias=nmx[:, hh:hh + 1], accum_out=rsum[:, hh:hh + 1]),
                   reads=[bpt, bx], writes=[B_P4[s], bx])
            op("dve", lambda E: E.reciprocal(out=rsum, in_=rsum), reads=[bx], writes=[bx])
            op("dve", lambda E: E.tensor_tensor(out=P4[s][:, :, :], in0=P4[s][:, :, :],
                                                in1=rsum.unsqueeze(2).to_broadcast([128, 4, 256]), op=ALU.mult),
               reads=[B_P4[s], bx], writes=[B_P4[s]])

        def x_s2(s):
            k = s % 2
            ts_ = slice(s * 128, (s + 1) * 128)
            pT, bpT = bank()
            pTb = pT[:, :].bitcast(BF16)
            for hh in range(4):
                for mc in range(2):
                    i8 = hh * 2 + mc
                    op("pe", lambda E: E.transpose(out=pTb[:, i8 * 128:(i8 + 1) * 128], in_=P4[s][:, hh, mc * 128:(mc + 1) * 128],
                                                   identity=ident_b), reads=[B_P4[s], B_const], writes=[bpT])
            op("act", lambda E: E.activation(out=PT[k][:, :, :], in_=pTb[:, :].rearrange("p (a b) -> p a b", b=128), func=AF.Copy),
               reads=[bpT], writes=[B_PT[k]])
            pO = [bank(), bank()]
            for ch in range(8):
                hh = ch // 2
                pt, bpt = pO[ch // 4]
                o = (ch % 4) * 128
                for mc in range(2):
                    op("pe", lambda E: E.matmul(out=pt[:, o:o + 128], lhsT=Vx[:, mc, ch * 128:(ch + 1) * 128],
                                                rhs=PT[k][:, hh * 2 + mc, :], start=(mc == 0), stop=(mc == 1)),
                       reads=[B_Vx, B_PT[k]], writes=[bpt])
            for i2 in range(2):
                pt, bpt = pO[i2]
                eng = "act" if i2 else "dve"
                wr = [B_at[4 * i2 + q][s] for q in range(4)]
                if eng == "act":
                    op("act", lambda E: E.activation(out=attnT[:, 4 * i2:4 * i2 + 4, ts_],
                                                     in_=pt[:, :].rearrange("p (a b) -> p a b", b=128), func=AF.Copy),
                       reads=[bpt], writes=wr)
                else:
                    op("dve", lambda E: E.tensor_copy(out=attnT[:, 4 * i2:4 * i2 + 4, ts_],
                                                      in_=pt[:, :].rearrange("p (a b) -> p a b", b=128)),
                       reads=[bpt], writes=wr)

        x_s1(0)
        x_s1(1)
        for s in range(NSUB):
            if s + 2 < NSUB:
                x_s1(s + 2)
            x_s2(s)
        linear_tm(dict(name="wo", w=wd["xattn_wo"]), 8, D, attnT, lambda j, s: [B_at[j][s]], None,
                  group_consume=lambda grp, acc: postnorm_group(grp, acc, h_t, h_bufs, 2, False, True))

    if need_x:
        xattn_setup()

    def load_x(it):
        h_t, h_bufs, h_sem = hbuf[it % 2]
        op("pool", lambda E: E.dma_start(out=h_t[:, :, :],
                                         in_=x_d[it * TT:(it + 1) * TT, :].rearrange("(s p) d -> p s d", p=128)),
           writes=h_bufs, dma_sem=h_sem)

    def store_out(it):
        h_t, h_bufs, h_sem = hbuf[it % 2]
        op("pool", lambda E: E.dma_start(out=out_d[it * TT:(it + 1) * TT, :].rearrange("(s p) d -> p s d", p=128),
                                         in_=h_t[:, :, :]),
           reads=h_bufs, dma_sem=h_sem)

    stage_list = [st_ for st_ in ("ffn1", "mix", "xattn", "ffn2") if st_ in stages]
    hoist = len(stage_list) == 4
    hoisted = set()
    load_x(0)
    for it in range(ntile):
        h_t, h_bufs, h_sem = hbuf[it % 2]
        pre_rs.clear()
        pre_scaled.clear()

        def deferred(it=it):
            if it >= 1:
                store_out(it - 1)
            if it >= 1 and it + 1 < ntile:
                load_x(it + 1)

        for si, st_ in enumerate(stage_list):
            nxt = si + 1 < len(stage_list)
            hook = deferred if si == 0 else None
            if st_ == "ffn1":
                ffn("ffn1", h_t, h_bufs, PV_PRE + 0, 0, nxt, after_prenorm=hook, skip_prenorm=(it in hoisted))
            elif st_ == "mix":
                mixer(h_t, h_bufs, after_prenorm=hook)
            elif st_ == "xattn":
                xattn(h_t, h_bufs, after_prenorm=hook)
            else:
                eh = mh = None
                if hoist and it >= 1 and it + 1 < ntile and si == len(stage_list) - 1:
                    nt, nb, _ = hbuf[(it + 1) % 2]
                    eh = lambda nt=nt, nb=nb: prenorm_early(nt, nb, list(range(NSUB)))
                    mh = lambda nt=nt, nb=nb: prenorm(nt, nb, PV_PRE + 0)
                    hoisted.add(it + 1)
                ffn("ffn2", h_t, h_bufs, PV_PRE + 24, 3, nxt, after_prenorm=hook, early_hook=eh, mid_hook=mh)
        flush_store()
        if it == 0 and ntile > 1:
            load_x(1)
    store_out(ntile - 1)
    kb.wait_all("pool", hbuf[0][1] + hbuf[1][1])
    return nc, kb


def _pack_params(inp):
    def col(v):
        v = np.asarray(v, np.float32).reshape(-1)
        return v.reshape(-1, 128).T
    pvec = np.zeros((128, PV_N), np.float32)
    for i, nm in enumerate(("ffn1_pre_g", "mix_pre_g", "xattn_pre_g", "ffn2_pre_g", "mem_norm_g")):
        pvec[:, PV_PRE + 8 * i:PV_PRE + 8 * i + 8] = col(inp[nm][0])
    cw = np.asarray(inp["conv_w"][0], np.float32)
    pvec[:, PV_CW:PV_CW + 4 * CK] = cw.T.reshape(4, 128, CK).transpose(1, 0, 2).reshape(128, 4 * CK)
    pvec[:, PV_CB:PV_CB + 4] = col(inp["conv_b"][0])
    pvec[:, PV_LG:PV_LG + 4] = col(inp["conv_ln_g"][0])
    pvec[:, PV_LB:PV_LB + 4] = col(inp["conv_ln_b"][0])
    qw = np.asarray(inp["qk_conv_w"][0], np.float32)
    pvec[:, PV_QW:PV_QW + 32] = qw.T.reshape(8, 128, QK).transpose(1, 0, 2).reshape(128, 32)
    pvec[:, PV_QB:PV_QB + 8] = col(inp["qk_conv_b"][0])
    pvec[0:4, PV_BI] = np.asarray(inp["b_igate"][0], np.float32)
    pvec[0:4, PV_BF] = np.asarray(inp["b_fgate"][0], np.float32)
    rgain = np.zeros((128, RG_N), np.float32)
    for i, nm in enumerate(("ffn1_post_g", "mix_post_g", "xattn_post_g", "ffn2_post_g")):
        rgain[:, RG_POST + i * D:RG_POST + (i + 1) * D] = np.asarray(inp[nm][0], np.float32)[None, :]
    rgain[:, RG_M:RG_M + MW] = np.asarray(inp["mlstm_norm_g"][0], np.float32)[None, :]
    consts = np.zeros((128, 1032), np.float32)
    consts[:, 512] = -0.5
    for hh in range(4):
        consts[hh, 520 + hh * 128:520 + (hh + 1) * 128] = 1.0
    consts[:, 0:128] = np.eye(128, dtype=np.float32)
    consts[:, 128:256] = np.triu(np.ones((128, 128), np.float32))
    consts[:, 256:384] = 1.0
    for hh in range(4):
        consts[hh, 384 + hh * 32:384 + (hh + 1) * 32] = 1.0
    return pvec, rgain, consts


_WNAMES = ("ffn1_w_gate", "ffn1_w_up", "ffn1_w_down", "w_in", "w_out", "xattn_wq", "xattn_wk",
           "xattn_wv", "xattn_wo", "ffn2_w_gate", "ffn2_w_up", "ffn2_w_down")


def make_in_maps(inp, ncores, seq):
    pvec, rgain, consts = _pack_params(inp)
    shared = {nm: np.ascontiguousarray(np.asarray(inp[nm], np.float32)[0]) for nm in _WNAMES}
    shared.update(pvec=pvec, rgain=rgain, consts=consts)
    maps = []
    for c in range(ncores):
        m = dict(shared)
        m["x"] = np.ascontiguousarray(np.asarray(inp["x"], np.float32)[c, :seq])
        m["mem"] = np.ascontiguousarray(np.asarray(inp["mem"], np.float32)[c])
        maps.append(m)
    return maps


def kernel(**inputs):
    nc, kb = build_program(SEQ)
    maps = make_in_maps(inputs, NCORES, SEQ)
    res = run_bass_kernel_spmd(nc, maps, core_ids=list(range(NCORES)))
    return np.stack([np.asarray(r["out"], np.float32) for r in res.results], axis=0)
```
